# Optimizing a Trainium2 kernel written in Bass

```python
import jax, jax.numpy as jnp
from jax import lax
import numpy as np

D_MODEL = 1024
BATCH = 2
SEQ = 16384
DEPTH = 4

HEAD_DIM = 64
N_TOK_HEADS = 12
N_KV_HEADS = 3
GQA_GROUP = N_TOK_HEADS // N_KV_HEADS
N_MEM_HEADS = 4
N_MEM = 256
Q_W = N_TOK_HEADS * HEAD_DIM
KV_W = N_KV_HEADS * HEAD_DIM
QM_W = N_MEM_HEADS * HEAD_DIM
IN_W = Q_W + 2 * KV_W + QM_W
MIX_WIDTH = Q_W + QM_W
D_FF = -(-8 * D_MODEL // (3 * 256)) * 256
N_MIXERS = 3
BLOCK = 128
A_RADIUS = 128
C_GROUPS = ((128, 1), (512, 4), (2048, 16))
ROPE_THETA = 500000.0
ROPE_DIMS = HEAD_DIM // 4
AXIAL_THETA = 10000.0
GRID_W = 64
EPS = 1e-6
N_A = (DEPTH + 2) // N_MIXERS
N_B = (DEPTH + 1) // N_MIXERS
ATTN_SCALE = HEAD_DIM ** -0.5

kernel_name = 'hybrid_interleaved_window_axial_dilated_encoder'


def rms_norm(x, g):
    xf = x.astype(jnp.float32)
    y = xf * lax.rsqrt(jnp.mean(xf * xf, axis=-1, keepdims=True) + EPS)
    return (y * g.astype(jnp.float32)).astype(x.dtype)


def rope_table(pos, n_dims, theta):
    inv = theta ** (-(jnp.arange(0, n_dims, 2, dtype=jnp.float32) / n_dims))
    ang = pos.astype(jnp.float32)[:, None] * inv[None, :]
    return jnp.cos(ang), jnp.sin(ang)


def apply_rotary(x, cos, sin):
    half = x.shape[-1] // 2
    xf = x.astype(jnp.float32)
    x1, x2 = xf[..., :half], xf[..., half:]
    c, s = cos[:, None, :], sin[:, None, :]
    return jnp.concatenate([x1 * c - x2 * s, x2 * c + x1 * s], axis=-1).astype(x.dtype)


def partial_rope(x, cos, sin):
    return jnp.concatenate([apply_rotary(x[..., :ROPE_DIMS], cos, sin), x[..., ROPE_DIMS:]], axis=-1)


def axial_rope(x, cos_r, sin_r, cos_c, sin_c):
    half = HEAD_DIM // 2
    return jnp.concatenate([apply_rotary(x[..., :half], cos_r, sin_r),
                            apply_rotary(x[..., half:], cos_c, sin_c)], axis=-1)


def banded_attention(q, k, v, radius, sink=None):
    B, L, KVH, G, HD = q.shape
    blk = radius
    nb = -(-L // blk)
    Lp = nb * blk
    pad = Lp - L
    qb = jnp.pad(q, [(0, 0), (0, pad), (0, 0), (0, 0), (0, 0)]).reshape(B, nb, blk, KVH, G, HD)

    def windows(t):
        tp = jnp.pad(t, [(0, 0), (blk, blk + pad), (0, 0), (0, 0)]).reshape(B, nb + 2, blk, KVH, HD)
        return jnp.concatenate([tp[:, :-2], tp[:, 1:-1], tp[:, 2:]], axis=2)

    kw, vw = windows(k), windows(v)
    qpos = jnp.arange(Lp).reshape(nb, blk)
    kpos = (jnp.arange(nb)[:, None] - 1) * blk + jnp.arange(3 * blk)[None, :]
    mask = ((jnp.abs(qpos[:, :, None] - kpos[:, None, :]) <= radius)
            & (kpos >= 0)[:, None, :] & (kpos < L)[:, None, :])
    s = jnp.einsum('bnqhgd,bnkhd->bnhgqk', qb, kw, preferred_element_type=jnp.float32) * ATTN_SCALE
    s = jnp.where(mask[None, :, None, None], s, -jnp.inf)
    m = jnp.max(s, axis=-1, keepdims=True)
    if sink is not None:
        sk = sink.astype(jnp.float32)[None, None, :, :, None, None]
        m = jnp.maximum(m, sk)
    p = jnp.exp(s - m)
    denom = jnp.sum(p, axis=-1)
    if sink is not None:
        denom = denom + jnp.exp(sk - m)[..., 0]
    o = jnp.einsum('bnhgqk,bnkhd->bnqhgd', p.astype(v.dtype), vw)
    den_t = jnp.transpose(denom, (0, 1, 4, 2, 3))
    o = (o / den_t[..., None]).astype(q.dtype).reshape(B, Lp, KVH, G, HD)[:, :L]
    lse = jnp.transpose(m[..., 0] + jnp.log(denom), (0, 1, 4, 2, 3)).reshape(B, Lp, KVH, G)[:, :L]
    return o, lse


def full_attention_blocks(q, k, v):
    B, S, KVH, G, HD = q.shape
    nb = S // BLOCK
    qb = jnp.moveaxis(q.reshape(B, nb, BLOCK, KVH, G, HD), 1, 0)

    def one_block(qblk):
        s = jnp.einsum('bqhgd,bkhd->bhgqk', qblk, k, preferred_element_type=jnp.float32) * ATTN_SCALE
        p = jax.nn.softmax(s, axis=-1)
        return jnp.einsum('bhgqk,bkhd->bqhgd', p.astype(v.dtype), v)

    o = lax.map(one_block, qb)
    return jnp.moveaxis(o, 0, 1).reshape(B, S, KVH, G, HD)


def mixer_a(q, k, v, sink, cos_p, sin_p):
    B, S = q.shape[:2]
    q = partial_rope(q, cos_p, sin_p).reshape(B, S, N_KV_HEADS, GQA_GROUP, HEAD_DIM)
    k = partial_rope(k, cos_p, sin_p)
    o, _ = banded_attention(q, k, v, A_RADIUS, sink.reshape(N_KV_HEADS, GQA_GROUP))
    return o.reshape(B, S, Q_W)


def mixer_b(q, k, v, qk_g, cos_r, sin_r, cos_c, sin_c):
    B, S = q.shape[:2]
    q = axial_rope(rms_norm(q, qk_g[0]), cos_r, sin_r, cos_c, sin_c)
    k = axial_rope(rms_norm(k, qk_g[1]), cos_r, sin_r, cos_c, sin_c)
    o = full_attention_blocks(q.reshape(B, S, N_KV_HEADS, GQA_GROUP, HEAD_DIM), k, v)
    return o.reshape(B, S, Q_W)


def dilated_group(q, k, v, dil, radius):
    B, S = q.shape[:2]
    L = S // dil

    def split(t):
        t = jnp.moveaxis(t.reshape((B, L, dil) + t.shape[2:]), 2, 1)
        return t.reshape((B * dil, L) + t.shape[3:])

    def merge(t):
        t = jnp.moveaxis(t.reshape((B, dil, L) + t.shape[2:]), 1, 2)
        return t.reshape((B, S) + t.shape[3:])

    o, lse = banded_attention(split(q)[:, :, None], split(k)[:, :, None], split(v)[:, :, None], radius)
    return merge(o[:, :, 0]), merge(lse[:, :, 0])


def mixer_c(q, k, v, cos_p, sin_p):
    B, S = q.shape[:2]
    q = partial_rope(q, cos_p, sin_p)
    k = partial_rope(k, cos_p, sin_p)
    outs, lses = [], []
    for g, (window, dil) in enumerate(C_GROUPS):
        o, l = dilated_group(q[:, :, g * GQA_GROUP:(g + 1) * GQA_GROUP], k[:, :, g], v[:, :, g],
                             dil, window // (2 * dil))
        outs.append(o)
        lses.append(l)
    alpha = jax.nn.softmax(jnp.stack(lses, axis=2), axis=2)
    o = jnp.stack(outs, axis=2) * alpha[..., None].astype(q.dtype)
    return o.reshape(B, S, Q_W)


def memory_attention(qm, km, vm):
    s = jnp.einsum('bshd,bmhd->bhsm', qm, km, preferred_element_type=jnp.float32) * ATTN_SCALE
    p = jax.nn.softmax(s, axis=-1)
    o = jnp.einsum('bhsm,bmhd->bshd', p.astype(vm.dtype), vm)
    return o.reshape(qm.shape[0], qm.shape[1], QM_W)


def setup_inputs(seed: int = 0) -> dict:
    key = jax.random.key(seed)
    ks = jax.random.split(key, 14)
    nrm = jax.random.normal
    f32 = jnp.float32
    return {
        'x': nrm(ks[0], (BATCH, SEQ, D_MODEL), f32),
        'mem': nrm(ks[1], (BATCH, N_MEM, D_MODEL), f32),
        'mem_norm_g': 1.0 + 0.02 * nrm(ks[2], (D_MODEL,), f32),
        'w_in': nrm(ks[3], (DEPTH, D_MODEL, IN_W), f32) * D_MODEL ** -0.5,
        'w_mem_kv': nrm(ks[4], (DEPTH, D_MODEL, 2 * QM_W), f32) * D_MODEL ** -0.5,
        'w_o': nrm(ks[5], (DEPTH, MIX_WIDTH, D_MODEL), f32) * MIX_WIDTH ** -0.5,
        'g_mix_pre': 1.0 + 0.02 * nrm(ks[6], (DEPTH, D_MODEL), f32),
        'g_mix_post': 1.0 + 0.02 * nrm(ks[7], (DEPTH, D_MODEL), f32),
        'attn_sink': 0.5 * nrm(ks[8], (N_A, N_TOK_HEADS), f32),
        'qk_norm_g': 1.0 + 0.02 * nrm(ks[9], (N_B, 2, HEAD_DIM), f32),
        'w_gate_up': nrm(ks[10], (DEPTH, D_MODEL, 2 * D_FF), f32) * D_MODEL ** -0.5,
        'w_down': nrm(ks[11], (DEPTH, D_FF, D_MODEL), f32) * D_FF ** -0.5,
        'g_ffn_pre': 1.0 + 0.02 * nrm(ks[12], (DEPTH, D_MODEL), f32),
        'g_ffn_post': 1.0 + 0.02 * nrm(ks[13], (DEPTH, D_MODEL), f32),
    }


def reference(x, mem, mem_norm_g, w_in, w_mem_kv, w_o, g_mix_pre, g_mix_post, attn_sink,
              qk_norm_g, w_gate_up, w_down, g_ffn_pre, g_ffn_post):
    B, S, _ = x.shape
    rows = S // GRID_W
    pos = jnp.arange(S, dtype=jnp.int32)
    row_ids = jnp.repeat(jnp.arange(rows, dtype=jnp.int32), GRID_W)
    col_ids = jnp.tile(jnp.arange(GRID_W, dtype=jnp.int32), rows)
    cos_p, sin_p = rope_table(pos, ROPE_DIMS, ROPE_THETA)
    cos_r, sin_r = rope_table(row_ids, HEAD_DIM // 2, AXIAL_THETA)
    cos_c, sin_c = rope_table(col_ids, HEAD_DIM // 2, AXIAL_THETA)
    mem_n = rms_norm(mem, mem_norm_g)

    for i in range(DEPTH):
        h = rms_norm(x, g_mix_pre[i])
        proj = h @ w_in[i]
        q = proj[..., :Q_W].reshape(B, S, N_TOK_HEADS, HEAD_DIM)
        k = proj[..., Q_W:Q_W + KV_W].reshape(B, S, N_KV_HEADS, HEAD_DIM)
        v = proj[..., Q_W + KV_W:Q_W + 2 * KV_W].reshape(B, S, N_KV_HEADS, HEAD_DIM)
        qm = proj[..., Q_W + 2 * KV_W:].reshape(B, S, N_MEM_HEADS, HEAD_DIM)
        kind = i % N_MIXERS
        if kind == 0:
            tok = mixer_a(q, k, v, attn_sink[i // N_MIXERS], cos_p, sin_p)
        elif kind == 1:
            tok = mixer_b(q, k, v, qk_norm_g[i // N_MIXERS], cos_r, sin_r, cos_c, sin_c)
        else:
            tok = mixer_c(q, k, v, cos_p, sin_p)
        mkv = mem_n @ w_mem_kv[i]
        km = mkv[..., :QM_W].reshape(B, N_MEM, N_MEM_HEADS, HEAD_DIM)
        vm = mkv[..., QM_W:].reshape(B, N_MEM, N_MEM_HEADS, HEAD_DIM)
        mo = memory_attention(qm, km, vm)
        o = jnp.concatenate([tok, mo], axis=-1) @ w_o[i]
        x = x + rms_norm(o, g_mix_post[i])

        h = rms_norm(x, g_ffn_pre[i])
        gu = h @ w_gate_up[i]
        f = (jax.nn.silu(gu[..., :D_FF]) * gu[..., D_FF:]) @ w_down[i]
        x = x + rms_norm(f, g_ffn_post[i])
    return x
```

```python
import numpy as np
import ml_dtypes
import concourse.bass as bass
import concourse.mybir as mybir
from concourse.bass_utils import run_bass_kernel_spmd

F32 = mybir.dt.float32
BF16 = mybir.dt.bfloat16
AF = mybir.ActivationFunctionType
ALU = mybir.AluOpType
AX = mybir.AxisListType

D = 1024
NCH = 8
HD = 64
H = 12
KVH = 3
HM = 4
NMEM = 256
QW = 768
KVW = 192
INW = 1408
DFF = 2816
NF = 22
EPS = 1e-6
PAD = 1152
T = 512
DEPTH = 4
GRID_W = 64
NCORES = 8
import os
FUSED = bool(int(os.environ.get('KFUSED', '1')))
NOPOOL = bool(int(os.environ.get('NOPOOL', '1')))


PSUM_KEYS = {"A0", "A1", "S0", "S1", "S0g", "S0u", "S1g", "S1u", "O0", "O1"}


class Op:
    __slots__ = ("eng", "fn", "deps", "flag", "idx", "dma", "dma_val", "real", "inc")

    def __init__(self, eng, fn):
        self.eng = eng
        self.fn = fn
        self.deps = []
        self.flag = False
        self.idx = 0
        self.dma = None
        self.dma_val = 0
        self.real = True
        self.inc = 16


class Sched:
    ENGS = ("pe", "act", "dve", "pool", "sp")

    def __init__(self):
        self.ops = {e: [] for e in self.ENGS}
        self.res = {}
        self.dma_cnt = {}
        self.pending_dma = []
        self.last_real = {e: None for e in self.ENGS}
        self.pre = {}

    def add(self, eng, fn, reads=(), writes=(), dma=None, inc=16):
        op = Op(eng, fn)
        op.inc = inc
        if dma is not None:
            op.dma = dma
            c = self.dma_cnt.get(dma, 0) + inc
            self.dma_cnt[dma] = c
            op.dma_val = c
        deps = {}
        px = [k for k in reads if k in PSUM_KEYS and k not in writes]
        if px:
            writes = list(writes) + px
        for k in reads:
            r = self.res.get(k)
            if r is not None and r[0] is not None:
                deps.setdefault(id(r[0]), [r[0], set()])[1].add("raw")
        for k in writes:
            r = self.res.get(k)
            if r is not None:
                if r[0] is not None:
                    deps.setdefault(id(r[0]), [r[0], set()])[1].add("waw")
                for rd in r[1].values():
                    deps.setdefault(id(rd), [rd, set()])[1].add("war")
                for rd in r[2]:
                    deps.setdefault(id(rd), [rd, set()])[1].add("war")
        for d, kinds in deps.values():
            if d is op:
                continue
            if d.dma is None and op.dma is None and d.eng == eng:
                if eng == "pe":
                    continue
            op.deps.append(d)
            d.flag = True
        for k in reads:
            r = self.res.setdefault(k, [None, {}, []])
            if op.dma is None:
                r[1][eng] = op
            else:
                r[2].append(op)
        for k in writes:
            self.res[k] = [op, {}, []]
        self.ops[eng].append(op)
        if op.dma is not None:
            self.pending_dma.append(op)
        else:
            self.last_real[eng] = op
        return op

    def barrier(self):
        lasts = [o for o in self.last_real.values() if o is not None]
        pl = {}
        for o in self.pending_dma:
            pl[o.dma] = o
        pend = list(pl.values())
        self.pending_dma = []
        for e in self.ENGS:
            op = Op(e, lambda h: h.nop())
            op.real = False
            for d in lasts:
                if d.eng != e or e != "pe":
                    op.deps.append(d)
                    d.flag = True
            for d in pend:
                op.deps.append(d)
            self.ops[e].append(op)

    def emit(self, nc, stack):
        esem = {e: stack.enter_context(nc.semaphore("s_" + e)) for e in self.ENGS}
        dsem = {k: stack.enter_context(nc.semaphore("d_" + str(i))) for i, k in enumerate(self.dma_cnt)}
        for e in self.ENGS:
            c = 0
            for op in self.ops[e]:
                if op.dma is None and op.flag:
                    c += 1
                    op.idx = c
        block = stack.enter_context(nc.Block())

        def run(e, h):
            known = {}
            if self.pre.get(e):
                self.pre[e](h)
            for op in self.ops[e]:
                for d in op.deps:
                    if d.dma is not None:
                        key, sem, val = ("d", d.dma), dsem[d.dma], d.dma_val
                    else:
                        key, sem, val = ("e", d.eng), esem[d.eng], d.idx
                    if known.get(key, 0) < val:
                        h.wait_ge(sem, val)
                        known[key] = val
                ins = op.fn(h)
                if op.dma is not None:
                    ins.then_inc(dsem[op.dma], op.inc)
                elif op.flag:
                    ins.then_inc(esem[e], 1)

        @block.sync
        def _(h):
            run("sp", h)

        @block.tensor
        def _(h):
            run("pe", h)

        @block.scalar
        def _(h):
            run("act", h)

        @block.vector
        def _(h):
            run("dve", h)

        @block.gpsimd
        def _(h):
            run("pool", h)


class Builder:
    def __init__(self, S, steps, nlayers_w):
        self.S = S
        self.TPC = S // 4
        self.NG = self.TPC // T
        self.steps = steps
        self.nlw = nlayers_w
        self.nc = bass.Bass("TRN2", target_bir_lowering=False)
        self.sched = Sched()
        self.uid = 0
        self.in_names = set()

    def key(self, base):
        self.uid += 1
        return "%s#%d" % (base, self.uid)

    def dram_in(self, name, shape, dt=F32):
        self.in_names.add(name)
        return self.nc.dram_tensor(name, list(shape), dt, kind="ExternalInput").ap()

    def dram_out(self, name, shape, dt=F32):
        return self.nc.dram_tensor(name, list(shape), dt, kind="ExternalOutput").ap()

    def alloc(self, shape, dt, parts=128):
        n = int(np.prod(shape))
        nbytes = n * (4 if dt == F32 else 2)
        nbytes = (nbytes + 63) // 64 * 64
        off = self.aoff
        self.aoff += nbytes
        assert self.aoff <= self.acap, ("arena overflow", self.aoff, self.acap)
        v = self.arena[0:parts, off // 2:(off + n * (4 if dt == F32 else 2)) // 2]
        if dt == F32:
            v = v.bitcast(F32)
        if len(shape) == 2:
            v = v.rearrange("p (a b) -> p a b", a=shape[0])
        elif len(shape) == 3:
            v = v.rearrange("p (a b c) -> p a b c", a=shape[0], b=shape[1])
        return v

    def dma(self, q, out, in_, reads, writes, key):
        self.sched.add(q, lambda h: h.dma_start(out=out, in_=in_), reads=reads, writes=writes, dma=q + ":" + key)

    def build(self):
        import contextlib
        nc = self.nc
        S_ = self.sched
        TPC, NG = self.TPC, self.NG
        kinds = set(st[0] for st in self.steps)
        has_init = "init" in kinds
        has_fin = "fin" in kinds
        has_p1 = "p1" in kinds
        has_p2 = "p2" in kinds
        has_ffn = "ffn" in kinds
        fused = "exch" in kinds
        self.fused = fused
        self.rank_vals = {}
        if fused:
            def _pre(h):
                npr = -(-PAD // self.TPC)
                r = h.partition_id() % 4
                self.rank_vals = {j: h.snap(r + (j + npr), min_val=j + npr, max_val=j + npr + 3) for j in range(-npr, npr + 1)}
            self.sched.pre["sp"] = _pre
        nlw = self.nlw
        dr = {}
        if has_init:
            dr["x_in"] = self.dram_in("x_in", [TPC, D])
            dr["xt"] = self.dram_out("xt_out", [NCH, 128, TPC]) if not has_fin else None
        else:
            dr["xt_in"] = self.dram_in("xt_in", [NCH, 128, TPC])
        if has_fin:
            dr["y_out"] = self.dram_out("y_out", [TPC, D])
        if not has_init and not has_fin:
            dr["xt"] = self.dram_out("xt_out", [NCH, 128, TPC])
        if has_init and has_fin:
            dr["xt"] = nc.dram_tensor("xt_scr", [NCH, 128, TPC], F32).ap()
        if not has_init and has_fin:
            dr["xt"] = nc.dram_tensor("xt_scr", [NCH, 128, TPC], F32).ap()
        if has_p1 or has_p2:
            dr["w_in"] = self.dram_in("w_in", [nlw, D, INW])
        if has_p2:
            dr["w_mem_kv"] = self.dram_in("w_mem_kv", [nlw, D, 512])
            dr["w_o"] = self.dram_in("w_o", [nlw, D, D])
        if has_ffn:
            dr["w_gu"] = self.dram_in("w_gate_up", [nlw, D, 2 * DFF])
            dr["w_dn"] = self.dram_in("w_down", [nlw, DFF, D])
        dr["gv"] = self.dram_in("gv", [nlw, 128, 4 * NCH])
        dr["sink"] = self.dram_in("sink", [nlw, 16])
        dr["qkg"] = self.dram_in("qkg", [nlw, 2 * HD])
        dr["mem"] = self.dram_in("mem", [NMEM, D])
        dr["mem_g"] = self.dram_in("mem_g", [D])
        dr["ropep"] = self.dram_in("ropep", [TPC, 16])
        dr["ropea"] = self.dram_in("ropea", [TPC, 64])
        dr["cident"] = self.dram_in("cident", [128, 128])
        dr["cmask"] = self.dram_in("cmask", [8, 128, 128])
        if fused:
            self.NPR = -(-PAD // TPC)
            NR = 4 + 2 * self.NPR
            for i in range(2):
                dr["kt_loc%d" % i] = nc.dram_tensor("kt_loc%d" % i, [KVH * HD, TPC], BF16).ap()
                dr["v_loc%d" % i] = nc.dram_tensor("v_loc%d" % i, [TPC, KVH * 128], BF16).ap()
                self.VCH = min(1024, TPC)
                self.NVC = TPC // self.VCH
                for kh in range(KVH):
                    dr["ktpad%d_%d" % (i, kh)] = nc.dram_tensor("ktpad%d_%d" % (i, kh), [NR * HD, TPC], BF16).ap()
                for c in range(self.NVC):
                    dr["vpad%d_%d" % (i, c)] = nc.dram_tensor("vpad%d_%d" % (i, c), [NR * self.VCH, KVH * 128], BF16).ap()
            dr["kt_rel"] = nc.dram_tensor("kt_rel", [KVH, HD, TPC + 2 * PAD], BF16).ap()
            dr["v_rel"] = nc.dram_tensor("v_rel", [TPC + 2 * PAD, KVH, 128], BF16).ap()
        elif has_p1:
            dr["kt_out"] = self.dram_out("kt_out", [KVH, HD, TPC], BF16)
            dr["v_out"] = self.dram_out("v_out", [TPC, KVH, 128], BF16)
        if has_p2 and not fused:
            self.NKEYS = {}
            for st in self.steps:
                if st[0] == "p2":
                    l = st[1]
                    nk = self.S if l % 3 == 1 else TPC + 2 * PAD
                    dr["kt_buf%d" % l] = self.dram_in("kt_buf%d" % l, [KVH, HD, nk], BF16)
                    dr["v_buf%d" % l] = self.dram_in("v_buf%d" % l, [nk, KVH, 128], BF16)
        self.dr = dr

        stack = contextlib.ExitStack()
        with stack:
            self.acap = 205 * 1024
            self.arena = stack.enter_context(nc.sbuf_tensor("arena", [128, self.acap // 2], BF16))
            self.aoff = 0
            self.psA = stack.enter_context(nc.psum_tensor("psA", [128, 1024], F32))
            self.psS = stack.enter_context(nc.psum_tensor("psS", [128, 2048], F32))
            self.psO = stack.enter_context(nc.psum_tensor("psO", [128, 1024], F32))
            self.setup_consts()
            self.pmark = self.aoff
            if fused:
                self.zero_pads()
            self.xsrc = "x_in" if has_init else "xt_in"
            first_x = True
            for st in self.steps:
                self.aoff = self.pmark
                S_.barrier()
                if st[0] == "init":
                    self.step_init()
                    self.xsrc = "xt"
                elif st[0] == "p1":
                    self.step_p1(st[1], st[2])
                elif st[0] == "p2":
                    self.step_p2(st[1], st[2])
                    self.xsrc = "xt"
                elif st[0] == "ffn":
                    self.step_ffn(st[1], st[2])
                    self.xsrc = "xt"
                elif st[0] == "exch":
                    pass
                elif st[0] == "fin":
                    self.step_fin()
            S_.barrier()
            S_.emit(nc, stack)
        return nc

    def setup_consts(self):
        S_ = self.sched
        dr = self.dr
        self.identF = self.alloc([128], F32)
        self.identB = self.alloc([128], BF16)
        self.onesD = self.alloc([128], BF16)
        self.masks = self.alloc([8, 128], BF16)
        self.gv = self.alloc([self.nlw, 4 * NCH], F32)
        self.rstd = self.alloc([T], F32)
        mark = self.aoff
        stg = self.alloc([8, 128], F32)
        self.dma("sp", self.identF, dr["cident"], [], ["identF"], "c0")
        self.dma("sp", stg, dr["cmask"].rearrange("m p q -> p m q"), [], ["cstg"], "c1")
        self.dma("sp", self.gv, dr["gv"].rearrange("l p c -> p l c"), [], ["gv"], "c2")
        S_.add("dve", lambda h: h.tensor_copy(out=self.identB, in_=self.identF), ["identF"], ["identB"])
        S_.add("dve", lambda h: h.memset(self.onesD, 1.0 / D), [], ["onesD"])
        S_.add("dve", lambda h: h.tensor_copy(out=self.masks, in_=stg), ["cstg"], ["masks"])
        S_.barrier()
        self.aoff = mark

    def zero_pads(self):
        S_ = self.sched
        dr = self.dr
        TPC = self.TPC
        NPR = self.NPR
        VCH = self.VCH
        m = self.aoff
        zt = self.alloc([max(TPC, (VCH // 128) * KVH * 128)], BF16)
        S_.add("dve", lambda h: h.memset(zt, 0.0), [], ["zt"])
        for i in range(2):
            for blk in list(range(NPR)) + list(range(NPR + 4, 2 * NPR + 4)):
                for kh in range(KVH):
                    self.dma("pool", dr["ktpad%d_%d" % (i, kh)][blk * HD:(blk + 1) * HD, :], zt[0:64, 0:TPC], ["zt"],
                             [self.key("zpk")], "zp")
                for c in range(self.NVC):
                    nj = VCH // 128
                    base = blk * VCH
                    q = "sp" if c % 2 == 0 else "pool"
                    self.dma(q, dr["vpad%d_%d" % (i, c)][base:base + VCH, :].rearrange("(j p) c -> p j c", p=128),
                             zt[:, 0:nj * KVH * 128].rearrange("p (j c) -> p j c", j=nj), ["zt"], [self.key("zpv")], "zp")
        S_.barrier()
        self.aoff = m

    def step_exch(self, l):
        S_ = self.sched
        dr = self.dr
        i = l % 2
        TPC = self.TPC
        NPR = self.NPR
        VCH = self.VCH
        groups = [[0, 1, 2, 3], [4, 5, 6, 7]]
        for kh in range(KVH):
            kin = dr["kt_loc%d" % i][kh * HD:(kh + 1) * HD, :]
            kout = dr["ktpad%d_%d" % (i, kh)][NPR * HD:(NPR + 4) * HD, :]
            S_.add("pool", lambda h, kin=kin, kout=kout: h.collective_compute(
                "AllGather", ALU.bypass, replica_groups=groups, ins=[kin], outs=[kout]),
                ["ktdram"], ["ktpad%d" % i], dma="pool:cck", inc=1)
        for c in range(self.NVC):
            vin = dr["v_loc%d" % i][c * VCH:(c + 1) * VCH, :]
            vout = dr["vpad%d_%d" % (i, c)][NPR * VCH:(NPR + 4) * VCH, :]
            S_.add("pool", lambda h, vin=vin, vout=vout: h.collective_compute(
                "AllGather", ALU.bypass, replica_groups=groups, ins=[vin], outs=[vout]),
                ["vdram"], ["vpad%d" % i], dma="pool:ccv", inc=1)
        if l % 3 == 1:
            return
        vrel2 = dr["v_rel"].rearrange("t h c -> t (h c)")
        for j in range(-NPR, NPR + 1):
            a = max(0, -PAD - j * TPC)
            b = min(TPC, TPC + PAD - j * TPC)
            if a >= b:
                continue
            dst = j * TPC + a + PAD
            for kh in range(KVH):
                kp3 = dr["ktpad%d_%d" % (i, kh)].rearrange("(n r) t -> n r t", r=HD)
                S_.add("sp", lambda h, j=j, a=a, b=b, dst=dst, kp3=kp3, kh=kh: h.dma_start(
                    out=dr["kt_rel"][kh, :, dst:dst + (b - a)], in_=kp3[bass.ds(self.rank_vals[j], 1), :, a:b]),
                    ["ktpad%d" % i], ["ktrel"], dma="sp:krel")
            for c in range(self.NVC):
                lo, hi = max(a, c * VCH), min(b, (c + 1) * VCH)
                if lo >= hi:
                    continue
                vp3 = dr["vpad%d_%d" % (i, c)].rearrange("(n t) c -> n t c", t=VCH)
                d2 = j * TPC + lo + PAD
                S_.add("sp", lambda h, j=j, lo=lo, hi=hi, d2=d2, vp3=vp3, c=c: h.dma_start(
                    out=vrel2[d2:d2 + (hi - lo), :], in_=vp3[bass.ds(self.rank_vals[j], 1), lo - c * VCH:hi - c * VCH, :]),
                    ["vpad%d" % i], ["vrel"], dma="sp:vrel")

    def rank_of(self, h):
        return self.rank_vals[id(h)]

    def load_weight(self, dst, src_rows, ncols, nchunks, keyname, col0=0, colchunk=2048):
        S_ = self.sched
        stg = [self.alloc([colchunk], F32) for _ in range(3)]
        i = 0
        for c in range(nchunks):
            for j0 in range(0, ncols, colchunk):
                n = min(colchunk, ncols - j0)
                s = i % 3
                sk = "wstg%d" % s
                q = "sp" if (i % 2 == 0 or NOPOOL) else "pool"
                self.dma(q, stg[s][:, 0:n], src_rows[c * 128:(c + 1) * 128, col0 + j0:col0 + j0 + n], [], [sk], "w%d" % s)
                eng = "dve" if (i % 2 == 0 or NOPOOL) else "pool"
                S_.add(eng, lambda h, c=c, j0=j0, n=n, s=s: h.tensor_copy(out=dst[:, c, j0:j0 + n], in_=stg[s][:, 0:n]),
                       [sk], [keyname])
                i += 1

    def stats_rstd(self, sq, sqkey):
        S_ = self.sched
        ps = self.psO[:, 512:1024]
        for c in range(NCH):
            S_.add("pe", lambda h, c=c: h.matmul(ps, lhsT=self.onesD, rhs=sq[:, c, :], start=(c == 0), stop=(c == NCH - 1)),
                   [sqkey, "onesD"], ["O1"])
        S_.add("dve", lambda h: h.tensor_scalar(out=self.rstd, in0=ps, scalar1=EPS, scalar2=None, op0=ALU.add),
               ["O1"], ["rstd"])
        S_.add("act", lambda h: h.activation(out=self.rstd, in_=self.rstd, func=AF.Ln), ["rstd"], ["rstd"])
        S_.add("act", lambda h: h.activation(out=self.rstd, in_=self.rstd, func=AF.Exp, scale=-0.5), ["rstd"], ["rstd"])

    def prenorm(self, xg, xkey, gcol, hT, sq, sqkey="sq"):
        S_ = self.sched
        S_.add("act", lambda h: h.activation(out=sq, in_=xg, func=AF.Square), [xkey], [sqkey])
        self.stats_rstd(sq, sqkey)
        for c in range(NCH):
            S_.add("dve", lambda h, c=c: h.scalar_tensor_tensor(out=hT[:, c, :], in0=xg[:, c, :], scalar=gcol[:, c:c + 1],
                                                                 in1=self.rstd, op0=ALU.mult, op1=ALU.mult),
                   [xkey, "rstd", "gv"], ["hT"])

    def postnorm_chunk_in(self, c, ps, pskey, fst, sq, sqkey="sq"):
        S_ = self.sched
        S_.add("act", lambda h: h.activation(out=sq[:, c, :], in_=ps, func=AF.Square), [pskey], [sqkey])
        S_.add("dve", lambda h: h.tensor_copy(out=fst[:, c, :], in_=ps), [pskey], ["fst"])

    def postnorm_finish(self, xg, xkey, gcol, fst, sq, sqkey="sq"):
        S_ = self.sched
        self.stats_rstd(sq, sqkey)
        for c in range(NCH):
            e1 = "dve" if (c % 2 == 0 or NOPOOL) else "pool"
            S_.add(e1, lambda h, c=c: h.tensor_tensor(out=fst[:, c, :], in0=fst[:, c, :], in1=self.rstd, op=ALU.mult),
                   ["fst", "rstd"], ["fst"])
            S_.add("dve", lambda h, c=c: h.scalar_tensor_tensor(out=xg[:, c, :], in0=fst[:, c, :], scalar=gcol[:, c:c + 1],
                                                                 in1=xg[:, c, :], op0=ALU.mult, op1=ALU.add),
                   ["fst", "gv", xkey], [xkey])

    def load_x(self, g, xg, xkey, q="sp"):
        dr = self.dr
        if self.xsrc == "x_in":
            raise RuntimeError
        src = dr[self.xsrc]
        self.dma(q, xg, src[:, :, g * T:(g + 1) * T].rearrange("c p t -> p c t"), ["xt_dram%d" % g], [xkey], xkey)

    def store_x(self, g, xg, xkey, q="pool"):
        self.dma(q, self.dr["xt"][:, :, g * T:(g + 1) * T].rearrange("c p t -> p c t"), xg, [xkey], ["xt_dram%d" % g],
                 "st" + xkey)

    def step_init(self):
        S_ = self.sched
        dr = self.dr
        xt = [self.alloc([D], F32) for _ in range(2)]
        xg = [self.alloc([NCH, T], F32) for _ in range(2)]
        for g in range(self.NG):
            gk = "ixg%d" % (g % 2)
            for t in range(4):
                tt = g * 4 + t
                tk = "ixt%d" % (tt % 2)
                xtt = xt[tt % 2]
                self.dma("sp", xtt, dr["x_in"][tt * 128:(tt + 1) * 128, :], [], [tk], tk)
                for hf in range(2):
                    ps = self.psA[:, hf * 512:(hf + 1) * 512]
                    pk = "A%d" % hf
                    for c4 in range(4):
                        c = hf * 4 + c4
                        S_.add("pe", lambda h, ps=ps, c4=c4, c=c, xtt=xtt: h.transpose(
                            out=ps[:, c4 * 128:(c4 + 1) * 128], in_=xtt[:, c * 128:(c + 1) * 128], identity=self.identF),
                            [tk, "identF"], [pk])
                    eng = "dve" if hf == 0 else "act"
                    dst = xg[g % 2][:, hf * 4:(hf + 1) * 4, t * 128:(t + 1) * 128]
                    srcp = ps.rearrange("p (c q) -> p c q", c=4)
                    if eng == "dve":
                        S_.add("dve", lambda h, dst=dst, srcp=srcp: h.tensor_copy(out=dst, in_=srcp), [pk], [gk])
                    else:
                        S_.add("act", lambda h, dst=dst, srcp=srcp: h.activation(out=dst, in_=srcp, func=AF.Copy), [pk], [gk])
            self.store_x(g, xg[g % 2], gk)

    def step_fin(self):
        S_ = self.sched
        dr = self.dr
        xg = [self.alloc([NCH, T], F32) for _ in range(2)]
        yt = [self.alloc([D], F32) for _ in range(2)]
        for g in range(self.NG):
            gk = "fxg%d" % (g % 2)
            self.load_x(g, xg[g % 2], gk)
            for t in range(4):
                tt = g * 4 + t
                yk = "fyt%d" % (tt % 2)
                ytt = yt[tt % 2]
                for hf in range(2):
                    ps = self.psA[:, hf * 512:(hf + 1) * 512]
                    pk = "A%d" % hf
                    for c4 in range(4):
                        c = hf * 4 + c4
                        S_.add("pe", lambda h, ps=ps, c4=c4, c=c, t=t, g=g: h.transpose(
                            out=ps[:, c4 * 128:(c4 + 1) * 128], in_=xg[g % 2][:, c, t * 128:(t + 1) * 128],
                            identity=self.identF), [gk, "identF"], [pk])
                    dst = ytt[:, hf * 512:(hf + 1) * 512]
                    if hf == 0:
                        S_.add("dve", lambda h, dst=dst, ps=ps: h.tensor_copy(out=dst, in_=ps), [pk], [yk])
                    else:
                        S_.add("act", lambda h, dst=dst, ps=ps: h.activation(out=dst, in_=ps, func=AF.Copy), [pk], [yk])
                self.dma("pool", dr["y_out"][tt * 128:(tt + 1) * 128, :], ytt, [yk], ["ydram"], "o" + yk)

    def rope_partial(self, eng, src, dst, nh, tt, skey, dkey, tmp):
        S_ = self.sched
        cs = self.ropep[:, tt, :]
        cb = cs[:, 0:8].unsqueeze(1).to_broadcast([128, nh, 8])
        sb = cs[:, 8:16].unsqueeze(1).to_broadcast([128, nh, 8])
        x1 = src[:, :, 0:8]
        x2 = src[:, :, 8:16]
        t1, t2, t3, t4 = (tmp[:, i, 0:nh, 0:8] for i in range(4))
        tk = "rtmp"
        skey = list(skey) if isinstance(skey, (list, tuple)) else [skey]
        S_.add(eng, lambda h: h.tensor_tensor(out=t1, in0=x1, in1=cb, op=ALU.mult), skey + ["ropep"], [tk + "1"])
        S_.add(eng, lambda h: h.tensor_tensor(out=t2, in0=x2, in1=sb, op=ALU.mult), skey + ["ropep"], [tk + "2"])
        S_.add(eng, lambda h: h.tensor_tensor(out=t3, in0=x2, in1=cb, op=ALU.mult), skey + ["ropep"], [tk + "3"])
        S_.add(eng, lambda h: h.tensor_tensor(out=t4, in0=x1, in1=sb, op=ALU.mult), skey + ["ropep"], [tk + "4"])
        S_.add(eng, lambda h: h.tensor_tensor(out=dst[:, :, 0:8], in0=t1, in1=t2, op=ALU.subtract), [tk + "1", tk + "2"], [dkey])
        S_.add(eng, lambda h: h.tensor_tensor(out=dst[:, :, 8:16], in0=t3, in1=t4, op=ALU.add), [tk + "3", tk + "4"], [dkey])
        S_.add(eng, lambda h: h.tensor_copy(out=dst[:, :, 16:64], in_=src[:, :, 16:64]), skey, [dkey])

    def norm_rope_axial(self, eng, src, dst, nh, tt, skey, dkey, gidx, tmpn, tmp, st, tmpx):
        S_ = self.sched
        sq = tmpn[:, 0:nh, :]
        ss = st[:, 0:nh]
        nk = "qn"
        skey = list(skey) if isinstance(skey, (list, tuple)) else [skey]
        xs = tmpx[:, 0:nh, :]
        S_.add("act", lambda h, s0=src: h.activation(out=xs, in_=s0, func=AF.Copy), skey, [nk + "xs"])
        src = xs
        skey = [nk + "xs"]
        S_.add(eng, lambda h: h.tensor_tensor(out=sq, in0=src, in1=src, op=ALU.mult), skey, [nk + "sq"])
        S_.add(eng, lambda h: h.reduce_sum(out=ss, in_=sq, axis=AX.X), [nk + "sq"], [nk + "ss"])
        S_.add(eng, lambda h: h.tensor_scalar(out=ss, in0=ss, scalar1=1.0 / HD, scalar2=EPS, op0=ALU.mult, op1=ALU.add),
               [nk + "ss"], [nk + "ss"])
        S_.add("act", lambda h: h.activation(out=ss, in_=ss, func=AF.Ln), [nk + "ss"], [nk + "ss"])
        S_.add("act", lambda h: h.activation(out=ss, in_=ss, func=AF.Exp, scale=-0.5), [nk + "ss"], [nk + "ss"])
        S_.add(eng, lambda h: h.tensor_tensor(out=sq, in0=src, in1=ss.unsqueeze(2).to_broadcast([128, nh, HD]), op=ALU.mult),
               skey + [nk + "ss"], [nk + "sq"])
        gb = self.qkg[:, gidx * HD:(gidx + 1) * HD].unsqueeze(1).to_broadcast([128, nh, HD])
        S_.add(eng, lambda h: h.tensor_tensor(out=sq, in0=sq, in1=gb, op=ALU.mult), [nk + "sq", "qkg"], [nk + "sq"])
        ca = self.ropea[:, tt, :]
        for half in range(2):
            o = half * 32
            cb = ca[:, half * 16:(half + 1) * 16].unsqueeze(1).to_broadcast([128, nh, 16])
            sb = ca[:, 32 + half * 16:32 + (half + 1) * 16].unsqueeze(1).to_broadcast([128, nh, 16])
            x1 = sq[:, :, o:o + 16]
            x2 = sq[:, :, o + 16:o + 32]
            t1, t2, t3, t4 = (tmp[:, i, 0:nh, :] for i in range(4))
            tk = "rtmp"
            S_.add(eng, lambda h, x1=x1, cb=cb, t1=t1: h.tensor_tensor(out=t1, in0=x1, in1=cb, op=ALU.mult), [nk + "sq", "ropea"], [tk + "1"])
            S_.add(eng, lambda h, x2=x2, sb=sb, t2=t2: h.tensor_tensor(out=t2, in0=x2, in1=sb, op=ALU.mult), [nk + "sq", "ropea"], [tk + "2"])
            S_.add(eng, lambda h, x2=x2, cb=cb, t3=t3: h.tensor_tensor(out=t3, in0=x2, in1=cb, op=ALU.mult), [nk + "sq", "ropea"], [tk + "3"])
            S_.add(eng, lambda h, x1=x1, sb=sb, t4=t4: h.tensor_tensor(out=t4, in0=x1, in1=sb, op=ALU.mult), [nk + "sq", "ropea"], [tk + "4"])
            S_.add(eng, lambda h, o=o, t1=t1, t2=t2: h.tensor_tensor(out=dst[:, :, o:o + 16], in0=t1, in1=t2, op=ALU.subtract),
                   [tk + "1", tk + "2"], [dkey])
            S_.add(eng, lambda h, o=o, t3=t3, t4=t4: h.tensor_tensor(out=dst[:, :, o + 16:o + 32], in0=t3, in1=t4, op=ALU.add),
                   [tk + "3", tk + "4"], [dkey])

    def load_layer_small(self, l, wi):
        S_ = self.sched
        dr = self.dr
        NT = self.TPC // 128
        kind = l % 3
        if kind == 1:
            self.ropea = self.alloc([NT, 64], F32)
            self.dma("sp", self.ropea, dr["ropea"].rearrange("(t p) c -> p t c", p=128), [], ["ropea"], self.key("rp"))
            self.qkg = self.alloc([2 * HD], F32)
            self.dma("sp", self.qkg, dr["qkg"][wi].partition_broadcast(128), [], ["qkg"], self.key("rp"))
        else:
            self.ropep = self.alloc([NT, 16], F32)
            self.dma("sp", self.ropep, dr["ropep"].rearrange("(t p) c -> p t c", p=128), [], ["ropep"], self.key("rp"))
        if kind == 0:
            self.sinke = self.alloc([16], F32)
            self.dma("sp", self.sinke, dr["sink"][wi].partition_broadcast(128), [], ["sinke"], self.key("rp"))
            S_.add("act", lambda h: h.activation(out=self.sinke, in_=self.sinke, func=AF.Exp), ["sinke"], ["sinke"])

    def step_p1(self, l, wi):
        S_ = self.sched
        dr = self.dr
        kind = l % 3
        self.load_layer_small(l, wi)
        wkv = self.alloc([NCH, 2 * KVW], BF16)
        m0 = self.aoff
        self.load_weight(wkv, dr["w_in"][wi], 2 * KVW, NCH, "wkv", col0=QW)
        S_.barrier()
        self.aoff = m0
        CUT = int(os.environ.get("P1CUT", "99"))
        if CUT <= 1:
            return
        gcol = self.gv[:, wi, 0:NCH]
        xg = [self.alloc([NCH, T], F32) for _ in range(2)]
        hT = self.alloc([NCH, T], BF16)
        sq = self.alloc([NCH, T], BF16)
        kb = self.alloc([KVH, HD], BF16)
        ktst = [self.alloc([KVH, T], BF16) for _ in range(2)]
        vst = [self.alloc([4, KVH, 128], BF16) for _ in range(2)]
        tmp = self.alloc([4, KVH, 16], F32)
        tmpn = self.alloc([KVH, HD], F32)
        tmpx = self.alloc([KVH, HD], F32)
        stt = self.alloc([16], F32)
        for i in range(2):
            S_.add("dve", lambda h, i=i: h.memset(vst[i][:, :, :, 64:128], 1.0), [], ["vst%d" % i])
        psT = self.psA[:, 512:768].bitcast(BF16)
        self.load_x(0, xg[0], "p1x0")
        for g in range(self.NG):
            xk = "p1x%d" % (g % 2)
            if g + 1 < self.NG:
                self.load_x(g + 1, xg[(g + 1) % 2], "p1x%d" % ((g + 1) % 2))
            self.prenorm(xg[g % 2], xk, gcol, hT, sq)
            if CUT <= 2:
                return
            kts = ktst[g % 2]
            vs = vst[g % 2]
            ktk = "ktst%d" % (g % 2)
            vk = "vst%d" % (g % 2)
            for t in range(4):
                tt = g * 4 + t
                ps = self.psA[:, 0:2 * KVW]
                for c in range(NCH):
                    S_.add("pe", lambda h, c=c, t=t: h.matmul(ps, lhsT=hT[:, c, t * 128:(t + 1) * 128], rhs=wkv[:, c, :],
                                                               start=(c == 0), stop=(c == NCH - 1)), ["hT", "wkv"], ["A0"])
                ksrc = ps[:, 0:KVW].rearrange("p (h d) -> p h d", h=KVH)
                if kind == 1:
                    self.norm_rope_axial("dve", ksrc, kb, KVH, tt, "A0", "kb", 1, tmpn, tmp, stt, tmpx)
                else:
                    self.rope_partial("dve", ksrc, kb, KVH, tt, "A0", "kb", tmp)
                if CUT <= 3:
                    return
                S_.add("act", lambda h, t=t, vs=vs: h.activation(out=vs[:, t, :, 0:64],
                                                                  in_=ps[:, KVW:2 * KVW].rearrange("p (h d) -> p h d", h=KVH),
                                                                  func=AF.Copy), ["A0"], [vk])
                if CUT == 5:
                    return
                for kh in range(KVH):
                    S_.add("pe", lambda h, kh=kh: h.transpose(out=psT[0:64, kh * 128:(kh + 1) * 128], in_=kb[:, kh, :],
                                                               identity=self.identB), ["kb", "identB"], ["A1"])
                if CUT == 6:
                    return
                S_.add("act", lambda h, t=t, kts=kts: h.activation(out=kts[0:64, :, t * 128:(t + 1) * 128],
                                                                    in_=psT[0:64, 0:384].rearrange("p (h q) -> p h q", h=KVH),
                                                                    func=AF.Copy), ["A1"], [ktk])
            if CUT <= 7:
                return
            if self.fused:
                kdst = dr["kt_loc%d" % (l % 2)].rearrange("(h d) t -> h d t", h=KVH)
                vdst = dr["v_loc%d" % (l % 2)].rearrange("t (h c) -> t h c", h=KVH)
            else:
                kdst, vdst = dr["kt_out"], dr["v_out"]
            self.dma("pool", kdst[:, :, g * T:(g + 1) * T].rearrange("h d t -> d h t"), kts[0:64], [ktk], ["ktdram"],
                     "o" + ktk)
            self.dma("pool", vdst[g * T:(g + 1) * T].rearrange("(t p) h c -> p t h c", p=128), vs, [vk], ["vdram"],
                     "o" + vk)

    def step_p2(self, l, wi):
        S_ = self.sched
        dr = self.dr
        kind = l % 3
        TPC = self.TPC
        self.load_layer_small(l, wi)
        if self.fused and kind == 1:
            ktb = [dr["ktpad%d_%d" % (l % 2, kh)] for kh in range(KVH)]
            vb = [dr["vpad%d_%d" % (l % 2, c)] for c in range(self.NVC)]
            self.kvkeys = ["ktpad%d" % (l % 2), "vpad%d" % (l % 2)]
        elif self.fused:
            ktb = dr["kt_rel"]
            vb = dr["v_rel"]
            self.kvkeys = ["ktrel", "vrel"]
        else:
            ktb = dr["kt_buf%d" % l]
            vb = dr["v_buf%d" % l]
            self.kvkeys = [None, None]
        self.cur_kind = kind
        wq = self.alloc([NCH, 1024], BF16)
        wo = self.alloc([NCH, D], BF16)
        kmT = self.alloc([HM, NMEM], BF16)
        vm = self.alloc([2, HM, 128], BF16)
        self.kmT, self.vm = kmT, vm
        m0 = self.aoff
        wm = self.alloc([NCH, 512], BF16)
        memT = self.alloc([NCH, NMEM], BF16)
        kmb = self.alloc([256], BF16)
        gb = self.alloc([D], F32)
        mt = self.alloc([2, D], F32)
        sqj = self.alloc([D], F32)
        ssq = self.alloc([2], F32)
        mb = self.alloc([2, D], BF16)
        self.load_weight(wq[:, :, 0:QW], dr["w_in"][wi], QW, NCH, "wq", col0=0, colchunk=QW)
        self.load_weight(wq[:, :, QW:1024], dr["w_in"][wi], 256, NCH, "wq", col0=QW + 2 * KVW, colchunk=256)
        self.load_weight(wo, dr["w_o"][wi], D, NCH, "wo", colchunk=1024)
        self.load_weight(wm, dr["w_mem_kv"][wi], 512, NCH, "wm", colchunk=512)
        self.dma("sp", gb, dr["mem_g"].partition_broadcast(128), [], ["memg"], self.key("mm"))
        self.dma("sp", mt, dr["mem"].rearrange("(b p) d -> p b d", p=128), [], ["memt"], self.key("mm"))
        for b in range(2):
            S_.add("dve", lambda h, b=b: h.tensor_tensor(out=sqj, in0=mt[:, b, :], in1=mt[:, b, :], op=ALU.mult),
                   ["memt"], ["sqj"])
            S_.add("dve", lambda h, b=b: h.reduce_sum(out=ssq[:, b:b + 1], in_=sqj, axis=AX.X), ["sqj"], ["mssq"])
        S_.add("dve", lambda h: h.tensor_scalar(out=ssq, in0=ssq, scalar1=1.0 / D, scalar2=EPS, op0=ALU.mult,
                                                 op1=ALU.add), ["mssq"], ["mssq"])
        S_.add("act", lambda h: h.activation(out=ssq, in_=ssq, func=AF.Ln), ["mssq"], ["mssq"])
        S_.add("act", lambda h: h.activation(out=ssq, in_=ssq, func=AF.Exp, scale=-0.5), ["mssq"], ["mssq"])
        for b in range(2):
            S_.add("dve", lambda h, b=b: h.scalar_tensor_tensor(out=mb[:, b, :], in0=mt[:, b, :], scalar=ssq[:, b:b + 1],
                                                                 in1=gb, op0=ALU.mult, op1=ALU.mult),
                   ["memt", "mssq", "memg"], ["mb"])
        psT8 = self.psA[:, 0:512].bitcast(BF16)
        for b in range(2):
            for c in range(NCH):
                S_.add("pe", lambda h, b=b, c=c: h.transpose(out=psT8[:, c * 128:(c + 1) * 128],
                                                               in_=mb[:, b, c * 128:(c + 1) * 128], identity=self.identB),
                       ["mb", "identB"], ["A0"])
            S_.add("dve", lambda h, b=b: h.tensor_copy(out=memT[:, :, b * 128:(b + 1) * 128],
                                                        in_=psT8.rearrange("p (c q) -> p c q", c=NCH)), ["A0"], ["memT"])
        S_.add("dve", lambda h: h.memset(vm[:, :, :, 64:128], 1.0), [], ["vm"])
        psTm = self.psA[:, 512:768].bitcast(BF16)
        for b in range(2):
            ps = self.psA[:, 0:512]
            for c in range(NCH):
                S_.add("pe", lambda h, b=b, c=c, ps=ps: h.matmul(ps, lhsT=memT[:, c, b * 128:(b + 1) * 128], rhs=wm[:, c, :],
                                                                  start=(c == 0), stop=(c == NCH - 1)), ["memT", "wm"], ["A0"])
            S_.add("dve", lambda h, ps=ps: h.tensor_copy(out=kmb, in_=ps[:, 0:256]), ["A0"], ["kmb"])
            S_.add("act", lambda h, b=b, ps=ps: h.activation(out=vm[:, b, :, 0:64],
                                                              in_=ps[:, 256:512].rearrange("p (h d) -> p h d", h=HM),
                                                              func=AF.Copy), ["A0"], ["vm"])
            for hh in range(HM):
                S_.add("pe", lambda h, hh=hh: h.transpose(out=psTm[0:64, hh * 128:(hh + 1) * 128], in_=kmb[:, hh * 64:(hh + 1) * 64],
                                                           identity=self.identB), ["kmb", "identB"], ["A1"])
            S_.add("dve", lambda h, b=b: h.tensor_copy(out=kmT[0:64, :, b * 128:(b + 1) * 128],
                                                        in_=psTm[0:64, 0:512].rearrange("p (h q) -> p h q", h=HM)), ["A1"], ["kmT"])
        if self.fused:
            self.step_exch(l)
        S_.barrier()
        self.aoff = m0
        gcol = self.gv[:, wi, 0:NCH]
        gpost = self.gv[:, wi, NCH:2 * NCH]
        xg = [self.alloc([NCH, T], F32) for _ in range(2)]
        hT = self.alloc([NCH, T], BF16)
        sq = self.alloc([NCH, T], BF16)
        fst = self.alloc([NCH, T], F32)
        qb = [self.alloc([16, 2 * HD], BF16) for _ in range(2)]
        QT = self.alloc([16, T], BF16)
        OT = self.alloc([NCH, T], BF16)
        self.OT = OT
        ost = [self.alloc([4, 512], F32) for _ in range(2)]
        self.rden = self.alloc([4, 512], F32)
        PT = [self.alloc([1024], BF16) for _ in range(3)]
        NKC = 2560 if kind == 2 else (min(1024, self.TPC) if kind == 1 else 1024)
        ktc = [self.alloc([NKC], BF16) for _ in range(2)]
        vc = [self.alloc([NKC // 128, 128], BF16) for _ in range(2)]
        tmp = self.alloc([4, H, 16], F32)
        tmpn = self.alloc([H, HD], F32)
        tmpx = self.alloc([H, HD], F32)
        stt = self.alloc([16], F32)
        psQT = self.psA[:, 0:512].bitcast(BF16)
        self.ptc = 0
        self.sc = 0
        self.kvc = 0
        self.oc = 0

        self.load_x(0, xg[0], "p2x0")
        for g in range(self.NG):
            xk = "p2x%d" % (g % 2)
            xgg = xg[g % 2]
            if g + 1 < self.NG:
                self.load_x(g + 1, xg[(g + 1) % 2], "p2x%d" % ((g + 1) % 2))
            self.prenorm(xgg, xk, gcol, hT, sq)
            for t in range(4):
                tt = g * 4 + t
                ps = self.psA
                for hf in range(2):
                    for c in range(NCH):
                        S_.add("pe", lambda h, c=c, t=t, hf=hf, ps=ps: h.matmul(ps[:, hf * 512:(hf + 1) * 512],
                                                                                 lhsT=hT[:, c, t * 128:(t + 1) * 128],
                                                                                 rhs=wq[:, c, hf * 512:(hf + 1) * 512],
                                                                                 start=(c == 0), stop=(c == NCH - 1)),
                               ["hT", "wq"], ["A%d" % hf])
                qbt = qb[t % 2]
                qk_ = "qb%d" % (t % 2)
                qsrc = ps[:, 0:QW].rearrange("p (h d) -> p h d", h=H)
                if kind == 1:
                    self.norm_rope_axial("dve", qsrc, qbt[:, 0:H, 0:HD], H, tt, ["A0", "A1"], qk_, 0, tmpn, tmp, stt, tmpx)
                else:
                    self.rope_partial("dve", qsrc, qbt[:, 0:H, 0:HD], H, tt, ["A0", "A1"], qk_, tmp)
                S_.add("act", lambda h, qbt=qbt, ps=ps: h.activation(out=qbt[:, H:16, 0:HD],
                                                                      in_=ps[:, QW:1024].rearrange("p (h d) -> p h d", h=HM),
                                                                      func=AF.Copy), ["A1"], [qk_])
                S_.add("dve", lambda h, qbt=qbt: h.tensor_copy(out=qbt[:, :, HD:2 * HD], in_=qbt[:, :, 0:HD]), [qk_], [qk_])
                for h0 in (0, 8):
                    for hh in range(h0, h0 + 8):
                        S_.add("pe", lambda h, hh=hh, h0=h0, qbt=qbt: h.transpose(
                            out=psQT[:, (hh - h0) * 128:(hh - h0 + 1) * 128], in_=qbt[:, hh, :], identity=self.identB),
                            [qk_, "identB"], ["A0"])
                    S_.add("act", lambda h, h0=h0, t=t: h.activation(
                        out=QT[:, h0:h0 + 8, t * 128:(t + 1) * 128],
                        in_=psQT.rearrange("p (h q) -> p h q", h=8), func=AF.Copy), ["A0"], ["QT"])
            if kind == 1:
                self.attn_global(g, ktb, vb, QT, PT, ktc, vc, ost, NKC)
            else:
                self.attn_window(g, kind, ktb, vb, QT, PT, ktc, vc, ost, NKC)
            for c in range(NCH):
                ps = self.psA[:, (c % 2) * 512:(c % 2 + 1) * 512]
                pk = "A%d" % (c % 2)
                for pr in range(NCH):
                    S_.add("pe", lambda h, c=c, pr=pr, ps=ps: h.matmul(ps, lhsT=wo[:, pr, c * 128:(c + 1) * 128], rhs=OT[:, pr, :],
                                                                        start=(pr == 0), stop=(pr == NCH - 1)), ["wo", "OT"], [pk])
                self.postnorm_chunk_in(c, ps, pk, fst, sq)
            self.postnorm_finish(xgg, xk, gpost, fst, sq)
            self.store_x(g, xgg, xk)

    def attn_pairs(self, qrhs, blocks, kt, ktkey, v, vkey, obank, okey, first, last):
        S_ = self.sched
        nb = len(blocks)
        i = 0
        while i < nb:
            pair = blocks[i:i + 2]
            sidx = self.sc % 2
            self.sc += 1
            sps = self.psS[:, sidx * 1024:(sidx + 1) * 1024]
            sk = "S%d" % sidx
            for j, (kcol, vblk, m) in enumerate(pair):
                pl = j * 64
                S_.add("pe", lambda h, j=j, kcol=kcol, sps=sps, pl=pl: h.matmul(
                    sps[:, j * 512:(j + 1) * 512].rearrange("p (h q) -> p h q", h=4),
                    lhsT=kt[pl:pl + 64, kcol:kcol + 128], rhs=qrhs[pl:pl + 64], start=True, stop=True),
                    [ktkey, "QT"], [sk])
            pidx = self.ptc % 3
            self.ptc += 1
            pt = self.PTl[pidx]
            pk = "PT%d" % pidx
            w = 512 * len(pair)
            S_.add("act", lambda h, pt=pt, sps=sps, w=w: h.activation(out=pt[:, 0:w], in_=sps[:, 0:w], func=AF.Exp, scale=0.125),
                   [sk], [pk])
            for j, (kcol, vblk, m) in enumerate(pair):
                if m is not None:
                    mb = self.masks[:, m, :].unsqueeze(1).to_broadcast([128, 4, 128])
                    S_.add("dve", lambda h, j=j, pt=pt, mb=mb: h.tensor_tensor(
                        out=pt[:, j * 512:(j + 1) * 512].rearrange("p (h q) -> p h q", h=4),
                        in0=pt[:, j * 512:(j + 1) * 512].rearrange("p (h q) -> p h q", h=4), in1=mb, op=ALU.mult),
                        [pk, "masks"], [pk])
            def pv(pair=pair, i=i, pt=pt, pk=pk):
                for j, (kcol, vblk, m) in enumerate(pair):
                    st = first and (i + j == 0)
                    sp = last and (i + j == nb - 1)
                    S_.add("pe", lambda h, j=j, vblk=vblk, pt=pt, st=st, sp=sp: h.matmul(obank, lhsT=v[:, vblk, :],
                                                                                          rhs=pt[:, j * 512:(j + 1) * 512],
                                                                                          start=st, stop=sp),
                           [vkey, pk], [okey])
            self.flush_pv()
            self.pending_pv = pv
            i += 2

    def load_kv(self, ktb, vb, kh, k0, n, ktc, vc):
        s = self.kvc % 2
        self.kvc += 1
        kk = "ktc%d" % s
        vk = "vc%d" % s
        TPC = self.TPC
        R = KVH * HD
        rk_ = [k for k in self.kvkeys[0:1] if k]
        rv_ = [k for k in self.kvkeys[1:2] if k]
        if self.fused and self.cur_kind == 1:
            NPR = self.NPR
            VCH = self.VCH
            rk, off = k0 // TPC, k0 % TPC
            for hf in range(2):
                self.dma("sp", ktc[s][hf * 64:(hf + 1) * 64, 0:n], ktb[kh][(rk + NPR) * HD:(rk + NPR + 1) * HD, off:off + n],
                         rk_, [kk], kk)
            for o2 in range(0, n, VCH):
                c = (off + o2) // VCH
                self.dma("sp", vc[s][:, o2 // 128:(o2 + VCH) // 128, :],
                         vb[c][(rk + NPR) * VCH:(rk + NPR + 1) * VCH, kh * 128:(kh + 1) * 128].rearrange(
                             "(b p) c -> p b c", p=128), rv_, [vk], vk)
        else:
            for hf in range(2):
                self.dma("sp", ktc[s][hf * 64:(hf + 1) * 64, 0:n], ktb[kh, :, k0:k0 + n], rk_, [kk], kk)
            self.dma("sp", vc[s][:, 0:n // 128, :], vb[k0:k0 + n, kh, :].rearrange("(b p) c -> p b c", p=128), rv_, [vk], vk)
        return ktc[s], kk, vc[s], vk

    def flush_pv(self):
        p = getattr(self, "pending_pv", None)
        self.pending_pv = None
        if p is not None:
            p()

    def evac_o(self, obank, okey, ostt, ostk, gi):
        S_ = self.sched
        self.flush_pv()
        S_.add("act", lambda h: h.activation(out=ostt[:, gi, :], in_=obank, func=AF.Copy), [okey], [ostk])

    def mem_and_finish(self, kind, t, QT, ostt, ostk):
        S_ = self.sched
        kmT, vm = self.kmT, self.vm
        sidx = self.sc % 2
        self.sc += 1
        sps = self.psS[:, sidx * 1024:(sidx + 1) * 1024]
        sk = "S%d" % sidx
        for b in range(2):
            for hh in range(HM):
                S_.add("pe", lambda h, b=b, hh=hh: h.matmul(sps[:, b * 512 + hh * 128: b * 512 + (hh + 1) * 128],
                                                             lhsT=kmT[0:64, hh, b * 128:(b + 1) * 128],
                                                             rhs=QT[0:64, H + hh, t * 128:(t + 1) * 128], start=True, stop=True),
                       ["kmT", "QT"], [sk])
        pidx = self.ptc % 3
        self.ptc += 1
        pt = self.PTl[pidx]
        pk = "PT%d" % pidx
        S_.add("act", lambda h: h.activation(out=pt, in_=sps, func=AF.Exp, scale=0.125), [sk], [pk])
        self.flush_pv()
        oi = self.oc % 2
        self.oc += 1
        obank = self.psO[:, oi * 512:(oi + 1) * 512]
        okey = "O%d" % oi
        for hh in range(HM):
            for b in range(2):
                S_.add("pe", lambda h, b=b, hh=hh: h.matmul(obank[:, hh * 128:(hh + 1) * 128], lhsT=vm[:, b, hh, :],
                                                             rhs=pt[:, b * 512 + hh * 128: b * 512 + (hh + 1) * 128],
                                                             start=(b == 0), stop=(b == 1)), ["vm", pk], [okey])
        self.evac_o(obank, okey, ostt, ostk, 3)
        den = ostt[64:128, :, :]
        rden = self.rden[0:64, :, :]
        if kind == 0:
            sb = self.sinke[64:128, 0:H].rearrange("p (g h) -> p g h", g=3).unsqueeze(3).to_broadcast([64, 3, 4, 128])
            S_.add("dve", lambda h: h.tensor_tensor(out=den[:, 0:3, :].rearrange("p g (h q) -> p g h q", h=4),
                                                     in0=den[:, 0:3, :].rearrange("p g (h q) -> p g h q", h=4), in1=sb, op=ALU.add),
                   [ostk, "sinke"], [ostk])
        if kind == 2:
            S_.add("dve", lambda h: h.tensor_tensor(out=den[:, 0, :], in0=den[:, 0, :], in1=den[:, 1, :], op=ALU.add), [ostk], [ostk])
            S_.add("dve", lambda h: h.tensor_tensor(out=den[:, 0, :], in0=den[:, 0, :], in1=den[:, 2, :], op=ALU.add), [ostk], [ostk])
            S_.add("dve", lambda h: h.tensor_copy(out=den[:, 1, :], in_=den[:, 0, :]), [ostk], [ostk])
            S_.add("dve", lambda h: h.tensor_copy(out=den[:, 2, :], in_=den[:, 0, :]), [ostk], [ostk])
        S_.add("act", lambda h: h.activation(out=rden, in_=den, func=AF.Ln), [ostk], ["rden"])
        S_.add("act", lambda h: h.activation(out=rden, in_=rden, func=AF.Exp, scale=-1.0), ["rden"], ["rden"])
        OT = self.OT
        for par in range(2):
            num = ostt[0:64, :, :].rearrange("p g (a b q) -> p g a b q", a=2, b=2)[:, :, :, par, :]
            rd = self.rden[0:64, :, :].rearrange("p g (a b q) -> p g a b q", a=2, b=2)[:, :, :, par, :]
            dst = OT[par * 64:(par + 1) * 64, :, t * 128:(t + 1) * 128].rearrange("p (g a) q -> p g a q", g=4)
            S_.add("dve", lambda h, num=num, rd=rd, dst=dst: h.tensor_tensor(out=dst, in0=num, in1=rd, op=ALU.mult),
                   [ostk, "rden"], ["OT"])

    def attn_window(self, g, kind, ktb, vb, QT, PT, ktc, vc, ost, NKC):
        self.PTl = PT
        base = PAD + g * T
        if kind == 0:
            specs = [(kh, 128, 1) for kh in range(KVH)]
        else:
            specs = [(0, 64, 1), (1, 256, 4), (2, 1024, 16)]
        for tp in range(2):
            tiles = [2 * tp, 2 * tp + 1]
            for (kh, halo, dil) in specs:
                k0 = base + tiles[0] * 128 - halo
                n = 256 + 2 * halo
                kt, kk, v, vk = self.load_kv(ktb, vb, kh, k0, n, ktc, vc)
                for ti, t in enumerate(tiles):
                    off = ti * 128
                    if kind == 0:
                        bl = [(off, 0), (off + 128, None), (off + 256, 1)]
                    elif dil == 1:
                        bl = [(off, 0), (off + 128, 1)]
                    elif dil == 4:
                        ms = [3, 2, 2, 2, 4]
                        bl = [(off + d * 128, ms[d]) for d in range(5)]
                    else:
                        ms = [6] + [5] * 15 + [7]
                        bl = [(off + d * 128, ms[d]) for d in range(17)]
                    blocks = [(kc, kc // 128, m) for (kc, m) in bl]
                    oi = self.oc % 2
                    self.oc += 1
                    obank = self.psO[:, oi * 512:(oi + 1) * 512]
                    okey = "O%d" % oi
                    qrhs = QT[:, 4 * kh:4 * kh + 4, t * 128:(t + 1) * 128]
                    self.attn_pairs(qrhs, blocks, kt, kk, v, vk, obank, okey, True, True)
                    self.evac_o(obank, okey, ost[ti], "ost%d" % ti, kh)
            for ti, t in enumerate(tiles):
                self.mem_and_finish(kind, t, QT, ost[ti], "ost%d" % ti)

    def attn_global(self, g, ktb, vb, QT, PT, ktc, vc, ost, NKC):
        self.PTl = PT
        S = self.S
        nchunk = S // NKC
        for tp in range(2):
            tiles = [2 * tp, 2 * tp + 1]
            for kh in range(KVH):
                for ci in range(nchunk):
                    kt, kk, v, vk = self.load_kv(ktb, vb, kh, ci * NKC, NKC, ktc, vc)
                    blocks = [(b * 128, b, None) for b in range(NKC // 128)]
                    for ti, t in enumerate(tiles):
                        obank = self.psO[:, ti * 512:(ti + 1) * 512]
                        okey = "O%d" % ti
                        qrhs = QT[:, 4 * kh:4 * kh + 4, t * 128:(t + 1) * 128]
                        self.attn_pairs(qrhs, blocks, kt, kk, v, vk, obank, okey, ci == 0, ci == nchunk - 1)
                for ti, t in enumerate(tiles):
                    self.evac_o(self.psO[:, ti * 512:(ti + 1) * 512], "O%d" % ti, ost[ti], "ost%d" % ti, kh)
            self.oc = 0
            for ti, t in enumerate(tiles):
                self.mem_and_finish(1, t, QT, ost[ti], "ost%d" % ti)

    def step_ffn(self, l, wi):
        S_ = self.sched
        dr = self.dr
        wgu = self.alloc([NCH, 2 * DFF], BF16)
        wdn = self.alloc([NF, D], BF16)
        m0 = self.aoff
        self.load_weight(wgu, dr["w_gu"][wi], 2 * DFF, NCH, "wgu", colchunk=1408)
        self.load_weight(wdn, dr["w_dn"][wi], D, NF, "wdn", colchunk=1024)
        S_.barrier()
        self.aoff = m0
        gcol = self.gv[:, wi, 2 * NCH:3 * NCH]
        gpost = self.gv[:, wi, 3 * NCH:4 * NCH]
        xg = self.alloc([NCH, T], F32)
        hT = self.alloc([NCH, T], BF16)
        act = self.alloc([NF, T], BF16)
        fst = self.alloc([NCH, T], F32)
        sil = [self.alloc([T], BF16) for _ in range(2)]
        for g in range(self.NG):
            xk = "fx"
            self.load_x(g, xg, xk)
            self.prenorm(xg, xk, gcol, hT, act[:, 0:NCH, :], "act")
            for j in range(NF):
                gi = j % 2
                gps = self.psS[:, gi * 1024:gi * 1024 + 512]
                ups = self.psS[:, gi * 1024 + 512:(gi + 1) * 1024]
                gk = "S%d" % gi
                for c in range(NCH):
                    S_.add("pe", lambda h, c=c, j=j, gps=gps: h.matmul(gps, lhsT=wgu[:, c, j * 128:(j + 1) * 128], rhs=hT[:, c, :],
                                                                        start=(c == 0), stop=(c == NCH - 1)), ["wgu", "hT"], [gk + "g"])
                for c in range(NCH):
                    S_.add("pe", lambda h, c=c, j=j, ups=ups: h.matmul(ups, lhsT=wgu[:, c, DFF + j * 128:DFF + (j + 1) * 128],
                                                                        rhs=hT[:, c, :], start=(c == 0), stop=(c == NCH - 1)),
                           ["wgu", "hT"], [gk + "u"])
                sl = sil[gi]
                S_.add("act", lambda h, sl=sl, gps=gps: h.activation(out=sl, in_=gps, func=AF.Silu), [gk + "g"], ["sil%d" % gi])
                S_.add("dve", lambda h, sl=sl, ups=ups, j=j: h.tensor_tensor(out=act[:, j, :], in0=sl, in1=ups, op=ALU.mult),
                       ["sil%d" % gi, gk + "u"], ["act"])
            for c in range(NCH):
                ps = self.psA[:, (c % 2) * 512:(c % 2 + 1) * 512]
                pk = "A%d" % (c % 2)
                for j in range(NF):
                    S_.add("pe", lambda h, c=c, j=j, ps=ps: h.matmul(ps, lhsT=wdn[:, j, c * 128:(c + 1) * 128], rhs=act[:, j, :],
                                                                      start=(j == 0), stop=(j == NF - 1)), ["wdn", "act"], [pk])
                self.postnorm_chunk_in(c, ps, pk, fst, hT, "hT")
            self.postnorm_finish(xg, xk, gpost, fst, hT, "hT")
            self.store_x(g, xg, xk)


def _rope_tables(S):
    pos = np.arange(S, dtype=np.float32)
    inv = (500000.0 ** (-(np.arange(0, 16, 2, dtype=np.float32) / 16))).astype(np.float32)
    ang = pos[:, None] * inv[None, :]
    ropep = np.concatenate([np.cos(ang), np.sin(ang)], axis=1).astype(np.float32)
    rows = (np.arange(S) // GRID_W).astype(np.float32)
    cols = (np.arange(S) % GRID_W).astype(np.float32)
    inv2 = (10000.0 ** (-(np.arange(0, 32, 2, dtype=np.float32) / 32))).astype(np.float32)
    ar = rows[:, None] * inv2[None, :]
    ac = cols[:, None] * inv2[None, :]
    ropea = np.concatenate([np.cos(ar), np.cos(ac), np.sin(ar), np.sin(ac)], axis=1).astype(np.float32)
    return ropep, ropea


def _masks():
    j = np.arange(128)[:, None]
    i = np.arange(128)[None, :]
    ge = (j >= i)
    le = (j <= i)
    m4 = ((j - i) % 4 == 0)
    m16 = ((j - i) % 16 == 0)
    ms = [ge, le, m4, m4 & ge, m4 & le, m16, m16 & ge, m16 & le]
    return np.stack(ms).astype(np.float32)


_PROG_CACHE = {}


def _get_prog(S, steps, nlw):
    k = (S, tuple(steps), nlw)
    if k not in _PROG_CACHE:
        b = Builder(S, list(steps), nlw)
        _PROG_CACHE[k] = (b.build(), b.in_names)
    return _PROG_CACHE[k]


def run_model(inputs, S, depth, fused=False):
    x = np.asarray(inputs["x"], dtype=np.float32)
    B = x.shape[0]
    TPC = S // 4
    ropep, ropea = _rope_tables(S)
    masks = _masks()
    ident = np.eye(128, dtype=np.float32)

    def gv_of(l):
        parts = [inputs["g_mix_pre"][l], inputs["g_mix_post"][l], inputs["g_ffn_pre"][l], inputs["g_ffn_post"][l]]
        return np.concatenate([np.asarray(p, np.float32).reshape(NCH, 128).T for p in parts], axis=1)

    def wmaps(layers):
        d = {}
        d["w_in"] = np.ascontiguousarray(np.asarray(inputs["w_in"])[layers])
        d["w_mem_kv"] = np.ascontiguousarray(np.asarray(inputs["w_mem_kv"])[layers])
        d["w_o"] = np.ascontiguousarray(np.asarray(inputs["w_o"])[layers])
        d["w_gate_up"] = np.ascontiguousarray(np.asarray(inputs["w_gate_up"])[layers])
        d["w_down"] = np.ascontiguousarray(np.asarray(inputs["w_down"])[layers])
        d["gv"] = np.stack([gv_of(l) for l in layers]).astype(np.float32)
        sk = np.zeros((len(layers), 16), np.float32)
        qk = np.ones((len(layers), 2 * HD), np.float32)
        for i, l in enumerate(layers):
            if l % 3 == 0:
                sk[i, :H] = np.asarray(inputs["attn_sink"])[l // 3]
            if l % 3 == 1:
                qk[i] = np.asarray(inputs["qk_norm_g"])[l // 3].reshape(-1)
        d["sink"] = sk
        d["qkg"] = qk
        return d

    def core_common(c):
        b, r = c // 4, c % 4
        return {
            "mem": np.ascontiguousarray(np.asarray(inputs["mem"], np.float32)[b]),
            "mem_g": np.asarray(inputs["mem_norm_g"], np.float32),
            "ropep": np.ascontiguousarray(ropep[r * TPC:(r + 1) * TPC]),
            "ropea": np.ascontiguousarray(ropea[r * TPC:(r + 1) * TPC]),
            "cident": ident, "cmask": masks,
        }

    plans = []
    for l in range(depth + 1):
        steps = []
        layers = []
        if l == 0:
            steps.append(("init",))
        else:
            layers.append(l - 1)
            steps.append(("p2", l - 1, 0))
            steps.append(("ffn", l - 1, 0))
        if l < depth:
            if l not in layers:
                layers.append(l)
            steps.append(("p1", l, layers.index(l)))
        else:
            steps.append(("fin",))
        plans.append((steps, layers))
    if depth == 0:
        plans = [([("init",), ("fin",)], [0])]
    if fused and depth > 0:
        steps = [("init",)]
        for l in range(depth):
            steps += [("p1", l, l), ("exch", l), ("p2", l, l), ("ffn", l, l)]
        steps.append(("fin",))
        plans = [(steps, list(range(depth)))]

    xt = None
    kv = None
    y = None
    for steps, layers in plans:
        nc, in_names = _get_prog(S, steps, len(layers))
        wm = wmaps(layers)
        in_maps = []
        for c in range(NCORES):
            b, r = c // 4, c % 4
            m = dict(wm)
            m.update(core_common(c))
            if steps[0][0] == "init":
                m["x_in"] = np.ascontiguousarray(x[b, r * TPC:(r + 1) * TPC])
            else:
                m["xt_in"] = xt[c]
            for st in steps:
                if st[0] == "p2" and not fused:
                    l = st[1]
                    ktf, vf = kv[b]
                    if l % 3 == 1:
                        m["kt_buf%d" % l] = ktf
                        m["v_buf%d" % l] = vf
                    else:
                        kp = np.zeros((KVH, HD, TPC + 2 * PAD), ml_dtypes.bfloat16)
                        vp = np.zeros((TPC + 2 * PAD, KVH, 128), ml_dtypes.bfloat16)
                        lo = r * TPC - PAD
                        hi = (r + 1) * TPC + PAD
                        slo, shi = max(lo, 0), min(hi, S)
                        kp[:, :, slo - lo:shi - lo] = ktf[:, :, slo:shi]
                        vp[slo - lo:shi - lo] = vf[slo:shi]
                        m["kt_buf%d" % l] = kp
                        m["v_buf%d" % l] = vp
            in_maps.append({k_: v_ for k_, v_ in m.items() if k_ in in_names})
        res = run_bass_kernel_spmd(nc, in_maps, core_ids=list(range(NCORES)))
        outs = res.results
        if "xt_out" in outs[0]:
            xt = [np.asarray(outs[c]["xt_out"]) for c in range(NCORES)]
        if "kt_out" in outs[0]:
            kv = []
            for b in range(B):
                ktf = np.concatenate([np.asarray(outs[b * 4 + r]["kt_out"]) for r in range(4)], axis=2)
                vf = np.concatenate([np.asarray(outs[b * 4 + r]["v_out"]) for r in range(4)], axis=0)
                kv.append((np.ascontiguousarray(ktf), np.ascontiguousarray(vf)))
        if "y_out" in outs[0]:
            y = np.stack([np.concatenate([np.asarray(outs[b * 4 + r]["y_out"]) for r in range(4)], axis=0) for b in range(B)])
    return y.astype(np.float32)


def kernel(**inputs):
    return run_model(inputs, 16384, DEPTH, fused=FUSED)
```

```python
import numpy as np
import ml_dtypes
import concourse.bass as bass
import concourse.mybir as mybir
from concourse.bass_utils import run_bass_kernel_spmd

F32 = mybir.dt.float32
BF16 = mybir.dt.bfloat16
AF = mybir.ActivationFunctionType
ALU = mybir.AluOpType
AX = mybir.AxisListType

D = 1024
NCH = 8
HD = 64
H = 12
KVH = 3
HM = 4
NMEM = 256
QW = 768
KVW = 192
INW = 1408
DFF = 2816
NF = 22
EPS = 1e-6
PAD = 1152
T = 512
DEPTH = 4
GRID_W = 64
NCORES = 8
import os
FUSED = bool(int(os.environ.get('KFUSED', '1')))
NOPOOL = bool(int(os.environ.get('NOPOOL', '1')))


PSUM_KEYS = {"A0", "A1", "S0", "S1", "S0g", "S0u", "S1g", "S1u", "O0", "O1"}


class Op:
    __slots__ = ("eng", "fn", "deps", "flag", "idx", "dma", "dma_val", "real", "inc")

    def __init__(self, eng, fn):
        self.eng = eng
        self.fn = fn
        self.deps = []
        self.flag = False
        self.idx = 0
        self.dma = None
        self.dma_val = 0
        self.real = True
        self.inc = 16


class Sched:
    ENGS = ("pe", "act", "dve", "pool", "sp")

    def __init__(self):
        self.ops = {e: [] for e in self.ENGS}
        self.res = {}
        self.dma_cnt = {}
        self.pending_dma = []
        self.last_real = {e: None for e in self.ENGS}
        self.pre = {}

    def add(self, eng, fn, reads=(), writes=(), dma=None, inc=16):
        op = Op(eng, fn)
        op.inc = inc
        if dma is not None:
            op.dma = dma
            c = self.dma_cnt.get(dma, 0) + inc
            self.dma_cnt[dma] = c
            op.dma_val = c
        deps = {}
        px = [k for k in reads if k in PSUM_KEYS and k not in writes]
        if px:
            writes = list(writes) + px
        for k in reads:
            r = self.res.get(k)
            if r is not None and r[0] is not None:
                deps.setdefault(id(r[0]), [r[0], set()])[1].add("raw")
        for k in writes:
            r = self.res.get(k)
            if r is not None:
                if r[0] is not None:
                    deps.setdefault(id(r[0]), [r[0], set()])[1].add("waw")
                for rd in r[1].values():
                    deps.setdefault(id(rd), [rd, set()])[1].add("war")
                for rd in r[2]:
                    deps.setdefault(id(rd), [rd, set()])[1].add("war")
        for d, kinds in deps.values():
            if d is op:
                continue
            if d.dma is None and op.dma is None and d.eng == eng:
                if eng == "pe":
                    continue
            op.deps.append(d)
            d.flag = True
        for k in reads:
            r = self.res.setdefault(k, [None, {}, []])
            if op.dma is None:
                r[1][eng] = op
            else:
                r[2].append(op)
        for k in writes:
            self.res[k] = [op, {}, []]
        self.ops[eng].append(op)
        if op.dma is not None:
            self.pending_dma.append(op)
        else:
            self.last_real[eng] = op
        return op

    def barrier(self):
        lasts = [o for o in self.last_real.values() if o is not None]
        pl = {}
        for o in self.pending_dma:
            pl[o.dma] = o
        pend = list(pl.values())
        self.pending_dma = []
        for e in self.ENGS:
            op = Op(e, lambda h: h.nop())
            op.real = False
            for d in lasts:
                if d.eng != e or e != "pe":
                    op.deps.append(d)
                    d.flag = True
            for d in pend:
                op.deps.append(d)
            self.ops[e].append(op)

    def emit(self, nc, stack):
        esem = {e: stack.enter_context(nc.semaphore("s_" + e)) for e in self.ENGS}
        dsem = {k: stack.enter_context(nc.semaphore("d_" + str(i))) for i, k in enumerate(self.dma_cnt)}
        for e in self.ENGS:
            c = 0
            for op in self.ops[e]:
                if op.dma is None and op.flag:
                    c += 1
                    op.idx = c
        block = stack.enter_context(nc.Block())

        def run(e, h):
            known = {}
            if self.pre.get(e):
                self.pre[e](h)
            for op in self.ops[e]:
                for d in op.deps:
                    if d.dma is not None:
                        key, sem, val = ("d", d.dma), dsem[d.dma], d.dma_val
                    else:
                        key, sem, val = ("e", d.eng), esem[d.eng], d.idx
                    if known.get(key, 0) < val:
                        h.wait_ge(sem, val)
                        known[key] = val
                ins = op.fn(h)
                if op.dma is not None:
                    ins.then_inc(dsem[op.dma], op.inc)
                elif op.flag:
                    ins.then_inc(esem[e], 1)

        @block.sync
        def _(h):
            run("sp", h)

        @block.tensor
        def _(h):
            run("pe", h)

        @block.scalar
        def _(h):
            run("act", h)

        @block.vector
        def _(h):
            run("dve", h)

        @block.gpsimd
        def _(h):
            run("pool", h)


class Builder:
    def __init__(self, S, steps, nlayers_w):
        self.S = S
        self.TPC = S // 4
        self.NG = self.TPC // T
        self.steps = steps
        self.nlw = nlayers_w
        self.nc = bass.Bass("TRN2", target_bir_lowering=False)
        self.sched = Sched()
        self.uid = 0
        self.in_names = set()

    def key(self, base):
        self.uid += 1
        return "%s#%d" % (base, self.uid)

    def dram_in(self, name, shape, dt=F32):
        self.in_names.add(name)
        return self.nc.dram_tensor(name, list(shape), dt, kind="ExternalInput").ap()

    def dram_out(self, name, shape, dt=F32):
        return self.nc.dram_tensor(name, list(shape), dt, kind="ExternalOutput").ap()

    def alloc(self, shape, dt, parts=128):
        n = int(np.prod(shape))
        nbytes = n * (4 if dt == F32 else 2)
        nbytes = (nbytes + 63) // 64 * 64
        off = self.aoff
        self.aoff += nbytes
        assert self.aoff <= self.acap, ("arena overflow", self.aoff, self.acap)
        v = self.arena[0:parts, off // 2:(off + n * (4 if dt == F32 else 2)) // 2]
        if dt == F32:
            v = v.bitcast(F32)
        if len(shape) == 2:
            v = v.rearrange("p (a b) -> p a b", a=shape[0])
        elif len(shape) == 3:
            v = v.rearrange("p (a b c) -> p a b c", a=shape[0], b=shape[1])
        return v

    def dma(self, q, out, in_, reads, writes, key):
        self.sched.add(q, lambda h: h.dma_start(out=out, in_=in_), reads=reads, writes=writes, dma=q + ":" + key)

    def build(self):
        import contextlib
        nc = self.nc
        S_ = self.sched
        TPC, NG = self.TPC, self.NG
        kinds = set(st[0] for st in self.steps)
        has_init = "init" in kinds
        has_fin = "fin" in kinds
        has_p1 = "p1" in kinds
        has_p2 = "p2" in kinds
        has_ffn = "ffn" in kinds
        fused = "exch" in kinds
        self.fused = fused
        self.rank_vals = {}
        if fused:
            def _pre(h):
                npr = -(-PAD // self.TPC)
                r = h.partition_id() % 4
                self.rank_vals = {j: h.snap(r + (j + npr), min_val=j + npr, max_val=j + npr + 3) for j in range(-npr, npr + 1)}
            self.sched.pre["sp"] = _pre
        nlw = self.nlw
        dr = {}
        if has_init:
            dr["x_in"] = self.dram_in("x_in", [TPC, D])
            dr["xt"] = self.dram_out("xt_out", [NCH, 128, TPC]) if not has_fin else None
        else:
            dr["xt_in"] = self.dram_in("xt_in", [NCH, 128, TPC])
        if has_fin:
            dr["y_out"] = self.dram_out("y_out", [TPC, D])
        if not has_init and not has_fin:
            dr["xt"] = self.dram_out("xt_out", [NCH, 128, TPC])
        if has_init and has_fin:
            dr["xt"] = nc.dram_tensor("xt_scr", [NCH, 128, TPC], F32).ap()
        if not has_init and has_fin:
            dr["xt"] = nc.dram_tensor("xt_scr", [NCH, 128, TPC], F32).ap()
        if has_p1 or has_p2:
            dr["w_in"] = self.dram_in("w_in", [nlw, D, INW])
        if has_p2:
            dr["w_mem_kv"] = self.dram_in("w_mem_kv", [nlw, D, 512])
            dr["w_o"] = self.dram_in("w_o", [nlw, D, D])
        if has_ffn:
            dr["w_gu"] = self.dram_in("w_gate_up", [nlw, D, 2 * DFF])
            dr["w_dn"] = self.dram_in("w_down", [nlw, DFF, D])
        dr["gv"] = self.dram_in("gv", [nlw, 128, 4 * NCH])
        dr["sink"] = self.dram_in("sink", [nlw, 16])
        dr["qkg"] = self.dram_in("qkg", [nlw, 2 * HD])
        dr["mem"] = self.dram_in("mem", [NMEM, D])
        dr["mem_g"] = self.dram_in("mem_g", [D])
        dr["ropep"] = self.dram_in("ropep", [TPC, 16])
        dr["ropea"] = self.dram_in("ropea", [TPC, 64])
        dr["cident"] = self.dram_in("cident", [128, 128])
        dr["cmask"] = self.dram_in("cmask", [8, 128, 128])
        if fused:
            self.NPR = -(-PAD // TPC)
            NR = 4 + 2 * self.NPR
            for i in range(2):
                dr["kt_loc%d" % i] = nc.dram_tensor("kt_loc%d" % i, [KVH * HD, TPC], BF16).ap()
                dr["v_loc%d" % i] = nc.dram_tensor("v_loc%d" % i, [TPC, KVH * 128], BF16).ap()
                self.VCH = min(1024, TPC)
                self.NVC = TPC // self.VCH
                for kh in range(KVH):
                    dr["ktpad%d_%d" % (i, kh)] = nc.dram_tensor("ktpad%d_%d" % (i, kh), [NR * HD, TPC], BF16).ap()
                for c in range(self.NVC):
                    dr["vpad%d_%d" % (i, c)] = nc.dram_tensor("vpad%d_%d" % (i, c), [NR * self.VCH, KVH * 128], BF16).ap()
            dr["kt_rel"] = nc.dram_tensor("kt_rel", [KVH, HD, TPC + 2 * PAD], BF16).ap()
            dr["v_rel"] = nc.dram_tensor("v_rel", [TPC + 2 * PAD, KVH, 128], BF16).ap()
        elif has_p1:
            dr["kt_out"] = self.dram_out("kt_out", [KVH, HD, TPC], BF16)
            dr["v_out"] = self.dram_out("v_out", [TPC, KVH, 128], BF16)
        if has_p2 and not fused:
            self.NKEYS = {}
            for st in self.steps:
                if st[0] == "p2":
                    l = st[1]
                    nk = self.S if l % 3 == 1 else TPC + 2 * PAD
                    dr["kt_buf%d" % l] = self.dram_in("kt_buf%d" % l, [KVH, HD, nk], BF16)
                    dr["v_buf%d" % l] = self.dram_in("v_buf%d" % l, [nk, KVH, 128], BF16)
        self.dr = dr

        stack = contextlib.ExitStack()
        with stack:
            self.acap = 205 * 1024
            self.arena = stack.enter_context(nc.sbuf_tensor("arena", [128, self.acap // 2], BF16))
            self.aoff = 0
            self.psA = stack.enter_context(nc.psum_tensor("psA", [128, 1024], F32))
            self.psS = stack.enter_context(nc.psum_tensor("psS", [128, 2048], F32))
            self.psO = stack.enter_context(nc.psum_tensor("psO", [128, 1024], F32))
            self.setup_consts()
            self.pmark = self.aoff
            if fused:
                self.zero_pads()
            self.xsrc = "x_in" if has_init else "xt_in"
            first_x = True
            for st in self.steps:
                self.aoff = self.pmark
                S_.barrier()
                if st[0] == "init":
                    self.step_init()
                    self.xsrc = "xt"
                elif st[0] == "p1":
                    self.step_p1(st[1], st[2])
                elif st[0] == "p2":
                    self.step_p2(st[1], st[2])
                    self.xsrc = "xt"
                elif st[0] == "ffn":
                    self.step_ffn(st[1], st[2])
                    self.xsrc = "xt"
                elif st[0] == "exch":
                    pass
                elif st[0] == "fin":
                    self.step_fin()
            S_.barrier()
            S_.emit(nc, stack)
        return nc

    def setup_consts(self):
        S_ = self.sched
        dr = self.dr
        self.identF = self.alloc([128], F32)
        self.identB = self.alloc([128], BF16)
        self.onesD = self.alloc([128], BF16)
        self.masks = self.alloc([8, 128], BF16)
        self.gv = self.alloc([self.nlw, 4 * NCH], F32)
        self.rstd = self.alloc([T], F32)
        mark = self.aoff
        stg = self.alloc([8, 128], F32)
        self.dma("sp", self.identF, dr["cident"], [], ["identF"], "c0")
        self.dma("sp", stg, dr["cmask"].rearrange("m p q -> p m q"), [], ["cstg"], "c1")
        self.dma("sp", self.gv, dr["gv"].rearrange("l p c -> p l c"), [], ["gv"], "c2")
        S_.add("dve", lambda h: h.tensor_copy(out=self.identB, in_=self.identF), ["identF"], ["identB"])
        S_.add("dve", lambda h: h.memset(self.onesD, 1.0 / D), [], ["onesD"])
        S_.add("dve", lambda h: h.tensor_copy(out=self.masks, in_=stg), ["cstg"], ["masks"])
        S_.barrier()
        self.aoff = mark

    def zero_pads(self):
        S_ = self.sched
        dr = self.dr
        TPC = self.TPC
        NPR = self.NPR
        VCH = self.VCH
        m = self.aoff
        zt = self.alloc([max(TPC, (VCH // 128) * KVH * 128)], BF16)
        S_.add("dve", lambda h: h.memset(zt, 0.0), [], ["zt"])
        for i in range(2):
            for blk in list(range(NPR)) + list(range(NPR + 4, 2 * NPR + 4)):
                for kh in range(KVH):
                    self.dma("pool", dr["ktpad%d_%d" % (i, kh)][blk * HD:(blk + 1) * HD, :], zt[0:64, 0:TPC], ["zt"],
                             [self.key("zpk")], "zp")
                for c in range(self.NVC):
                    nj = VCH // 128
                    base = blk * VCH
                    q = "sp" if c % 2 == 0 else "pool"
                    self.dma(q, dr["vpad%d_%d" % (i, c)][base:base + VCH, :].rearrange("(j p) c -> p j c", p=128),
                             zt[:, 0:nj * KVH * 128].rearrange("p (j c) -> p j c", j=nj), ["zt"], [self.key("zpv")], "zp")
        S_.barrier()
        self.aoff = m

    def step_exch(self, l):
        S_ = self.sched
        dr = self.dr
        i = l % 2
        TPC = self.TPC
        NPR = self.NPR
        VCH = self.VCH
        groups = [[0, 1, 2, 3], [4, 5, 6, 7]]
        for kh in range(KVH):
            kin = dr["kt_loc%d" % i][kh * HD:(kh + 1) * HD, :]
            kout = dr["ktpad%d_%d" % (i, kh)][NPR * HD:(NPR + 4) * HD, :]
            S_.add("pool", lambda h, kin=kin, kout=kout: h.collective_compute(
                "AllGather", ALU.bypass, replica_groups=groups, ins=[kin], outs=[kout]),
                ["ktdram"], ["ktpad%d" % i], dma="pool:cck", inc=1)
        for c in range(self.NVC):
            vin = dr["v_loc%d" % i][c * VCH:(c + 1) * VCH, :]
            vout = dr["vpad%d_%d" % (i, c)][NPR * VCH:(NPR + 4) * VCH, :]
            S_.add("pool", lambda h, vin=vin, vout=vout: h.collective_compute(
                "AllGather", ALU.bypass, replica_groups=groups, ins=[vin], outs=[vout]),
                ["vdram"], ["vpad%d" % i], dma="pool:ccv", inc=1)
        if l % 3 == 1:
            return
        vrel2 = dr["v_rel"].rearrange("t h c -> t (h c)")
        for j in range(-NPR, NPR + 1):
            a = max(0, -PAD - j * TPC)
            b = min(TPC, TPC + PAD - j * TPC)
            if a >= b:
                continue
            dst = j * TPC + a + PAD
            for kh in range(KVH):
                kp3 = dr["ktpad%d_%d" % (i, kh)].rearrange("(n r) t -> n r t", r=HD)
                S_.add("sp", lambda h, j=j, a=a, b=b, dst=dst, kp3=kp3, kh=kh: h.dma_start(
                    out=dr["kt_rel"][kh, :, dst:dst + (b - a)], in_=kp3[bass.ds(self.rank_vals[j], 1), :, a:b]),
                    ["ktpad%d" % i], ["ktrel"], dma="sp:krel")
            for c in range(self.NVC):
                lo, hi = max(a, c * VCH), min(b, (c + 1) * VCH)
                if lo >= hi:
                    continue
                vp3 = dr["vpad%d_%d" % (i, c)].rearrange("(n t) c -> n t c", t=VCH)
                d2 = j * TPC + lo + PAD
                S_.add("sp", lambda h, j=j, lo=lo, hi=hi, d2=d2, vp3=vp3, c=c: h.dma_start(
                    out=vrel2[d2:d2 + (hi - lo), :], in_=vp3[bass.ds(self.rank_vals[j], 1), lo - c * VCH:hi - c * VCH, :]),
                    ["vpad%d" % i], ["vrel"], dma="sp:vrel")

    def rank_of(self, h):
        return self.rank_vals[id(h)]

    def load_weight(self, dst, src_rows, ncols, nchunks, keyname, col0=0, colchunk=2048):
        S_ = self.sched
        stg = [self.alloc([colchunk], F32) for _ in range(3)]
        i = 0
        for c in range(nchunks):
            for j0 in range(0, ncols, colchunk):
                n = min(colchunk, ncols - j0)
                s = i % 3
                sk = "wstg%d" % s
                q = "sp" if (i % 2 == 0 or NOPOOL) else "pool"
                self.dma(q, stg[s][:, 0:n], src_rows[c * 128:(c + 1) * 128, col0 + j0:col0 + j0 + n], [], [sk], "w%d" % s)
                eng = "dve" if (i % 2 == 0 or NOPOOL) else "pool"
                S_.add(eng, lambda h, c=c, j0=j0, n=n, s=s: h.tensor_copy(out=dst[:, c, j0:j0 + n], in_=stg[s][:, 0:n]),
                       [sk], [keyname])
                i += 1

    def stats_rstd(self, sq, sqkey):
        S_ = self.sched
        ps = self.psO[:, 512:1024]
        for c in range(NCH):
            S_.add("pe", lambda h, c=c: h.matmul(ps, lhsT=self.onesD, rhs=sq[:, c, :], start=(c == 0), stop=(c == NCH - 1)),
                   [sqkey, "onesD"], ["O1"])
        S_.add("dve", lambda h: h.tensor_scalar(out=self.rstd, in0=ps, scalar1=EPS, scalar2=None, op0=ALU.add),
               ["O1"], ["rstd"])
        S_.add("act", lambda h: h.activation(out=self.rstd, in_=self.rstd, func=AF.Ln), ["rstd"], ["rstd"])
        S_.add("act", lambda h: h.activation(out=self.rstd, in_=self.rstd, func=AF.Exp, scale=-0.5), ["rstd"], ["rstd"])

    def prenorm(self, xg, xkey, gcol, hT, sq, sqkey="sq"):
        S_ = self.sched
        S_.add("act", lambda h: h.activation(out=sq, in_=xg, func=AF.Square), [xkey], [sqkey])
        self.stats_rstd(sq, sqkey)
        for c in range(NCH):
            S_.add("dve", lambda h, c=c: h.scalar_tensor_tensor(out=hT[:, c, :], in0=xg[:, c, :], scalar=gcol[:, c:c + 1],
                                                                 in1=self.rstd, op0=ALU.mult, op1=ALU.mult),
                   [xkey, "rstd", "gv"], ["hT"])

    def postnorm_chunk_in(self, c, ps, pskey, fst, sq, sqkey="sq"):
        S_ = self.sched
        S_.add("act", lambda h: h.activation(out=sq[:, c, :], in_=ps, func=AF.Square), [pskey], [sqkey])
        S_.add("dve", lambda h: h.tensor_copy(out=fst[:, c, :], in_=ps), [pskey], ["fst"])

    def postnorm_finish(self, xg, xkey, gcol, fst, sq, sqkey="sq"):
        S_ = self.sched
        self.stats_rstd(sq, sqkey)
        for c in range(NCH):
            e1 = "dve" if (c % 2 == 0 or NOPOOL) else "pool"
            S_.add(e1, lambda h, c=c: h.tensor_tensor(out=fst[:, c, :], in0=fst[:, c, :], in1=self.rstd, op=ALU.mult),
                   ["fst", "rstd"], ["fst"])
            S_.add("dve", lambda h, c=c: h.scalar_tensor_tensor(out=xg[:, c, :], in0=fst[:, c, :], scalar=gcol[:, c:c + 1],
                                                                 in1=xg[:, c, :], op0=ALU.mult, op1=ALU.add),
                   ["fst", "gv", xkey], [xkey])

    def load_x(self, g, xg, xkey, q="sp"):
        dr = self.dr
        if self.xsrc == "x_in":
            raise RuntimeError
        src = dr[self.xsrc]
        self.dma(q, xg, src[:, :, g * T:(g + 1) * T].rearrange("c p t -> p c t"), ["xt_dram%d" % g], [xkey], xkey)

    def store_x(self, g, xg, xkey, q="pool"):
        self.dma(q, self.dr["xt"][:, :, g * T:(g + 1) * T].rearrange("c p t -> p c t"), xg, [xkey], ["xt_dram%d" % g],
                 "st" + xkey)

    def step_init(self):
        S_ = self.sched
        dr = self.dr
        xt = [self.alloc([D], F32) for _ in range(2)]
        xg = [self.alloc([NCH, T], F32) for _ in range(2)]
        for g in range(self.NG):
            gk = "ixg%d" % (g % 2)
            for t in range(4):
                tt = g * 4 + t
                tk = "ixt%d" % (tt % 2)
                xtt = xt[tt % 2]
                self.dma("sp", xtt, dr["x_in"][tt * 128:(tt + 1) * 128, :], [], [tk], tk)
                for hf in range(2):
                    ps = self.psA[:, hf * 512:(hf + 1) * 512]
                    pk = "A%d" % hf
                    for c4 in range(4):
                        c = hf * 4 + c4
                        S_.add("pe", lambda h, ps=ps, c4=c4, c=c, xtt=xtt: h.transpose(
                            out=ps[:, c4 * 128:(c4 + 1) * 128], in_=xtt[:, c * 128:(c + 1) * 128], identity=self.identF),
                            [tk, "identF"], [pk])
                    eng = "dve" if hf == 0 else "act"
                    dst = xg[g % 2][:, hf * 4:(hf + 1) * 4, t * 128:(t + 1) * 128]
                    srcp = ps.rearrange("p (c q) -> p c q", c=4)
                    if eng == "dve":
                        S_.add("dve", lambda h, dst=dst, srcp=srcp: h.tensor_copy(out=dst, in_=srcp), [pk], [gk])
                    else:
                        S_.add("act", lambda h, dst=dst, srcp=srcp: h.activation(out=dst, in_=srcp, func=AF.Copy), [pk], [gk])
            self.store_x(g, xg[g % 2], gk)

    def step_fin(self):
        S_ = self.sched
        dr = self.dr
        xg = [self.alloc([NCH, T], F32) for _ in range(2)]
        yt = [self.alloc([D], F32) for _ in range(2)]
        for g in range(self.NG):
            gk = "fxg%d" % (g % 2)
            self.load_x(g, xg[g % 2], gk)
            for t in range(4):
                tt = g * 4 + t
                yk = "fyt%d" % (tt % 2)
                ytt = yt[tt % 2]
                for hf in range(2):
                    ps = self.psA[:, hf * 512:(hf + 1) * 512]
                    pk = "A%d" % hf
                    for c4 in range(4):
                        c = hf * 4 + c4
                        S_.add("pe", lambda h, ps=ps, c4=c4, c=c, t=t, g=g: h.transpose(
                            out=ps[:, c4 * 128:(c4 + 1) * 128], in_=xg[g % 2][:, c, t * 128:(t + 1) * 128],
                            identity=self.identF), [gk, "identF"], [pk])
                    dst = ytt[:, hf * 512:(hf + 1) * 512]
                    if hf == 0:
                        S_.add("dve", lambda h, dst=dst, ps=ps: h.tensor_copy(out=dst, in_=ps), [pk], [yk])
                    else:
                        S_.add("act", lambda h, dst=dst, ps=ps: h.activation(out=dst, in_=ps, func=AF.Copy), [pk], [yk])
                self.dma("pool", dr["y_out"][tt * 128:(tt + 1) * 128, :], ytt, [yk], ["ydram"], "o" + yk)

    def rope_partial(self, eng, src, dst, nh, tt, skey, dkey, tmp):
        S_ = self.sched
        cs = self.ropep[:, tt, :]
        cb = cs[:, 0:8].unsqueeze(1).to_broadcast([128, nh, 8])
        sb = cs[:, 8:16].unsqueeze(1).to_broadcast([128, nh, 8])
        x1 = src[:, :, 0:8]
        x2 = src[:, :, 8:16]
        t1, t2, t3, t4 = (tmp[:, i, 0:nh, 0:8] for i in range(4))
        tk = "rtmp"
        skey = list(skey) if isinstance(skey, (list, tuple)) else [skey]
        S_.add(eng, lambda h: h.tensor_tensor(out=t1, in0=x1, in1=cb, op=ALU.mult), skey + ["ropep"], [tk + "1"])
        S_.add(eng, lambda h: h.tensor_tensor(out=t2, in0=x2, in1=sb, op=ALU.mult), skey + ["ropep"], [tk + "2"])
        S_.add(eng, lambda h: h.tensor_tensor(out=t3, in0=x2, in1=cb, op=ALU.mult), skey + ["ropep"], [tk + "3"])
        S_.add(eng, lambda h: h.tensor_tensor(out=t4, in0=x1, in1=sb, op=ALU.mult), skey + ["ropep"], [tk + "4"])
        S_.add(eng, lambda h: h.tensor_tensor(out=dst[:, :, 0:8], in0=t1, in1=t2, op=ALU.subtract), [tk + "1", tk + "2"], [dkey])
        S_.add(eng, lambda h: h.tensor_tensor(out=dst[:, :, 8:16], in0=t3, in1=t4, op=ALU.add), [tk + "3", tk + "4"], [dkey])
        S_.add(eng, lambda h: h.tensor_copy(out=dst[:, :, 16:64], in_=src[:, :, 16:64]), skey, [dkey])

    def norm_rope_axial(self, eng, src, dst, nh, tt, skey, dkey, gidx, tmpn, tmp, st, tmpx):
        S_ = self.sched
        sq = tmpn[:, 0:nh, :]
        ss = st[:, 0:nh]
        nk = "qn"
        skey = list(skey) if isinstance(skey, (list, tuple)) else [skey]
        xs = tmpx[:, 0:nh, :]
        S_.add("act", lambda h, s0=src: h.activation(out=xs, in_=s0, func=AF.Copy), skey, [nk + "xs"])
        src = xs
        skey = [nk + "xs"]
        S_.add(eng, lambda h: h.tensor_tensor(out=sq, in0=src, in1=src, op=ALU.mult), skey, [nk + "sq"])
        S_.add(eng, lambda h: h.reduce_sum(out=ss, in_=sq, axis=AX.X), [nk + "sq"], [nk + "ss"])
        S_.add(eng, lambda h: h.tensor_scalar(out=ss, in0=ss, scalar1=1.0 / HD, scalar2=EPS, op0=ALU.mult, op1=ALU.add),
               [nk + "ss"], [nk + "ss"])
        S_.add("act", lambda h: h.activation(out=ss, in_=ss, func=AF.Ln), [nk + "ss"], [nk + "ss"])
        S_.add("act", lambda h: h.activation(out=ss, in_=ss, func=AF.Exp, scale=-0.5), [nk + "ss"], [nk + "ss"])
        S_.add(eng, lambda h: h.tensor_tensor(out=sq, in0=src, in1=ss.unsqueeze(2).to_broadcast([128, nh, HD]), op=ALU.mult),
               skey + [nk + "ss"], [nk + "sq"])
        gb = self.qkg[:, gidx * HD:(gidx + 1) * HD].unsqueeze(1).to_broadcast([128, nh, HD])
        S_.add(eng, lambda h: h.tensor_tensor(out=sq, in0=sq, in1=gb, op=ALU.mult), [nk + "sq", "qkg"], [nk + "sq"])
        ca = self.ropea[:, tt, :]
        for half in range(2):
            o = half * 32
            cb = ca[:, half * 16:(half + 1) * 16].unsqueeze(1).to_broadcast([128, nh, 16])
            sb = ca[:, 32 + half * 16:32 + (half + 1) * 16].unsqueeze(1).to_broadcast([128, nh, 16])
            x1 = sq[:, :, o:o + 16]
            x2 = sq[:, :, o + 16:o + 32]
            t1, t2, t3, t4 = (tmp[:, i, 0:nh, :] for i in range(4))
            tk = "rtmp"
            S_.add(eng, lambda h, x1=x1, cb=cb, t1=t1: h.tensor_tensor(out=t1, in0=x1, in1=cb, op=ALU.mult), [nk + "sq", "ropea"], [tk + "1"])
            S_.add(eng, lambda h, x2=x2, sb=sb, t2=t2: h.tensor_tensor(out=t2, in0=x2, in1=sb, op=ALU.mult), [nk + "sq", "ropea"], [tk + "2"])
            S_.add(eng, lambda h, x2=x2, cb=cb, t3=t3: h.tensor_tensor(out=t3, in0=x2, in1=cb, op=ALU.mult), [nk + "sq", "ropea"], [tk + "3"])
            S_.add(eng, lambda h, x1=x1, sb=sb, t4=t4: h.tensor_tensor(out=t4, in0=x1, in1=sb, op=ALU.mult), [nk + "sq", "ropea"], [tk + "4"])
            S_.add(eng, lambda h, o=o, t1=t1, t2=t2: h.tensor_tensor(out=dst[:, :, o:o + 16], in0=t1, in1=t2, op=ALU.subtract),
                   [tk + "1", tk + "2"], [dkey])
            S_.add(eng, lambda h, o=o, t3=t3, t4=t4: h.tensor_tensor(out=dst[:, :, o + 16:o + 32], in0=t3, in1=t4, op=ALU.add),
                   [tk + "3", tk + "4"], [dkey])

    def load_layer_small(self, l, wi):
        S_ = self.sched
        dr = self.dr
        NT = self.TPC // 128
        kind = l % 3
        if kind == 1:
            self.ropea = self.alloc([NT, 64], F32)
            self.dma("sp", self.ropea, dr["ropea"].rearrange("(t p) c -> p t c", p=128), [], ["ropea"], self.key("rp"))
            self.qkg = self.alloc([2 * HD], F32)
            self.dma("sp", self.qkg, dr["qkg"][wi].partition_broadcast(128), [], ["qkg"], self.key("rp"))
        else:
            self.ropep = self.alloc([NT, 16], F32)
            self.dma("sp", self.ropep, dr["ropep"].rearrange("(t p) c -> p t c", p=128), [], ["ropep"], self.key("rp"))
        if kind == 0:
            self.sinke = self.alloc([16], F32)
            self.dma("sp", self.sinke, dr["sink"][wi].partition_broadcast(128), [], ["sinke"], self.key("rp"))
            S_.add("act", lambda h: h.activation(out=self.sinke, in_=self.sinke, func=AF.Exp), ["sinke"], ["sinke"])

    def step_p1(self, l, wi):
        S_ = self.sched
        dr = self.dr
        kind = l % 3
        self.load_layer_small(l, wi)
        wkv = self.alloc([NCH, 2 * KVW], BF16)
        m0 = self.aoff
        self.load_weight(wkv, dr["w_in"][wi], 2 * KVW, NCH, "wkv", col0=QW)
        S_.barrier()
        self.aoff = m0
        CUT = int(os.environ.get("P1CUT", "99"))
        if CUT <= 1:
            return
        gcol = self.gv[:, wi, 0:NCH]
        xg = [self.alloc([NCH, T], F32) for _ in range(2)]
        hT = self.alloc([NCH, T], BF16)
        sq = self.alloc([NCH, T], BF16)
        kb = self.alloc([KVH, HD], BF16)
        ktst = [self.alloc([KVH, T], BF16) for _ in range(2)]
        vst = [self.alloc([4, KVH, 128], BF16) for _ in range(2)]
        tmp = self.alloc([4, KVH, 16], F32)
        tmpn = self.alloc([KVH, HD], F32)
        tmpx = self.alloc([KVH, HD], F32)
        stt = self.alloc([16], F32)
        for i in range(2):
            S_.add("dve", lambda h, i=i: h.memset(vst[i][:, :, :, 64:128], 1.0), [], ["vst%d" % i])
        psT = self.psA[:, 512:768].bitcast(BF16)
        self.load_x(0, xg[0], "p1x0")
        for g in range(self.NG):
            xk = "p1x%d" % (g % 2)
            if g + 1 < self.NG:
                self.load_x(g + 1, xg[(g + 1) % 2], "p1x%d" % ((g + 1) % 2))
            self.prenorm(xg[g % 2], xk, gcol, hT, sq)
            if CUT <= 2:
                return
            kts = ktst[g % 2]
            vs = vst[g % 2]
            ktk = "ktst%d" % (g % 2)
            vk = "vst%d" % (g % 2)
            for t in range(4):
                tt = g * 4 + t
                ps = self.psA[:, 0:2 * KVW]
                for c in range(NCH):
                    S_.add("pe", lambda h, c=c, t=t: h.matmul(ps, lhsT=hT[:, c, t * 128:(t + 1) * 128], rhs=wkv[:, c, :],
                                                               start=(c == 0), stop=(c == NCH - 1)), ["hT", "wkv"], ["A0"])
                ksrc = ps[:, 0:KVW].rearrange("p (h d) -> p h d", h=KVH)
                if kind == 1:
                    self.norm_rope_axial("dve", ksrc, kb, KVH, tt, "A0", "kb", 1, tmpn, tmp, stt, tmpx)
                else:
                    self.rope_partial("dve", ksrc, kb, KVH, tt, "A0", "kb", tmp)
                if CUT <= 3:
                    return
                S_.add("act", lambda h, t=t, vs=vs: h.activation(out=vs[:, t, :, 0:64],
                                                                  in_=ps[:, KVW:2 * KVW].rearrange("p (h d) -> p h d", h=KVH),
                                                                  func=AF.Copy), ["A0"], [vk])
                if CUT == 5:
                    return
                for kh in range(KVH):
                    S_.add("pe", lambda h, kh=kh: h.transpose(out=psT[0:64, kh * 128:(kh + 1) * 128], in_=kb[:, kh, :],
                                                               identity=self.identB), ["kb", "identB"], ["A1"])
                if CUT == 6:
                    return
                S_.add("act", lambda h, t=t, kts=kts: h.activation(out=kts[0:64, :, t * 128:(t + 1) * 128],
                                                                    in_=psT[0:64, 0:384].rearrange("p (h q) -> p h q", h=KVH),
                                                                    func=AF.Copy), ["A1"], [ktk])
            if CUT <= 7:
                return
            if self.fused:
                kdst = dr["kt_loc%d" % (l % 2)].rearrange("(h d) t -> h d t", h=KVH)
                vdst = dr["v_loc%d" % (l % 2)].rearrange("t (h c) -> t h c", h=KVH)
            else:
                kdst, vdst = dr["kt_out"], dr["v_out"]
            self.dma("pool", kdst[:, :, g * T:(g + 1) * T].rearrange("h d t -> d h t"), kts[0:64], [ktk], ["ktdram"],
                     "o" + ktk)
            self.dma("pool", vdst[g * T:(g + 1) * T].rearrange("(t p) h c -> p t h c", p=128), vs, [vk], ["vdram"],
                     "o" + vk)

    def step_p2(self, l, wi):
        S_ = self.sched
        dr = self.dr
        kind = l % 3
        TPC = self.TPC
        self.load_layer_small(l, wi)
        if self.fused and kind == 1:
            ktb = [dr["ktpad%d_%d" % (l % 2, kh)] for kh in range(KVH)]
            vb = [dr["vpad%d_%d" % (l % 2, c)] for c in range(self.NVC)]
            self.kvkeys = ["ktpad%d" % (l % 2), "vpad%d" % (l % 2)]
        elif self.fused:
            ktb = dr["kt_rel"]
            vb = dr["v_rel"]
            self.kvkeys = ["ktrel", "vrel"]
        else:
            ktb = dr["kt_buf%d" % l]
            vb = dr["v_buf%d" % l]
            self.kvkeys = [None, None]
        self.cur_kind = kind
        wq = self.alloc([NCH, 1024], BF16)
        wo = self.alloc([NCH, D], BF16)
        kmT = self.alloc([HM, NMEM], BF16)
        vm = self.alloc([2, HM, 128], BF16)
        self.kmT, self.vm = kmT, vm
        m0 = self.aoff
        wm = self.alloc([NCH, 512], BF16)
        memT = self.alloc([NCH, NMEM], BF16)
        kmb = self.alloc([256], BF16)
        gb = self.alloc([D], F32)
        mt = self.alloc([2, D], F32)
        sqj = self.alloc([D], F32)
        ssq = self.alloc([2], F32)
        mb = self.alloc([2, D], BF16)
        self.load_weight(wq[:, :, 0:QW], dr["w_in"][wi], QW, NCH, "wq", col0=0, colchunk=QW)
        self.load_weight(wq[:, :, QW:1024], dr["w_in"][wi], 256, NCH, "wq", col0=QW + 2 * KVW, colchunk=256)
        self.load_weight(wo, dr["w_o"][wi], D, NCH, "wo", colchunk=1024)
        self.load_weight(wm, dr["w_mem_kv"][wi], 512, NCH, "wm", colchunk=512)
        self.dma("sp", gb, dr["mem_g"].partition_broadcast(128), [], ["memg"], self.key("mm"))
        self.dma("sp", mt, dr["mem"].rearrange("(b p) d -> p b d", p=128), [], ["memt"], self.key("mm"))
        for b in range(2):
            S_.add("dve", lambda h, b=b: h.tensor_tensor(out=sqj, in0=mt[:, b, :], in1=mt[:, b, :], op=ALU.mult),
                   ["memt"], ["sqj"])
            S_.add("dve", lambda h, b=b: h.reduce_sum(out=ssq[:, b:b + 1], in_=sqj, axis=AX.X), ["sqj"], ["mssq"])
        S_.add("dve", lambda h: h.tensor_scalar(out=ssq, in0=ssq, scalar1=1.0 / D, scalar2=EPS, op0=ALU.mult,
                                                 op1=ALU.add), ["mssq"], ["mssq"])
        S_.add("act", lambda h: h.activation(out=ssq, in_=ssq, func=AF.Ln), ["mssq"], ["mssq"])
        S_.add("act", lambda h: h.activation(out=ssq, in_=ssq, func=AF.Exp, scale=-0.5), ["mssq"], ["mssq"])
        for b in range(2):
            S_.add("dve", lambda h, b=b: h.scalar_tensor_tensor(out=mb[:, b, :], in0=mt[:, b, :], scalar=ssq[:, b:b + 1],
                                                                 in1=gb, op0=ALU.mult, op1=ALU.mult),
                   ["memt", "mssq", "memg"], ["mb"])
        psT8 = self.psA[:, 0:512].bitcast(BF16)
        for b in range(2):
            for c in range(NCH):
                S_.add("pe", lambda h, b=b, c=c: h.transpose(out=psT8[:, c * 128:(c + 1) * 128],
                                                               in_=mb[:, b, c * 128:(c + 1) * 128], identity=self.identB),
                       ["mb", "identB"], ["A0"])
            S_.add("dve", lambda h, b=b: h.tensor_copy(out=memT[:, :, b * 128:(b + 1) * 128],
                                                        in_=psT8.rearrange("p (c q) -> p c q", c=NCH)), ["A0"], ["memT"])
        S_.add("dve", lambda h: h.memset(vm[:, :, :, 64:128], 1.0), [], ["vm"])
        psTm = self.psA[:, 512:768].bitcast(BF16)
        for b in range(2):
            ps = self.psA[:, 0:512]
            for c in range(NCH):
                S_.add("pe", lambda h, b=b, c=c, ps=ps: h.matmul(ps, lhsT=memT[:, c, b * 128:(b + 1) * 128], rhs=wm[:, c, :],
                                                                  start=(c == 0), stop=(c == NCH - 1)), ["memT", "wm"], ["A0"])
            S_.add("dve", lambda h, ps=ps: h.tensor_copy(out=kmb, in_=ps[:, 0:256]), ["A0"], ["kmb"])
            S_.add("act", lambda h, b=b, ps=ps: h.activation(out=vm[:, b, :, 0:64],
                                                              in_=ps[:, 256:512].rearrange("p (h d) -> p h d", h=HM),
                                                              func=AF.Copy), ["A0"], ["vm"])
            for hh in range(HM):
                S_.add("pe", lambda h, hh=hh: h.transpose(out=psTm[0:64, hh * 128:(hh + 1) * 128], in_=kmb[:, hh * 64:(hh + 1) * 64],
                                                           identity=self.identB), ["kmb", "identB"], ["A1"])
            S_.add("dve", lambda h, b=b: h.tensor_copy(out=kmT[0:64, :, b * 128:(b + 1) * 128],
                                                        in_=psTm[0:64, 0:512].rearrange("p (h q) -> p h q", h=HM)), ["A1"], ["kmT"])
        if self.fused:
            self.step_exch(l)
        S_.barrier()
        self.aoff = m0
        gcol = self.gv[:, wi, 0:NCH]
        gpost = self.gv[:, wi, NCH:2 * NCH]
        xg = [self.alloc([NCH, T], F32) for _ in range(2)]
        hT = self.alloc([NCH, T], BF16)
        sq = self.alloc([NCH, T], BF16)
        fst = self.alloc([NCH, T], F32)
        qb = [self.alloc([16, 2 * HD], BF16) for _ in range(2)]
        QT = self.alloc([16, T], BF16)
        OT = self.alloc([NCH, T], BF16)
        self.OT = OT
        ost = [self.alloc([4, 512], F32) for _ in range(2)]
        self.rden = self.alloc([4, 512], F32)
        PT = [self.alloc([1024], BF16) for _ in range(3)]
        NKC = 2560 if kind == 2 else (min(1024, self.TPC) if kind == 1 else 1024)
        ktc = [self.alloc([NKC], BF16) for _ in range(2)]
        vc = [self.alloc([NKC // 128, 128], BF16) for _ in range(2)]
        tmp = self.alloc([4, H, 16], F32)
        tmpn = self.alloc([H, HD], F32)
        tmpx = self.alloc([H, HD], F32)
        stt = self.alloc([16], F32)
        psQT = self.psA[:, 0:512].bitcast(BF16)
        self.ptc = 0
        self.sc = 0
        self.kvc = 0
        self.oc = 0

        self.load_x(0, xg[0], "p2x0")
        for g in range(self.NG):
            xk = "p2x%d" % (g % 2)
            xgg = xg[g % 2]
            if g + 1 < self.NG:
                self.load_x(g + 1, xg[(g + 1) % 2], "p2x%d" % ((g + 1) % 2))
            self.prenorm(xgg, xk, gcol, hT, sq)
            for t in range(4):
                tt = g * 4 + t
                ps = self.psA
                for hf in range(2):
                    for c in range(NCH):
                        S_.add("pe", lambda h, c=c, t=t, hf=hf, ps=ps: h.matmul(ps[:, hf * 512:(hf + 1) * 512],
                                                                                 lhsT=hT[:, c, t * 128:(t + 1) * 128],
                                                                                 rhs=wq[:, c, hf * 512:(hf + 1) * 512],
                                                                                 start=(c == 0), stop=(c == NCH - 1)),
                               ["hT", "wq"], ["A%d" % hf])
                qbt = qb[t % 2]
                qk_ = "qb%d" % (t % 2)
                qsrc = ps[:, 0:QW].rearrange("p (h d) -> p h d", h=H)
                if kind == 1:
                    self.norm_rope_axial("dve", qsrc, qbt[:, 0:H, 0:HD], H, tt, ["A0", "A1"], qk_, 0, tmpn, tmp, stt, tmpx)
                else:
                    self.rope_partial("dve", qsrc, qbt[:, 0:H, 0:HD], H, tt, ["A0", "A1"], qk_, tmp)
                S_.add("act", lambda h, qbt=qbt, ps=ps: h.activation(out=qbt[:, H:16, 0:HD],
                                                                      in_=ps[:, QW:1024].rearrange("p (h d) -> p h d", h=HM),
                                                                      func=AF.Copy), ["A1"], [qk_])
                S_.add("dve", lambda h, qbt=qbt: h.tensor_copy(out=qbt[:, :, HD:2 * HD], in_=qbt[:, :, 0:HD]), [qk_], [qk_])
                for h0 in (0, 8):
                    for hh in range(h0, h0 + 8):
                        S_.add("pe", lambda h, hh=hh, h0=h0, qbt=qbt: h.transpose(
                            out=psQT[:, (hh - h0) * 128:(hh - h0 + 1) * 128], in_=qbt[:, hh, :], identity=self.identB),
                            [qk_, "identB"], ["A0"])
                    S_.add("act", lambda h, h0=h0, t=t: h.activation(
                        out=QT[:, h0:h0 + 8, t * 128:(t + 1) * 128],
                        in_=psQT.rearrange("p (h q) -> p h q", h=8), func=AF.Copy), ["A0"], ["QT"])
            if kind == 1:
                self.attn_global(g, ktb, vb, QT, PT, ktc, vc, ost, NKC)
            else:
                self.attn_window(g, kind, ktb, vb, QT, PT, ktc, vc, ost, NKC)
            for c in range(NCH):
                ps = self.psA[:, (c % 2) * 512:(c % 2 + 1) * 512]
                pk = "A%d" % (c % 2)
                for pr in range(NCH):
                    S_.add("pe", lambda h, c=c, pr=pr, ps=ps: h.matmul(ps, lhsT=wo[:, pr, c * 128:(c + 1) * 128], rhs=OT[:, pr, :],
                                                                        start=(pr == 0), stop=(pr == NCH - 1)), ["wo", "OT"], [pk])
                self.postnorm_chunk_in(c, ps, pk, fst, sq)
            self.postnorm_finish(xgg, xk, gpost, fst, sq)
            self.store_x(g, xgg, xk)

    def attn_pairs(self, qrhs, blocks, kt, ktkey, v, vkey, obank, okey, first, last):
        S_ = self.sched
        nb = len(blocks)
        i = 0
        while i < nb:
            pair = blocks[i:i + 2]
            sidx = self.sc % 2
            self.sc += 1
            sps = self.psS[:, sidx * 1024:(sidx + 1) * 1024]
            sk = "S%d" % sidx
            for j, (kcol, vblk, m) in enumerate(pair):
                pl = j * 64
                S_.add("pe", lambda h, j=j, kcol=kcol, sps=sps, pl=pl: h.matmul(
                    sps[:, j * 512:(j + 1) * 512].rearrange("p (h q) -> p h q", h=4),
                    lhsT=kt[pl:pl + 64, kcol:kcol + 128], rhs=qrhs[pl:pl + 64], start=True, stop=True),
                    [ktkey, "QT"], [sk])
            pidx = self.ptc % 3
            self.ptc += 1
            pt = self.PTl[pidx]
            pk = "PT%d" % pidx
            w = 512 * len(pair)
            S_.add("act", lambda h, pt=pt, sps=sps, w=w: h.activation(out=pt[:, 0:w], in_=sps[:, 0:w], func=AF.Exp, scale=0.125),
                   [sk], [pk])
            for j, (kcol, vblk, m) in enumerate(pair):
                if m is not None:
                    mb = self.masks[:, m, :].unsqueeze(1).to_broadcast([128, 4, 128])
                    S_.add("dve", lambda h, j=j, pt=pt, mb=mb: h.tensor_tensor(
                        out=pt[:, j * 512:(j + 1) * 512].rearrange("p (h q) -> p h q", h=4),
                        in0=pt[:, j * 512:(j + 1) * 512].rearrange("p (h q) -> p h q", h=4), in1=mb, op=ALU.mult),
                        [pk, "masks"], [pk])
            def pv(pair=pair, i=i, pt=pt, pk=pk):
                for j, (kcol, vblk, m) in enumerate(pair):
                    st = first and (i + j == 0)
                    sp = last and (i + j == nb - 1)
                    S_.add("pe", lambda h, j=j, vblk=vblk, pt=pt, st=st, sp=sp: h.matmul(obank, lhsT=v[:, vblk, :],
                                                                                          rhs=pt[:, j * 512:(j + 1) * 512],
                                                                                          start=st, stop=sp),
                           [vkey, pk], [okey])
            self.flush_pv()
            self.pending_pv = pv
            i += 2

    def load_kv(self, ktb, vb, kh, k0, n, ktc, vc):
        s = self.kvc % 2
        self.kvc += 1
        kk = "ktc%d" % s
        vk = "vc%d" % s
        TPC = self.TPC
        R = KVH * HD
        rk_ = [k for k in self.kvkeys[0:1] if k]
        rv_ = [k for k in self.kvkeys[1:2] if k]
        if self.fused and self.cur_kind == 1:
            NPR = self.NPR
            VCH = self.VCH
            rk, off = k0 // TPC, k0 % TPC
            for hf in range(2):
                self.dma("sp", ktc[s][hf * 64:(hf + 1) * 64, 0:n], ktb[kh][(rk + NPR) * HD:(rk + NPR + 1) * HD, off:off + n],
                         rk_, [kk], kk)
            for o2 in range(0, n, VCH):
                c = (off + o2) // VCH
                self.dma("sp", vc[s][:, o2 // 128:(o2 + VCH) // 128, :],
                         vb[c][(rk + NPR) * VCH:(rk + NPR + 1) * VCH, kh * 128:(kh + 1) * 128].rearrange(
                             "(b p) c -> p b c", p=128), rv_, [vk], vk)
        else:
            for hf in range(2):
                self.dma("sp", ktc[s][hf * 64:(hf + 1) * 64, 0:n], ktb[kh, :, k0:k0 + n], rk_, [kk], kk)
            self.dma("sp", vc[s][:, 0:n // 128, :], vb[k0:k0 + n, kh, :].rearrange("(b p) c -> p b c", p=128), rv_, [vk], vk)
        return ktc[s], kk, vc[s], vk

    def flush_pv(self):
        p = getattr(self, "pending_pv", None)
        self.pending_pv = None
        if p is not None:
            p()

    def evac_o(self, obank, okey, ostt, ostk, gi):
        S_ = self.sched
        self.flush_pv()
        if getattr(self, "cur_kind", 0) == 1:
            S_.add("dve", lambda h: h.tensor_copy(out=ostt[:, gi, :], in_=obank), [okey], [ostk])
        else:
            S_.add("act", lambda h: h.activation(out=ostt[:, gi, :], in_=obank, func=AF.Copy), [okey], [ostk])

    def mem_and_finish(self, kind, t, QT, ostt, ostk):
        S_ = self.sched
        kmT, vm = self.kmT, self.vm
        sidx = self.sc % 2
        self.sc += 1
        sps = self.psS[:, sidx * 1024:(sidx + 1) * 1024]
        sk = "S%d" % sidx
        for b in range(2):
            for hh in range(HM):
                S_.add("pe", lambda h, b=b, hh=hh: h.matmul(sps[:, b * 512 + hh * 128: b * 512 + (hh + 1) * 128],
                                                             lhsT=kmT[0:64, hh, b * 128:(b + 1) * 128],
                                                             rhs=QT[0:64, H + hh, t * 128:(t + 1) * 128], start=True, stop=True),
                       ["kmT", "QT"], [sk])
        pidx = self.ptc % 3
        self.ptc += 1
        pt = self.PTl[pidx]
        pk = "PT%d" % pidx
        S_.add("act", lambda h: h.activation(out=pt, in_=sps, func=AF.Exp, scale=0.125), [sk], [pk])
        self.flush_pv()
        oi = self.oc % 2
        self.oc += 1
        obank = self.psO[:, oi * 512:(oi + 1) * 512]
        okey = "O%d" % oi
        for hh in range(HM):
            for b in range(2):
                S_.add("pe", lambda h, b=b, hh=hh: h.matmul(obank[:, hh * 128:(hh + 1) * 128], lhsT=vm[:, b, hh, :],
                                                             rhs=pt[:, b * 512 + hh * 128: b * 512 + (hh + 1) * 128],
                                                             start=(b == 0), stop=(b == 1)), ["vm", pk], [okey])
        self.evac_o(obank, okey, ostt, ostk, 3)
        den = ostt[64:128, :, :]
        rden = self.rden[0:64, :, :]
        if kind == 0:
            sb = self.sinke[64:128, 0:H].rearrange("p (g h) -> p g h", g=3).unsqueeze(3).to_broadcast([64, 3, 4, 128])
            S_.add("dve", lambda h: h.tensor_tensor(out=den[:, 0:3, :].rearrange("p g (h q) -> p g h q", h=4),
                                                     in0=den[:, 0:3, :].rearrange("p g (h q) -> p g h q", h=4), in1=sb, op=ALU.add),
                   [ostk, "sinke"], [ostk])
        if kind == 2:
            S_.add("dve", lambda h: h.tensor_tensor(out=den[:, 0, :], in0=den[:, 0, :], in1=den[:, 1, :], op=ALU.add), [ostk], [ostk])
            S_.add("dve", lambda h: h.tensor_tensor(out=den[:, 0, :], in0=den[:, 0, :], in1=den[:, 2, :], op=ALU.add), [ostk], [ostk])
            S_.add("dve", lambda h: h.tensor_copy(out=den[:, 1, :], in_=den[:, 0, :]), [ostk], [ostk])
            S_.add("dve", lambda h: h.tensor_copy(out=den[:, 2, :], in_=den[:, 0, :]), [ostk], [ostk])
        S_.add("act", lambda h: h.activation(out=rden, in_=den, func=AF.Ln), [ostk], ["rden"])
        S_.add("act", lambda h: h.activation(out=rden, in_=rden, func=AF.Exp, scale=-1.0), ["rden"], ["rden"])
        OT = self.OT
        for par in range(2):
            num = ostt[0:64, :, :].rearrange("p g (a b q) -> p g a b q", a=2, b=2)[:, :, :, par, :]
            rd = self.rden[0:64, :, :].rearrange("p g (a b q) -> p g a b q", a=2, b=2)[:, :, :, par, :]
            dst = OT[par * 64:(par + 1) * 64, :, t * 128:(t + 1) * 128].rearrange("p (g a) q -> p g a q", g=4)
            S_.add("dve", lambda h, num=num, rd=rd, dst=dst: h.tensor_tensor(out=dst, in0=num, in1=rd, op=ALU.mult),
                   [ostk, "rden"], ["OT"])

    def attn_window(self, g, kind, ktb, vb, QT, PT, ktc, vc, ost, NKC):
        self.PTl = PT
        base = PAD + g * T
        if kind == 0:
            specs = [(kh, 128, 1) for kh in range(KVH)]
        else:
            specs = [(0, 64, 1), (1, 256, 4), (2, 1024, 16)]
        for tp in range(2):
            tiles = [2 * tp, 2 * tp + 1]
            for (kh, halo, dil) in specs:
                k0 = base + tiles[0] * 128 - halo
                n = 256 + 2 * halo
                kt, kk, v, vk = self.load_kv(ktb, vb, kh, k0, n, ktc, vc)
                for ti, t in enumerate(tiles):
                    off = ti * 128
                    if kind == 0:
                        bl = [(off, 0), (off + 128, None), (off + 256, 1)]
                    elif dil == 1:
                        bl = [(off, 0), (off + 128, 1)]
                    elif dil == 4:
                        ms = [3, 2, 2, 2, 4]
                        bl = [(off + d * 128, ms[d]) for d in range(5)]
                    else:
                        ms = [6] + [5] * 15 + [7]
                        bl = [(off + d * 128, ms[d]) for d in range(17)]
                    blocks = [(kc, kc // 128, m) for (kc, m) in bl]
                    oi = self.oc % 2
                    self.oc += 1
                    obank = self.psO[:, oi * 512:(oi + 1) * 512]
                    okey = "O%d" % oi
                    qrhs = QT[:, 4 * kh:4 * kh + 4, t * 128:(t + 1) * 128]
                    self.attn_pairs(qrhs, blocks, kt, kk, v, vk, obank, okey, True, True)
                    self.evac_o(obank, okey, ost[ti], "ost%d" % ti, kh)
            for ti, t in enumerate(tiles):
                self.mem_and_finish(kind, t, QT, ost[ti], "ost%d" % ti)

    def attn_global(self, g, ktb, vb, QT, PT, ktc, vc, ost, NKC):
        self.PTl = PT
        S = self.S
        nchunk = S // NKC
        for tp in range(2):
            tiles = [2 * tp, 2 * tp + 1]
            for kh in range(KVH):
                for ci in range(nchunk):
                    kt, kk, v, vk = self.load_kv(ktb, vb, kh, ci * NKC, NKC, ktc, vc)
                    blocks = [(b * 128, b, None) for b in range(NKC // 128)]
                    for ti, t in enumerate(tiles):
                        obank = self.psO[:, ti * 512:(ti + 1) * 512]
                        okey = "O%d" % ti
                        qrhs = QT[:, 4 * kh:4 * kh + 4, t * 128:(t + 1) * 128]
                        self.attn_pairs(qrhs, blocks, kt, kk, v, vk, obank, okey, ci == 0, ci == nchunk - 1)
                for ti, t in enumerate(tiles):
                    self.evac_o(self.psO[:, ti * 512:(ti + 1) * 512], "O%d" % ti, ost[ti], "ost%d" % ti, kh)
            self.oc = 0
            for ti, t in enumerate(tiles):
                self.mem_and_finish(1, t, QT, ost[ti], "ost%d" % ti)

    def step_ffn(self, l, wi):
        S_ = self.sched
        dr = self.dr
        wgu = self.alloc([NCH, 2 * DFF], BF16)
        wdn = self.alloc([NF, D], BF16)
        m0 = self.aoff
        self.load_weight(wgu, dr["w_gu"][wi], 2 * DFF, NCH, "wgu", colchunk=1408)
        self.load_weight(wdn, dr["w_dn"][wi], D, NF, "wdn", colchunk=1024)
        S_.barrier()
        self.aoff = m0
        gcol = self.gv[:, wi, 2 * NCH:3 * NCH]
        gpost = self.gv[:, wi, 3 * NCH:4 * NCH]
        xg = self.alloc([NCH, T], F32)
        hT = self.alloc([NCH, T], BF16)
        act = self.alloc([NF, T], BF16)
        fst = self.alloc([NCH, T], F32)
        sil = [self.alloc([T], BF16) for _ in range(2)]
        for g in range(self.NG):
            xk = "fx"
            self.load_x(g, xg, xk)
            self.prenorm(xg, xk, gcol, hT, act[:, 0:NCH, :], "act")
            for j in range(NF):
                gi = j % 2
                gps = self.psS[:, gi * 1024:gi * 1024 + 512]
                ups = self.psS[:, gi * 1024 + 512:(gi + 1) * 1024]
                gk = "S%d" % gi
                for c in range(NCH):
                    S_.add("pe", lambda h, c=c, j=j, gps=gps: h.matmul(gps, lhsT=wgu[:, c, j * 128:(j + 1) * 128], rhs=hT[:, c, :],
                                                                        start=(c == 0), stop=(c == NCH - 1)), ["wgu", "hT"], [gk + "g"])
                for c in range(NCH):
                    S_.add("pe", lambda h, c=c, j=j, ups=ups: h.matmul(ups, lhsT=wgu[:, c, DFF + j * 128:DFF + (j + 1) * 128],
                                                                        rhs=hT[:, c, :], start=(c == 0), stop=(c == NCH - 1)),
                           ["wgu", "hT"], [gk + "u"])
                sl = sil[gi]
                S_.add("act", lambda h, sl=sl, gps=gps: h.activation(out=sl, in_=gps, func=AF.Silu), [gk + "g"], ["sil%d" % gi])
                S_.add("dve", lambda h, sl=sl, ups=ups, j=j: h.tensor_tensor(out=act[:, j, :], in0=sl, in1=ups, op=ALU.mult),
                       ["sil%d" % gi, gk + "u"], ["act"])
            for c in range(NCH):
                ps = self.psA[:, (c % 2) * 512:(c % 2 + 1) * 512]
                pk = "A%d" % (c % 2)
                for j in range(NF):
                    S_.add("pe", lambda h, c=c, j=j, ps=ps: h.matmul(ps, lhsT=wdn[:, j, c * 128:(c + 1) * 128], rhs=act[:, j, :],
                                                                      start=(j == 0), stop=(j == NF - 1)), ["wdn", "act"], [pk])
                self.postnorm_chunk_in(c, ps, pk, fst, hT, "hT")
            self.postnorm_finish(xg, xk, gpost, fst, hT, "hT")
            self.store_x(g, xg, xk)


def _rope_tables(S):
    pos = np.arange(S, dtype=np.float32)
    inv = (500000.0 ** (-(np.arange(0, 16, 2, dtype=np.float32) / 16))).astype(np.float32)
    ang = pos[:, None] * inv[None, :]
    ropep = np.concatenate([np.cos(ang), np.sin(ang)], axis=1).astype(np.float32)
    rows = (np.arange(S) // GRID_W).astype(np.float32)
    cols = (np.arange(S) % GRID_W).astype(np.float32)
    inv2 = (10000.0 ** (-(np.arange(0, 32, 2, dtype=np.float32) / 32))).astype(np.float32)
    ar = rows[:, None] * inv2[None, :]
    ac = cols[:, None] * inv2[None, :]
    ropea = np.concatenate([np.cos(ar), np.cos(ac), np.sin(ar), np.sin(ac)], axis=1).astype(np.float32)
    return ropep, ropea


def _masks():
    j = np.arange(128)[:, None]
    i = np.arange(128)[None, :]
    ge = (j >= i)
    le = (j <= i)
    m4 = ((j - i) % 4 == 0)
    m16 = ((j - i) % 16 == 0)
    ms = [ge, le, m4, m4 & ge, m4 & le, m16, m16 & ge, m16 & le]
    return np.stack(ms).astype(np.float32)


_PROG_CACHE = {}


def _get_prog(S, steps, nlw):
    k = (S, tuple(steps), nlw)
    if k not in _PROG_CACHE:
        b = Builder(S, list(steps), nlw)
        _PROG_CACHE[k] = (b.build(), b.in_names)
    return _PROG_CACHE[k]


def run_model(inputs, S, depth, fused=False):
    x = np.asarray(inputs["x"], dtype=np.float32)
    B = x.shape[0]
    TPC = S // 4
    ropep, ropea = _rope_tables(S)
    masks = _masks()
    ident = np.eye(128, dtype=np.float32)

    def gv_of(l):
        parts = [inputs["g_mix_pre"][l], inputs["g_mix_post"][l], inputs["g_ffn_pre"][l], inputs["g_ffn_post"][l]]
        return np.concatenate([np.asarray(p, np.float32).reshape(NCH, 128).T for p in parts], axis=1)

    def wmaps(layers):
        d = {}
        d["w_in"] = np.ascontiguousarray(np.asarray(inputs["w_in"])[layers])
        d["w_mem_kv"] = np.ascontiguousarray(np.asarray(inputs["w_mem_kv"])[layers])
        d["w_o"] = np.ascontiguousarray(np.asarray(inputs["w_o"])[layers])
        d["w_gate_up"] = np.ascontiguousarray(np.asarray(inputs["w_gate_up"])[layers])
        d["w_down"] = np.ascontiguousarray(np.asarray(inputs["w_down"])[layers])
        d["gv"] = np.stack([gv_of(l) for l in layers]).astype(np.float32)
        sk = np.zeros((len(layers), 16), np.float32)
        qk = np.ones((len(layers), 2 * HD), np.float32)
        for i, l in enumerate(layers):
            if l % 3 == 0:
                sk[i, :H] = np.asarray(inputs["attn_sink"])[l // 3]
            if l % 3 == 1:
                qk[i] = np.asarray(inputs["qk_norm_g"])[l // 3].reshape(-1)
        d["sink"] = sk
        d["qkg"] = qk
        return d

    def core_common(c):
        b, r = c // 4, c % 4
        return {
            "mem": np.ascontiguousarray(np.asarray(inputs["mem"], np.float32)[b]),
            "mem_g": np.asarray(inputs["mem_norm_g"], np.float32),
            "ropep": np.ascontiguousarray(ropep[r * TPC:(r + 1) * TPC]),
            "ropea": np.ascontiguousarray(ropea[r * TPC:(r + 1) * TPC]),
            "cident": ident, "cmask": masks,
        }

    plans = []
    for l in range(depth + 1):
        steps = []
        layers = []
        if l == 0:
            steps.append(("init",))
        else:
            layers.append(l - 1)
            steps.append(("p2", l - 1, 0))
            steps.append(("ffn", l - 1, 0))
        if l < depth:
            if l not in layers:
                layers.append(l)
            steps.append(("p1", l, layers.index(l)))
        else:
            steps.append(("fin",))
        plans.append((steps, layers))
    if depth == 0:
        plans = [([("init",), ("fin",)], [0])]
    if fused and depth > 0:
        steps = [("init",)]
        for l in range(depth):
            steps += [("p1", l, l), ("exch", l), ("p2", l, l), ("ffn", l, l)]
        steps.append(("fin",))
        plans = [(steps, list(range(depth)))]

    xt = None
    kv = None
    y = None
    for steps, layers in plans:
        nc, in_names = _get_prog(S, steps, len(layers))
        wm = wmaps(layers)
        in_maps = []
        for c in range(NCORES):
            b, r = c // 4, c % 4
            m = dict(wm)
            m.update(core_common(c))
            if steps[0][0] == "init":
                m["x_in"] = np.ascontiguousarray(x[b, r * TPC:(r + 1) * TPC])
            else:
                m["xt_in"] = xt[c]
            for st in steps:
                if st[0] == "p2" and not fused:
                    l = st[1]
                    ktf, vf = kv[b]
                    if l % 3 == 1:
                        m["kt_buf%d" % l] = ktf
                        m["v_buf%d" % l] = vf
                    else:
                        kp = np.zeros((KVH, HD, TPC + 2 * PAD), ml_dtypes.bfloat16)
                        vp = np.zeros((TPC + 2 * PAD, KVH, 128), ml_dtypes.bfloat16)
                        lo = r * TPC - PAD
                        hi = (r + 1) * TPC + PAD
                        slo, shi = max(lo, 0), min(hi, S)
                        kp[:, :, slo - lo:shi - lo] = ktf[:, :, slo:shi]
                        vp[slo - lo:shi - lo] = vf[slo:shi]
                        m["kt_buf%d" % l] = kp
                        m["v_buf%d" % l] = vp
            in_maps.append({k_: v_ for k_, v_ in m.items() if k_ in in_names})
        res = run_bass_kernel_spmd(nc, in_maps, core_ids=list(range(NCORES)))
        outs = res.results
        if "xt_out" in outs[0]:
            xt = [np.asarray(outs[c]["xt_out"]) for c in range(NCORES)]
        if "kt_out" in outs[0]:
            kv = []
            for b in range(B):
                ktf = np.concatenate([np.asarray(outs[b * 4 + r]["kt_out"]) for r in range(4)], axis=2)
                vf = np.concatenate([np.asarray(outs[b * 4 + r]["v_out"]) for r in range(4)], axis=0)
                kv.append((np.ascontiguousarray(ktf), np.ascontiguousarray(vf)))
        if "y_out" in outs[0]:
            y = np.stack([np.concatenate([np.asarray(outs[b * 4 + r]["y_out"]) for r in range(4)], axis=0) for b in range(B)])
    return y.astype(np.float32)


def kernel(**inputs):
    return run_model(inputs, 16384, DEPTH, fused=FUSED)
```

```python
import numpy as np
import ml_dtypes
import concourse.bass as bass
import concourse.mybir as mybir
from concourse.bass_utils import run_bass_kernel_spmd

F32 = mybir.dt.float32
BF16 = mybir.dt.bfloat16
AF = mybir.ActivationFunctionType
ALU = mybir.AluOpType
AX = mybir.AxisListType

D = 1024
NCH = 8
HD = 64
H = 12
KVH = 3
HM = 4
NMEM = 256
QW = 768
KVW = 192
INW = 1408
DFF = 2816
NF = 22
EPS = 1e-6
PAD = 1152
T = 512
DEPTH = 4
GRID_W = 64
NCORES = 8
import os
FUSED = bool(int(os.environ.get('KFUSED', '1')))
NOPOOL = bool(int(os.environ.get('NOPOOL', '1')))


PSUM_KEYS = {"A0", "A1", "S0", "S1", "S0g", "S0u", "S1g", "S1u", "O0", "O1"}


class Op:
    __slots__ = ("eng", "fn", "deps", "flag", "idx", "dma", "dma_val", "real", "inc")

    def __init__(self, eng, fn):
        self.eng = eng
        self.fn = fn
        self.deps = []
        self.flag = False
        self.idx = 0
        self.dma = None
        self.dma_val = 0
        self.real = True
        self.inc = 16


class Sched:
    ENGS = ("pe", "act", "dve", "pool", "sp")

    def __init__(self):
        self.ops = {e: [] for e in self.ENGS}
        self.res = {}
        self.dma_cnt = {}
        self.pending_dma = []
        self.last_real = {e: None for e in self.ENGS}
        self.pre = {}

    def add(self, eng, fn, reads=(), writes=(), dma=None, inc=16):
        op = Op(eng, fn)
        op.inc = inc
        if dma is not None:
            op.dma = dma
            c = self.dma_cnt.get(dma, 0) + inc
            self.dma_cnt[dma] = c
            op.dma_val = c
        deps = {}
        px = [k for k in reads if k in PSUM_KEYS and k not in writes]
        if px:
            writes = list(writes) + px
        for k in reads:
            r = self.res.get(k)
            if r is not None and r[0] is not None:
                deps.setdefault(id(r[0]), [r[0], set()])[1].add("raw")
        for k in writes:
            r = self.res.get(k)
            if r is not None:
                if r[0] is not None:
                    deps.setdefault(id(r[0]), [r[0], set()])[1].add("waw")
                for rd in r[1].values():
                    deps.setdefault(id(rd), [rd, set()])[1].add("war")
                for rd in r[2]:
                    deps.setdefault(id(rd), [rd, set()])[1].add("war")
        for d, kinds in deps.values():
            if d is op:
                continue
            if d.dma is None and op.dma is None and d.eng == eng:
                if eng == "pe":
                    continue
            op.deps.append(d)
            d.flag = True
        for k in reads:
            r = self.res.setdefault(k, [None, {}, []])
            if op.dma is None:
                r[1][eng] = op
            else:
                r[2].append(op)
        for k in writes:
            self.res[k] = [op, {}, []]
        self.ops[eng].append(op)
        if op.dma is not None:
            self.pending_dma.append(op)
        else:
            self.last_real[eng] = op
        return op

    def barrier(self):
        lasts = [o for o in self.last_real.values() if o is not None]
        pl = {}
        for o in self.pending_dma:
            pl[o.dma] = o
        pend = list(pl.values())
        self.pending_dma = []
        for e in self.ENGS:
            op = Op(e, lambda h: h.nop())
            op.real = False
            for d in lasts:
                if d.eng != e or e != "pe":
                    op.deps.append(d)
                    d.flag = True
            for d in pend:
                op.deps.append(d)
            self.ops[e].append(op)

    def emit(self, nc, stack):
        esem = {e: stack.enter_context(nc.semaphore("s_" + e)) for e in self.ENGS}
        dsem = {k: stack.enter_context(nc.semaphore("d_" + str(i))) for i, k in enumerate(self.dma_cnt)}
        for e in self.ENGS:
            c = 0
            for op in self.ops[e]:
                if op.dma is None and op.flag:
                    c += 1
                    op.idx = c
        block = stack.enter_context(nc.Block())

        def run(e, h):
            known = {}
            if self.pre.get(e):
                self.pre[e](h)
            for op in self.ops[e]:
                for d in op.deps:
                    if d.dma is not None:
                        key, sem, val = ("d", d.dma), dsem[d.dma], d.dma_val
                    else:
                        key, sem, val = ("e", d.eng), esem[d.eng], d.idx
                    if known.get(key, 0) < val:
                        h.wait_ge(sem, val)
                        known[key] = val
                ins = op.fn(h)
                if op.dma is not None:
                    ins.then_inc(dsem[op.dma], op.inc)
                elif op.flag:
                    ins.then_inc(esem[e], 1)

        @block.sync
        def _(h):
            run("sp", h)

        @block.tensor
        def _(h):
            run("pe", h)

        @block.scalar
        def _(h):
            run("act", h)

        @block.vector
        def _(h):
            run("dve", h)

        @block.gpsimd
        def _(h):
            run("pool", h)


class Builder:
    def __init__(self, S, steps, nlayers_w):
        self.S = S
        self.TPC = S // 4
        self.NG = self.TPC // T
        self.steps = steps
        self.nlw = nlayers_w
        self.nc = bass.Bass("TRN2", target_bir_lowering=False)
        self.sched = Sched()
        self.uid = 0
        self.in_names = set()

    def key(self, base):
        self.uid += 1
        return "%s#%d" % (base, self.uid)

    def dram_in(self, name, shape, dt=F32):
        self.in_names.add(name)
        return self.nc.dram_tensor(name, list(shape), dt, kind="ExternalInput").ap()

    def dram_out(self, name, shape, dt=F32):
        return self.nc.dram_tensor(name, list(shape), dt, kind="ExternalOutput").ap()

    def alloc(self, shape, dt, parts=128):
        n = int(np.prod(shape))
        nbytes = n * (4 if dt == F32 else 2)
        nbytes = (nbytes + 63) // 64 * 64
        off = self.aoff
        self.aoff += nbytes
        assert self.aoff <= self.acap, ("arena overflow", self.aoff, self.acap)
        v = self.arena[0:parts, off // 2:(off + n * (4 if dt == F32 else 2)) // 2]
        if dt == F32:
            v = v.bitcast(F32)
        if len(shape) == 2:
            v = v.rearrange("p (a b) -> p a b", a=shape[0])
        elif len(shape) == 3:
            v = v.rearrange("p (a b c) -> p a b c", a=shape[0], b=shape[1])
        return v

    def dma(self, q, out, in_, reads, writes, key):
        self.sched.add(q, lambda h: h.dma_start(out=out, in_=in_), reads=reads, writes=writes, dma=q + ":" + key)

    def build(self):
        import contextlib
        nc = self.nc
        S_ = self.sched
        TPC, NG = self.TPC, self.NG
        kinds = set(st[0] for st in self.steps)
        has_init = "init" in kinds
        has_fin = "fin" in kinds
        has_p1 = "p1" in kinds
        has_p2 = "p2" in kinds
        has_ffn = "ffn" in kinds
        fused = "exch" in kinds
        self.fused = fused
        self.rank_vals = {}
        if fused:
            def _pre(h):
                npr = -(-PAD // self.TPC)
                r = h.partition_id() % 4
                self.rank_vals = {j: h.snap(r + (j + npr), min_val=j + npr, max_val=j + npr + 3) for j in range(-npr, npr + 1)}
            self.sched.pre["sp"] = _pre
        nlw = self.nlw
        dr = {}
        if has_init:
            dr["x_in"] = self.dram_in("x_in", [TPC, D])
            dr["xt"] = self.dram_out("xt_out", [NCH, 128, TPC]) if not has_fin else None
        else:
            dr["xt_in"] = self.dram_in("xt_in", [NCH, 128, TPC])
        if has_fin:
            dr["y_out"] = self.dram_out("y_out", [TPC, D])
        if not has_init and not has_fin:
            dr["xt"] = self.dram_out("xt_out", [NCH, 128, TPC])
        if has_init and has_fin:
            dr["xt"] = nc.dram_tensor("xt_scr", [NCH, 128, TPC], F32).ap()
        if not has_init and has_fin:
            dr["xt"] = nc.dram_tensor("xt_scr", [NCH, 128, TPC], F32).ap()
        if has_p1 or has_p2:
            dr["w_in"] = self.dram_in("w_in", [nlw, D, INW])
        if has_p2:
            dr["w_mem_kv"] = self.dram_in("w_mem_kv", [nlw, D, 512])
            dr["w_o"] = self.dram_in("w_o", [nlw, D, D])
        if has_ffn:
            dr["w_gu"] = self.dram_in("w_gate_up", [nlw, D, 2 * DFF])
            dr["w_dn"] = self.dram_in("w_down", [nlw, DFF, D])
        dr["gv"] = self.dram_in("gv", [nlw, 128, 4 * NCH])
        dr["sink"] = self.dram_in("sink", [nlw, 16])
        dr["qkg"] = self.dram_in("qkg", [nlw, 2 * HD])
        dr["mem"] = self.dram_in("mem", [NMEM, D])
        dr["mem_g"] = self.dram_in("mem_g", [D])
        dr["ropep"] = self.dram_in("ropep", [TPC, 16])
        dr["ropea"] = self.dram_in("ropea", [TPC, 64])
        dr["cident"] = self.dram_in("cident", [128, 128])
        dr["cmask"] = self.dram_in("cmask", [8, 128, 128])
        if fused:
            self.NPR = -(-PAD // TPC)
            NR = 4 + 2 * self.NPR
            for i in range(2):
                dr["kt_loc%d" % i] = nc.dram_tensor("kt_loc%d" % i, [KVH * HD, TPC], BF16).ap()
                dr["v_loc%d" % i] = nc.dram_tensor("v_loc%d" % i, [TPC, KVH * 128], BF16).ap()
                self.VCH = min(1024, TPC)
                self.NVC = TPC // self.VCH
                for kh in range(KVH):
                    dr["ktpad%d_%d" % (i, kh)] = nc.dram_tensor("ktpad%d_%d" % (i, kh), [NR * HD, TPC], BF16).ap()
                for c in range(self.NVC):
                    dr["vpad%d_%d" % (i, c)] = nc.dram_tensor("vpad%d_%d" % (i, c), [NR * self.VCH, KVH * 128], BF16).ap()
            dr["kt_rel"] = nc.dram_tensor("kt_rel", [KVH, HD, TPC + 2 * PAD], BF16).ap()
            dr["v_rel"] = nc.dram_tensor("v_rel", [TPC + 2 * PAD, KVH, 128], BF16).ap()
        elif has_p1:
            dr["kt_out"] = self.dram_out("kt_out", [KVH, HD, TPC], BF16)
            dr["v_out"] = self.dram_out("v_out", [TPC, KVH, 128], BF16)
        if has_p2 and not fused:
            self.NKEYS = {}
            for st in self.steps:
                if st[0] == "p2":
                    l = st[1]
                    nk = self.S if l % 3 == 1 else TPC + 2 * PAD
                    dr["kt_buf%d" % l] = self.dram_in("kt_buf%d" % l, [KVH, HD, nk], BF16)
                    dr["v_buf%d" % l] = self.dram_in("v_buf%d" % l, [nk, KVH, 128], BF16)
        self.dr = dr

        stack = contextlib.ExitStack()
        with stack:
            self.acap = 205 * 1024
            self.arena = stack.enter_context(nc.sbuf_tensor("arena", [128, self.acap // 2], BF16))
            self.aoff = 0
            self.psA = stack.enter_context(nc.psum_tensor("psA", [128, 1024], F32))
            self.psS = stack.enter_context(nc.psum_tensor("psS", [128, 2048], F32))
            self.psO = stack.enter_context(nc.psum_tensor("psO", [128, 1024], F32))
            self.setup_consts()
            self.pmark = self.aoff
            if fused:
                self.zero_pads()
            self.xsrc = "x_in" if has_init else "xt_in"
            first_x = True
            for st in self.steps:
                self.aoff = self.pmark
                S_.barrier()
                if st[0] == "init":
                    self.step_init()
                    self.xsrc = "xt"
                elif st[0] == "p1":
                    self.step_p1(st[1], st[2])
                elif st[0] == "p2":
                    self.step_p2(st[1], st[2])
                    self.xsrc = "xt"
                elif st[0] == "ffn":
                    self.step_ffn(st[1], st[2])
                    self.xsrc = "xt"
                elif st[0] == "exch":
                    pass
                elif st[0] == "fin":
                    self.step_fin()
            S_.barrier()
            S_.emit(nc, stack)
        return nc

    def setup_consts(self):
        S_ = self.sched
        dr = self.dr
        self.identF = self.alloc([128], F32)
        self.identB = self.alloc([128], BF16)
        self.onesD = self.alloc([128], BF16)
        self.masks = self.alloc([8, 128], BF16)
        self.gv = self.alloc([self.nlw, 4 * NCH], F32)
        self.rstd = self.alloc([T], F32)
        mark = self.aoff
        stg = self.alloc([8, 128], F32)
        self.dma("sp", self.identF, dr["cident"], [], ["identF"], "c0")
        self.dma("sp", stg, dr["cmask"].rearrange("m p q -> p m q"), [], ["cstg"], "c1")
        self.dma("sp", self.gv, dr["gv"].rearrange("l p c -> p l c"), [], ["gv"], "c2")
        S_.add("dve", lambda h: h.tensor_copy(out=self.identB, in_=self.identF), ["identF"], ["identB"])
        S_.add("dve", lambda h: h.memset(self.onesD, 1.0 / D), [], ["onesD"])
        S_.add("dve", lambda h: h.tensor_copy(out=self.masks, in_=stg), ["cstg"], ["masks"])
        S_.barrier()
        self.aoff = mark

    def zero_pads(self):
        S_ = self.sched
        dr = self.dr
        TPC = self.TPC
        NPR = self.NPR
        VCH = self.VCH
        m = self.aoff
        zt = self.alloc([max(TPC, (VCH // 128) * KVH * 128)], BF16)
        S_.add("dve", lambda h: h.memset(zt, 0.0), [], ["zt"])
        for i in range(2):
            for blk in list(range(NPR)) + list(range(NPR + 4, 2 * NPR + 4)):
                for kh in range(KVH):
                    self.dma("pool", dr["ktpad%d_%d" % (i, kh)][blk * HD:(blk + 1) * HD, :], zt[0:64, 0:TPC], ["zt"],
                             [self.key("zpk")], "zp")
                for c in range(self.NVC):
                    nj = VCH // 128
                    base = blk * VCH
                    q = "sp" if c % 2 == 0 else "pool"
                    self.dma(q, dr["vpad%d_%d" % (i, c)][base:base + VCH, :].rearrange("(j p) c -> p j c", p=128),
                             zt[:, 0:nj * KVH * 128].rearrange("p (j c) -> p j c", j=nj), ["zt"], [self.key("zpv")], "zp")
        S_.barrier()
        self.aoff = m

    def step_exch(self, l):
        S_ = self.sched
        dr = self.dr
        i = l % 2
        TPC = self.TPC
        NPR = self.NPR
        VCH = self.VCH
        groups = [[0, 1, 2, 3], [4, 5, 6, 7]]
        for kh in range(KVH):
            kin = dr["kt_loc%d" % i][kh * HD:(kh + 1) * HD, :]
            kout = dr["ktpad%d_%d" % (i, kh)][NPR * HD:(NPR + 4) * HD, :]
            S_.add("pool", lambda h, kin=kin, kout=kout: h.collective_compute(
                "AllGather", ALU.bypass, replica_groups=groups, ins=[kin], outs=[kout]),
                ["ktdram"], ["ktpad%d" % i], dma="pool:cck", inc=1)
        for c in range(self.NVC):
            vin = dr["v_loc%d" % i][c * VCH:(c + 1) * VCH, :]
            vout = dr["vpad%d_%d" % (i, c)][NPR * VCH:(NPR + 4) * VCH, :]
            S_.add("pool", lambda h, vin=vin, vout=vout: h.collective_compute(
                "AllGather", ALU.bypass, replica_groups=groups, ins=[vin], outs=[vout]),
                ["vdram"], ["vpad%d" % i], dma="pool:ccv", inc=1)
        if l % 3 == 1:
            return
        vrel2 = dr["v_rel"].rearrange("t h c -> t (h c)")
        for j in range(-NPR, NPR + 1):
            a = max(0, -PAD - j * TPC)
            b = min(TPC, TPC + PAD - j * TPC)
            if a >= b:
                continue
            dst = j * TPC + a + PAD
            for kh in range(KVH):
                kp3 = dr["ktpad%d_%d" % (i, kh)].rearrange("(n r) t -> n r t", r=HD)
                S_.add("sp", lambda h, j=j, a=a, b=b, dst=dst, kp3=kp3, kh=kh: h.dma_start(
                    out=dr["kt_rel"][kh, :, dst:dst + (b - a)], in_=kp3[bass.ds(self.rank_vals[j], 1), :, a:b]),
                    ["ktpad%d" % i], ["ktrel"], dma="sp:krel")
            for c in range(self.NVC):
                lo, hi = max(a, c * VCH), min(b, (c + 1) * VCH)
                if lo >= hi:
                    continue
                vp3 = dr["vpad%d_%d" % (i, c)].rearrange("(n t) c -> n t c", t=VCH)
                d2 = j * TPC + lo + PAD
                S_.add("sp", lambda h, j=j, lo=lo, hi=hi, d2=d2, vp3=vp3, c=c: h.dma_start(
                    out=vrel2[d2:d2 + (hi - lo), :], in_=vp3[bass.ds(self.rank_vals[j], 1), lo - c * VCH:hi - c * VCH, :]),
                    ["vpad%d" % i], ["vrel"], dma="sp:vrel")

    def rank_of(self, h):
        return self.rank_vals[id(h)]

    def load_weight(self, dst, src_rows, ncols, nchunks, keyname, col0=0, colchunk=2048):
        S_ = self.sched
        stg = [self.alloc([colchunk], F32) for _ in range(3)]
        i = 0
        for c in range(nchunks):
            for j0 in range(0, ncols, colchunk):
                n = min(colchunk, ncols - j0)
                s = i % 3
                sk = "wstg%d" % s
                q = "sp" if (i % 2 == 0 or NOPOOL) else "pool"
                self.dma(q, stg[s][:, 0:n], src_rows[c * 128:(c + 1) * 128, col0 + j0:col0 + j0 + n], [], [sk], "w%d" % s)
                eng = "dve" if (i % 2 == 0 or NOPOOL) else "pool"
                S_.add(eng, lambda h, c=c, j0=j0, n=n, s=s: h.tensor_copy(out=dst[:, c, j0:j0 + n], in_=stg[s][:, 0:n]),
                       [sk], [keyname])
                i += 1

    def stats_rstd(self, sq, sqkey):
        S_ = self.sched
        ps = self.psO[:, 512:1024]
        for c in range(NCH):
            S_.add("pe", lambda h, c=c: h.matmul(ps, lhsT=self.onesD, rhs=sq[:, c, :], start=(c == 0), stop=(c == NCH - 1)),
                   [sqkey, "onesD"], ["O1"])
        S_.add("dve", lambda h: h.tensor_scalar(out=self.rstd, in0=ps, scalar1=EPS, scalar2=None, op0=ALU.add),
               ["O1"], ["rstd"])
        S_.add("act", lambda h: h.activation(out=self.rstd, in_=self.rstd, func=AF.Ln), ["rstd"], ["rstd"])
        S_.add("act", lambda h: h.activation(out=self.rstd, in_=self.rstd, func=AF.Exp, scale=-0.5), ["rstd"], ["rstd"])

    def prenorm(self, xg, xkey, gcol, hT, sq, sqkey="sq"):
        S_ = self.sched
        S_.add("act", lambda h: h.activation(out=sq, in_=xg, func=AF.Square), [xkey], [sqkey])
        self.stats_rstd(sq, sqkey)
        for c in range(NCH):
            S_.add("dve", lambda h, c=c: h.scalar_tensor_tensor(out=hT[:, c, :], in0=xg[:, c, :], scalar=gcol[:, c:c + 1],
                                                                 in1=self.rstd, op0=ALU.mult, op1=ALU.mult),
                   [xkey, "rstd", "gv"], ["hT"])

    def postnorm_chunk_in(self, c, ps, pskey, fst, sq, sqkey="sq"):
        S_ = self.sched
        S_.add("act", lambda h: h.activation(out=sq[:, c, :], in_=ps, func=AF.Square), [pskey], [sqkey])
        S_.add("dve", lambda h: h.tensor_copy(out=fst[:, c, :], in_=ps), [pskey], ["fst"])

    def postnorm_finish(self, xg, xkey, gcol, fst, sq, sqkey="sq"):
        S_ = self.sched
        self.stats_rstd(sq, sqkey)
        for c in range(NCH):
            e1 = "dve" if (c % 2 == 0 or NOPOOL) else "pool"
            S_.add(e1, lambda h, c=c: h.tensor_tensor(out=fst[:, c, :], in0=fst[:, c, :], in1=self.rstd, op=ALU.mult),
                   ["fst", "rstd"], ["fst"])
            S_.add("dve", lambda h, c=c: h.scalar_tensor_tensor(out=xg[:, c, :], in0=fst[:, c, :], scalar=gcol[:, c:c + 1],
                                                                 in1=xg[:, c, :], op0=ALU.mult, op1=ALU.add),
                   ["fst", "gv", xkey], [xkey])

    def load_x(self, g, xg, xkey, q="sp"):
        dr = self.dr
        if self.xsrc == "x_in":
            raise RuntimeError
        src = dr[self.xsrc]
        self.dma(q, xg, src[:, :, g * T:(g + 1) * T].rearrange("c p t -> p c t"), ["xt_dram%d" % g], [xkey], xkey)

    def store_x(self, g, xg, xkey, q="pool"):
        self.dma(q, self.dr["xt"][:, :, g * T:(g + 1) * T].rearrange("c p t -> p c t"), xg, [xkey], ["xt_dram%d" % g],
                 "st" + xkey)

    def step_init(self):
        S_ = self.sched
        dr = self.dr
        xt = [self.alloc([D], F32) for _ in range(2)]
        xg = [self.alloc([NCH, T], F32) for _ in range(2)]
        for g in range(self.NG):
            gk = "ixg%d" % (g % 2)
            for t in range(4):
                tt = g * 4 + t
                tk = "ixt%d" % (tt % 2)
                xtt = xt[tt % 2]
                self.dma("sp", xtt, dr["x_in"][tt * 128:(tt + 1) * 128, :], [], [tk], tk)
                for hf in range(2):
                    ps = self.psA[:, hf * 512:(hf + 1) * 512]
                    pk = "A%d" % hf
                    for c4 in range(4):
                        c = hf * 4 + c4
                        S_.add("pe", lambda h, ps=ps, c4=c4, c=c, xtt=xtt: h.transpose(
                            out=ps[:, c4 * 128:(c4 + 1) * 128], in_=xtt[:, c * 128:(c + 1) * 128], identity=self.identF),
                            [tk, "identF"], [pk])
                    eng = "dve" if hf == 0 else "act"
                    dst = xg[g % 2][:, hf * 4:(hf + 1) * 4, t * 128:(t + 1) * 128]
                    srcp = ps.rearrange("p (c q) -> p c q", c=4)
                    if eng == "dve":
                        S_.add("dve", lambda h, dst=dst, srcp=srcp: h.tensor_copy(out=dst, in_=srcp), [pk], [gk])
                    else:
                        S_.add("act", lambda h, dst=dst, srcp=srcp: h.activation(out=dst, in_=srcp, func=AF.Copy), [pk], [gk])
            self.store_x(g, xg[g % 2], gk)

    def step_fin(self):
        S_ = self.sched
        dr = self.dr
        xg = [self.alloc([NCH, T], F32) for _ in range(2)]
        yt = [self.alloc([D], F32) for _ in range(2)]
        for g in range(self.NG):
            gk = "fxg%d" % (g % 2)
            self.load_x(g, xg[g % 2], gk)
            for t in range(4):
                tt = g * 4 + t
                yk = "fyt%d" % (tt % 2)
                ytt = yt[tt % 2]
                for hf in range(2):
                    ps = self.psA[:, hf * 512:(hf + 1) * 512]
                    pk = "A%d" % hf
                    for c4 in range(4):
                        c = hf * 4 + c4
                        S_.add("pe", lambda h, ps=ps, c4=c4, c=c, t=t, g=g: h.transpose(
                            out=ps[:, c4 * 128:(c4 + 1) * 128], in_=xg[g % 2][:, c, t * 128:(t + 1) * 128],
                            identity=self.identF), [gk, "identF"], [pk])
                    dst = ytt[:, hf * 512:(hf + 1) * 512]
                    if hf == 0:
                        S_.add("dve", lambda h, dst=dst, ps=ps: h.tensor_copy(out=dst, in_=ps), [pk], [yk])
                    else:
                        S_.add("act", lambda h, dst=dst, ps=ps: h.activation(out=dst, in_=ps, func=AF.Copy), [pk], [yk])
                self.dma("pool", dr["y_out"][tt * 128:(tt + 1) * 128, :], ytt, [yk], ["ydram"], "o" + yk)

    def rope_partial(self, eng, src, dst, nh, tt, skey, dkey, tmp):
        S_ = self.sched
        cs = self.ropep[:, tt, :]
        cb = cs[:, 0:8].unsqueeze(1).to_broadcast([128, nh, 8])
        sb = cs[:, 8:16].unsqueeze(1).to_broadcast([128, nh, 8])
        x1 = src[:, :, 0:8]
        x2 = src[:, :, 8:16]
        t1, t2, t3, t4 = (tmp[:, i, 0:nh, 0:8] for i in range(4))
        tk = "rtmp"
        skey = list(skey) if isinstance(skey, (list, tuple)) else [skey]
        S_.add(eng, lambda h: h.tensor_tensor(out=t1, in0=x1, in1=cb, op=ALU.mult), skey + ["ropep"], [tk + "1"])
        S_.add(eng, lambda h: h.tensor_tensor(out=t2, in0=x2, in1=sb, op=ALU.mult), skey + ["ropep"], [tk + "2"])
        S_.add(eng, lambda h: h.tensor_tensor(out=t3, in0=x2, in1=cb, op=ALU.mult), skey + ["ropep"], [tk + "3"])
        S_.add(eng, lambda h: h.tensor_tensor(out=t4, in0=x1, in1=sb, op=ALU.mult), skey + ["ropep"], [tk + "4"])
        S_.add(eng, lambda h: h.tensor_tensor(out=dst[:, :, 0:8], in0=t1, in1=t2, op=ALU.subtract), [tk + "1", tk + "2"], [dkey])
        S_.add(eng, lambda h: h.tensor_tensor(out=dst[:, :, 8:16], in0=t3, in1=t4, op=ALU.add), [tk + "3", tk + "4"], [dkey])
        S_.add(eng, lambda h: h.tensor_copy(out=dst[:, :, 16:64], in_=src[:, :, 16:64]), skey, [dkey])

    def norm_rope_axial(self, eng, src, dst, nh, tt, skey, dkey, gidx, tmpn, tmp, st, tmpx):
        S_ = self.sched
        sq = tmpn[:, 0:nh, :]
        ss = st[:, 0:nh]
        nk = "qn"
        skey = list(skey) if isinstance(skey, (list, tuple)) else [skey]
        xs = tmpx[:, 0:nh, :]
        S_.add("act", lambda h, s0=src: h.activation(out=xs, in_=s0, func=AF.Copy), skey, [nk + "xs"])
        src = xs
        skey = [nk + "xs"]
        S_.add(eng, lambda h: h.tensor_tensor(out=sq, in0=src, in1=src, op=ALU.mult), skey, [nk + "sq"])
        S_.add(eng, lambda h: h.reduce_sum(out=ss, in_=sq, axis=AX.X), [nk + "sq"], [nk + "ss"])
        S_.add(eng, lambda h: h.tensor_scalar(out=ss, in0=ss, scalar1=1.0 / HD, scalar2=EPS, op0=ALU.mult, op1=ALU.add),
               [nk + "ss"], [nk + "ss"])
        S_.add("act", lambda h: h.activation(out=ss, in_=ss, func=AF.Ln), [nk + "ss"], [nk + "ss"])
        S_.add("act", lambda h: h.activation(out=ss, in_=ss, func=AF.Exp, scale=-0.5), [nk + "ss"], [nk + "ss"])
        S_.add(eng, lambda h: h.tensor_tensor(out=sq, in0=src, in1=ss.unsqueeze(2).to_broadcast([128, nh, HD]), op=ALU.mult),
               skey + [nk + "ss"], [nk + "sq"])
        gb = self.qkg[:, gidx * HD:(gidx + 1) * HD].unsqueeze(1).to_broadcast([128, nh, HD])
        S_.add(eng, lambda h: h.tensor_tensor(out=sq, in0=sq, in1=gb, op=ALU.mult), [nk + "sq", "qkg"], [nk + "sq"])
        ca = self.ropea[:, tt, :]
        for half in range(2):
            o = half * 32
            cb = ca[:, half * 16:(half + 1) * 16].unsqueeze(1).to_broadcast([128, nh, 16])
            sb = ca[:, 32 + half * 16:32 + (half + 1) * 16].unsqueeze(1).to_broadcast([128, nh, 16])
            x1 = sq[:, :, o:o + 16]
            x2 = sq[:, :, o + 16:o + 32]
            t1, t2, t3, t4 = (tmp[:, i, 0:nh, :] for i in range(4))
            tk = "rtmp"
            S_.add(eng, lambda h, x1=x1, cb=cb, t1=t1: h.tensor_tensor(out=t1, in0=x1, in1=cb, op=ALU.mult), [nk + "sq", "ropea"], [tk + "1"])
            S_.add(eng, lambda h, x2=x2, sb=sb, t2=t2: h.tensor_tensor(out=t2, in0=x2, in1=sb, op=ALU.mult), [nk + "sq", "ropea"], [tk + "2"])
            S_.add(eng, lambda h, x2=x2, cb=cb, t3=t3: h.tensor_tensor(out=t3, in0=x2, in1=cb, op=ALU.mult), [nk + "sq", "ropea"], [tk + "3"])
            S_.add(eng, lambda h, x1=x1, sb=sb, t4=t4: h.tensor_tensor(out=t4, in0=x1, in1=sb, op=ALU.mult), [nk + "sq", "ropea"], [tk + "4"])
            S_.add(eng, lambda h, o=o, t1=t1, t2=t2: h.tensor_tensor(out=dst[:, :, o:o + 16], in0=t1, in1=t2, op=ALU.subtract),
                   [tk + "1", tk + "2"], [dkey])
            S_.add(eng, lambda h, o=o, t3=t3, t4=t4: h.tensor_tensor(out=dst[:, :, o + 16:o + 32], in0=t3, in1=t4, op=ALU.add),
                   [tk + "3", tk + "4"], [dkey])

    def load_layer_small(self, l, wi):
        S_ = self.sched
        dr = self.dr
        NT = self.TPC // 128
        kind = l % 3
        if kind == 1:
            self.ropea = self.alloc([NT, 64], F32)
            self.dma("sp", self.ropea, dr["ropea"].rearrange("(t p) c -> p t c", p=128), [], ["ropea"], self.key("rp"))
            self.qkg = self.alloc([2 * HD], F32)
            self.dma("sp", self.qkg, dr["qkg"][wi].partition_broadcast(128), [], ["qkg"], self.key("rp"))
        else:
            self.ropep = self.alloc([NT, 16], F32)
            self.dma("sp", self.ropep, dr["ropep"].rearrange("(t p) c -> p t c", p=128), [], ["ropep"], self.key("rp"))
        if kind == 0:
            self.sinke = self.alloc([16], F32)
            self.dma("sp", self.sinke, dr["sink"][wi].partition_broadcast(128), [], ["sinke"], self.key("rp"))
            S_.add("act", lambda h: h.activation(out=self.sinke, in_=self.sinke, func=AF.Exp), ["sinke"], ["sinke"])

    def step_p1(self, l, wi):
        S_ = self.sched
        dr = self.dr
        kind = l % 3
        self.load_layer_small(l, wi)
        wkv = self.alloc([NCH, 2 * KVW], BF16)
        m0 = self.aoff
        self.load_weight(wkv, dr["w_in"][wi], 2 * KVW, NCH, "wkv", col0=QW)
        S_.barrier()
        self.aoff = m0
        CUT = int(os.environ.get("P1CUT", "99"))
        if CUT <= 1:
            return
        gcol = self.gv[:, wi, 0:NCH]
        xg = [self.alloc([NCH, T], F32) for _ in range(2)]
        hT = self.alloc([NCH, T], BF16)
        sq = self.alloc([NCH, T], BF16)
        kb = self.alloc([KVH, HD], BF16)
        ktst = [self.alloc([KVH, T], BF16) for _ in range(2)]
        vst = [self.alloc([4, KVH, 128], BF16) for _ in range(2)]
        tmp = self.alloc([4, KVH, 16], F32)
        tmpn = self.alloc([KVH, HD], F32)
        tmpx = self.alloc([KVH, HD], F32)
        stt = self.alloc([16], F32)
        for i in range(2):
            S_.add("dve", lambda h, i=i: h.memset(vst[i][:, :, :, 64:128], 1.0), [], ["vst%d" % i])
        psT = self.psA[:, 512:768].bitcast(BF16)
        self.load_x(0, xg[0], "p1x0")
        for g in range(self.NG):
            xk = "p1x%d" % (g % 2)
            if g + 1 < self.NG:
                self.load_x(g + 1, xg[(g + 1) % 2], "p1x%d" % ((g + 1) % 2))
            self.prenorm(xg[g % 2], xk, gcol, hT, sq)
            if CUT <= 2:
                return
            kts = ktst[g % 2]
            vs = vst[g % 2]
            ktk = "ktst%d" % (g % 2)
            vk = "vst%d" % (g % 2)
            for t in range(4):
                tt = g * 4 + t
                ps = self.psA[:, 0:2 * KVW]
                for c in range(NCH):
                    S_.add("pe", lambda h, c=c, t=t: h.matmul(ps, lhsT=hT[:, c, t * 128:(t + 1) * 128], rhs=wkv[:, c, :],
                                                               start=(c == 0), stop=(c == NCH - 1)), ["hT", "wkv"], ["A0"])
                ksrc = ps[:, 0:KVW].rearrange("p (h d) -> p h d", h=KVH)
                if kind == 1:
                    self.norm_rope_axial("dve", ksrc, kb, KVH, tt, "A0", "kb", 1, tmpn, tmp, stt, tmpx)
                else:
                    self.rope_partial("dve", ksrc, kb, KVH, tt, "A0", "kb", tmp)
                if CUT <= 3:
                    return
                S_.add("act", lambda h, t=t, vs=vs: h.activation(out=vs[:, t, :, 0:64],
                                                                  in_=ps[:, KVW:2 * KVW].rearrange("p (h d) -> p h d", h=KVH),
                                                                  func=AF.Copy), ["A0"], [vk])
                if CUT == 5:
                    return
                for kh in range(KVH):
                    S_.add("pe", lambda h, kh=kh: h.transpose(out=psT[0:64, kh * 128:(kh + 1) * 128], in_=kb[:, kh, :],
                                                               identity=self.identB), ["kb", "identB"], ["A1"])
                if CUT == 6:
                    return
                S_.add("act", lambda h, t=t, kts=kts: h.activation(out=kts[0:64, :, t * 128:(t + 1) * 128],
                                                                    in_=psT[0:64, 0:384].rearrange("p (h q) -> p h q", h=KVH),
                                                                    func=AF.Copy), ["A1"], [ktk])
            if CUT <= 7:
                return
            if self.fused:
                kdst = dr["kt_loc%d" % (l % 2)].rearrange("(h d) t -> h d t", h=KVH)
                vdst = dr["v_loc%d" % (l % 2)].rearrange("t (h c) -> t h c", h=KVH)
            else:
                kdst, vdst = dr["kt_out"], dr["v_out"]
            self.dma("pool", kdst[:, :, g * T:(g + 1) * T].rearrange("h d t -> d h t"), kts[0:64], [ktk], ["ktdram"],
                     "o" + ktk)
            self.dma("pool", vdst[g * T:(g + 1) * T].rearrange("(t p) h c -> p t h c", p=128), vs, [vk], ["vdram"],
                     "o" + vk)

    def step_p2(self, l, wi):
        S_ = self.sched
        dr = self.dr
        kind = l % 3
        TPC = self.TPC
        self.load_layer_small(l, wi)
        if self.fused and kind == 1:
            ktb = [dr["ktpad%d_%d" % (l % 2, kh)] for kh in range(KVH)]
            vb = [dr["vpad%d_%d" % (l % 2, c)] for c in range(self.NVC)]
            self.kvkeys = ["ktpad%d" % (l % 2), "vpad%d" % (l % 2)]
        elif self.fused:
            ktb = dr["kt_rel"]
            vb = dr["v_rel"]
            self.kvkeys = ["ktrel", "vrel"]
        else:
            ktb = dr["kt_buf%d" % l]
            vb = dr["v_buf%d" % l]
            self.kvkeys = [None, None]
        self.cur_kind = kind
        wq = self.alloc([NCH, 1024], BF16)
        wo = self.alloc([NCH, D], BF16)
        kmT = self.alloc([HM, NMEM], BF16)
        vm = self.alloc([2, HM, 128], BF16)
        self.kmT, self.vm = kmT, vm
        m0 = self.aoff
        wm = self.alloc([NCH, 512], BF16)
        memT = self.alloc([NCH, NMEM], BF16)
        kmb = self.alloc([256], BF16)
        gb = self.alloc([D], F32)
        mt = self.alloc([2, D], F32)
        sqj = self.alloc([D], F32)
        ssq = self.alloc([2], F32)
        mb = self.alloc([2, D], BF16)
        self.load_weight(wq[:, :, 0:QW], dr["w_in"][wi], QW, NCH, "wq", col0=0, colchunk=QW)
        self.load_weight(wq[:, :, QW:1024], dr["w_in"][wi], 256, NCH, "wq", col0=QW + 2 * KVW, colchunk=256)
        self.load_weight(wo, dr["w_o"][wi], D, NCH, "wo", colchunk=1024)
        self.load_weight(wm, dr["w_mem_kv"][wi], 512, NCH, "wm", colchunk=512)
        self.dma("sp", gb, dr["mem_g"].partition_broadcast(128), [], ["memg"], self.key("mm"))
        self.dma("sp", mt, dr["mem"].rearrange("(b p) d -> p b d", p=128), [], ["memt"], self.key("mm"))
        for b in range(2):
            S_.add("dve", lambda h, b=b: h.tensor_tensor(out=sqj, in0=mt[:, b, :], in1=mt[:, b, :], op=ALU.mult),
                   ["memt"], ["sqj"])
            S_.add("dve", lambda h, b=b: h.reduce_sum(out=ssq[:, b:b + 1], in_=sqj, axis=AX.X), ["sqj"], ["mssq"])
        S_.add("dve", lambda h: h.tensor_scalar(out=ssq, in0=ssq, scalar1=1.0 / D, scalar2=EPS, op0=ALU.mult,
                                                 op1=ALU.add), ["mssq"], ["mssq"])
        S_.add("act", lambda h: h.activation(out=ssq, in_=ssq, func=AF.Ln), ["mssq"], ["mssq"])
        S_.add("act", lambda h: h.activation(out=ssq, in_=ssq, func=AF.Exp, scale=-0.5), ["mssq"], ["mssq"])
        for b in range(2):
            S_.add("dve", lambda h, b=b: h.scalar_tensor_tensor(out=mb[:, b, :], in0=mt[:, b, :], scalar=ssq[:, b:b + 1],
                                                                 in1=gb, op0=ALU.mult, op1=ALU.mult),
                   ["memt", "mssq", "memg"], ["mb"])
        psT8 = self.psA[:, 0:512].bitcast(BF16)
        for b in range(2):
            for c in range(NCH):
                S_.add("pe", lambda h, b=b, c=c: h.transpose(out=psT8[:, c * 128:(c + 1) * 128],
                                                               in_=mb[:, b, c * 128:(c + 1) * 128], identity=self.identB),
                       ["mb", "identB"], ["A0"])
            S_.add("dve", lambda h, b=b: h.tensor_copy(out=memT[:, :, b * 128:(b + 1) * 128],
                                                        in_=psT8.rearrange("p (c q) -> p c q", c=NCH)), ["A0"], ["memT"])
        S_.add("dve", lambda h: h.memset(vm[:, :, :, 64:128], 1.0), [], ["vm"])
        psTm = self.psA[:, 512:768].bitcast(BF16)
        for b in range(2):
            ps = self.psA[:, 0:512]
            for c in range(NCH):
                S_.add("pe", lambda h, b=b, c=c, ps=ps: h.matmul(ps, lhsT=memT[:, c, b * 128:(b + 1) * 128], rhs=wm[:, c, :],
                                                                  start=(c == 0), stop=(c == NCH - 1)), ["memT", "wm"], ["A0"])
            S_.add("dve", lambda h, ps=ps: h.tensor_copy(out=kmb, in_=ps[:, 0:256]), ["A0"], ["kmb"])
            S_.add("act", lambda h, b=b, ps=ps: h.activation(out=vm[:, b, :, 0:64],
                                                              in_=ps[:, 256:512].rearrange("p (h d) -> p h d", h=HM),
                                                              func=AF.Copy), ["A0"], ["vm"])
            for hh in range(HM):
                S_.add("pe", lambda h, hh=hh: h.transpose(out=psTm[0:64, hh * 128:(hh + 1) * 128], in_=kmb[:, hh * 64:(hh + 1) * 64],
                                                           identity=self.identB), ["kmb", "identB"], ["A1"])
            S_.add("dve", lambda h, b=b: h.tensor_copy(out=kmT[0:64, :, b * 128:(b + 1) * 128],
                                                        in_=psTm[0:64, 0:512].rearrange("p (h q) -> p h q", h=HM)), ["A1"], ["kmT"])
        if self.fused:
            self.step_exch(l)
        S_.barrier()
        self.aoff = m0
        gcol = self.gv[:, wi, 0:NCH]
        gpost = self.gv[:, wi, NCH:2 * NCH]
        xg = [self.alloc([NCH, T], F32) for _ in range(2)]
        hT = self.alloc([NCH, T], BF16)
        sq = self.alloc([NCH, T], BF16)
        fst = self.alloc([NCH, T], F32)
        qb = [self.alloc([16, 2 * HD], BF16) for _ in range(2)]
        QT = self.alloc([16, T], BF16)
        OT = self.alloc([NCH, T], BF16)
        self.OT = OT
        ost = [self.alloc([4, 512], F32) for _ in range(2)]
        self.rden = self.alloc([4, 512], F32)
        PT = [self.alloc([1024], BF16) for _ in range(3)]
        NKC = 2560 if kind == 2 else (min(1024, self.TPC) if kind == 1 else 1024)
        ktc = [self.alloc([NKC], BF16) for _ in range(2)]
        vc = [self.alloc([NKC // 128, 128], BF16) for _ in range(2)]
        tmp = self.alloc([4, H, 16], F32)
        tmpn = self.alloc([H, HD], F32)
        tmpx = self.alloc([H, HD], F32)
        stt = self.alloc([16], F32)
        psQT = self.psA[:, 0:512].bitcast(BF16)
        self.ptc = 0
        self.sc = 0
        self.kvc = 0
        self.oc = 0

        self.load_x(0, xg[0], "p2x0")
        for g in range(self.NG):
            xk = "p2x%d" % (g % 2)
            xgg = xg[g % 2]
            if g + 1 < self.NG:
                self.load_x(g + 1, xg[(g + 1) % 2], "p2x%d" % ((g + 1) % 2))
            self.prenorm(xgg, xk, gcol, hT, sq)
            for t in range(4):
                tt = g * 4 + t
                ps = self.psA
                for hf in range(2):
                    for c in range(NCH):
                        S_.add("pe", lambda h, c=c, t=t, hf=hf, ps=ps: h.matmul(ps[:, hf * 512:(hf + 1) * 512],
                                                                                 lhsT=hT[:, c, t * 128:(t + 1) * 128],
                                                                                 rhs=wq[:, c, hf * 512:(hf + 1) * 512],
                                                                                 start=(c == 0), stop=(c == NCH - 1)),
                               ["hT", "wq"], ["A%d" % hf])
                qbt = qb[t % 2]
                qk_ = "qb%d" % (t % 2)
                qsrc = ps[:, 0:QW].rearrange("p (h d) -> p h d", h=H)
                if kind == 1:
                    self.norm_rope_axial("dve", qsrc, qbt[:, 0:H, 0:HD], H, tt, ["A0", "A1"], qk_, 0, tmpn, tmp, stt, tmpx)
                else:
                    self.rope_partial("dve", qsrc, qbt[:, 0:H, 0:HD], H, tt, ["A0", "A1"], qk_, tmp)
                S_.add("act", lambda h, qbt=qbt, ps=ps: h.activation(out=qbt[:, H:16, 0:HD],
                                                                      in_=ps[:, QW:1024].rearrange("p (h d) -> p h d", h=HM),
                                                                      func=AF.Copy), ["A1"], [qk_])
                S_.add("dve", lambda h, qbt=qbt: h.tensor_copy(out=qbt[:, :, HD:2 * HD], in_=qbt[:, :, 0:HD]), [qk_], [qk_])
                for h0 in (0, 8):
                    for hh in range(h0, h0 + 8):
                        S_.add("pe", lambda h, hh=hh, h0=h0, qbt=qbt: h.transpose(
                            out=psQT[:, (hh - h0) * 128:(hh - h0 + 1) * 128], in_=qbt[:, hh, :], identity=self.identB),
                            [qk_, "identB"], ["A0"])
                    S_.add("act", lambda h, h0=h0, t=t: h.activation(
                        out=QT[:, h0:h0 + 8, t * 128:(t + 1) * 128],
                        in_=psQT.rearrange("p (h q) -> p h q", h=8), func=AF.Copy), ["A0"], ["QT"])
            if kind == 1:
                self.attn_global(g, ktb, vb, QT, PT, ktc, vc, ost, NKC)
            else:
                self.attn_window(g, kind, ktb, vb, QT, PT, ktc, vc, ost, NKC)
            for c in range(NCH):
                ps = self.psA[:, (c % 2) * 512:(c % 2 + 1) * 512]
                pk = "A%d" % (c % 2)
                for pr in range(NCH):
                    S_.add("pe", lambda h, c=c, pr=pr, ps=ps: h.matmul(ps, lhsT=wo[:, pr, c * 128:(c + 1) * 128], rhs=OT[:, pr, :],
                                                                        start=(pr == 0), stop=(pr == NCH - 1)), ["wo", "OT"], [pk])
                self.postnorm_chunk_in(c, ps, pk, fst, sq)
            self.postnorm_finish(xgg, xk, gpost, fst, sq)
            self.store_x(g, xgg, xk)

    def attn_pairs(self, qrhs, blocks, kt, ktkey, v, vkey, obank, okey, first, last):
        S_ = self.sched
        nb = len(blocks)
        i = 0
        while i < nb:
            pair = blocks[i:i + 2]
            sidx = self.sc % 2
            self.sc += 1
            sps = self.psS[:, sidx * 1024:(sidx + 1) * 1024]
            sk = "S%d" % sidx
            for j, (kcol, vblk, m) in enumerate(pair):
                pl = j * 64
                S_.add("pe", lambda h, j=j, kcol=kcol, sps=sps, pl=pl: h.matmul(
                    sps[:, j * 512:(j + 1) * 512].rearrange("p (h q) -> p h q", h=4),
                    lhsT=kt[pl:pl + 64, kcol:kcol + 128], rhs=qrhs[pl:pl + 64], start=True, stop=True),
                    [ktkey, "QT"], [sk])
            pidx = self.ptc % 3
            self.ptc += 1
            pt = self.PTl[pidx]
            pk = "PT%d" % pidx
            w = 512 * len(pair)
            S_.add("act", lambda h, pt=pt, sps=sps, w=w: h.activation(out=pt[:, 0:w], in_=sps[:, 0:w], func=AF.Exp, scale=0.125),
                   [sk], [pk])
            for j, (kcol, vblk, m) in enumerate(pair):
                if m is not None:
                    mb = self.masks[:, m, :].unsqueeze(1).to_broadcast([128, 4, 128])
                    S_.add("dve", lambda h, j=j, pt=pt, mb=mb: h.tensor_tensor(
                        out=pt[:, j * 512:(j + 1) * 512].rearrange("p (h q) -> p h q", h=4),
                        in0=pt[:, j * 512:(j + 1) * 512].rearrange("p (h q) -> p h q", h=4), in1=mb, op=ALU.mult),
                        [pk, "masks"], [pk])
            def pv(pair=pair, i=i, pt=pt, pk=pk):
                for j, (kcol, vblk, m) in enumerate(pair):
                    st = first and (i + j == 0)
                    sp = last and (i + j == nb - 1)
                    S_.add("pe", lambda h, j=j, vblk=vblk, pt=pt, st=st, sp=sp: h.matmul(obank, lhsT=v[:, vblk, :],
                                                                                          rhs=pt[:, j * 512:(j + 1) * 512],
                                                                                          start=st, stop=sp),
                           [vkey, pk], [okey])
            self.flush_pv()
            self.pending_pv = pv
            i += 2

    def load_kv(self, ktb, vb, kh, k0, n, ktc, vc):
        s = self.kvc % 2
        self.kvc += 1
        kk = "ktc%d" % s
        vk = "vc%d" % s
        TPC = self.TPC
        R = KVH * HD
        rk_ = [k for k in self.kvkeys[0:1] if k]
        rv_ = [k for k in self.kvkeys[1:2] if k]
        if self.fused and self.cur_kind == 1:
            NPR = self.NPR
            VCH = self.VCH
            rk, off = k0 // TPC, k0 % TPC
            for hf in range(2):
                self.dma("sp", ktc[s][hf * 64:(hf + 1) * 64, 0:n], ktb[kh][(rk + NPR) * HD:(rk + NPR + 1) * HD, off:off + n],
                         rk_, [kk], kk)
            for o2 in range(0, n, VCH):
                c = (off + o2) // VCH
                self.dma("sp", vc[s][:, o2 // 128:(o2 + VCH) // 128, :],
                         vb[c][(rk + NPR) * VCH:(rk + NPR + 1) * VCH, kh * 128:(kh + 1) * 128].rearrange(
                             "(b p) c -> p b c", p=128), rv_, [vk], vk)
        else:
            for hf in range(2):
                self.dma("sp", ktc[s][hf * 64:(hf + 1) * 64, 0:n], ktb[kh, :, k0:k0 + n], rk_, [kk], kk)
            self.dma("sp", vc[s][:, 0:n // 128, :], vb[k0:k0 + n, kh, :].rearrange("(b p) c -> p b c", p=128), rv_, [vk], vk)
        return ktc[s], kk, vc[s], vk

    def flush_pv(self):
        p = getattr(self, "pending_pv", None)
        self.pending_pv = None
        if p is not None:
            p()

    def evac_o(self, obank, okey, ostt, ostk, gi):
        S_ = self.sched
        self.flush_pv()
        S_.add("act", lambda h: h.activation(out=ostt[:, gi, :], in_=obank, func=AF.Copy), [okey], [ostk])

    def mem_and_finish(self, kind, t, QT, ostt, ostk):
        S_ = self.sched
        kmT, vm = self.kmT, self.vm
        sidx = self.sc % 2
        self.sc += 1
        sps = self.psS[:, sidx * 1024:(sidx + 1) * 1024]
        sk = "S%d" % sidx
        for b in range(2):
            for hh in range(HM):
                S_.add("pe", lambda h, b=b, hh=hh: h.matmul(sps[:, b * 512 + hh * 128: b * 512 + (hh + 1) * 128],
                                                             lhsT=kmT[0:64, hh, b * 128:(b + 1) * 128],
                                                             rhs=QT[0:64, H + hh, t * 128:(t + 1) * 128], start=True, stop=True),
                       ["kmT", "QT"], [sk])
        pidx = self.ptc % 3
        self.ptc += 1
        pt = self.PTl[pidx]
        pk = "PT%d" % pidx
        S_.add("act", lambda h: h.activation(out=pt, in_=sps, func=AF.Exp, scale=0.125), [sk], [pk])
        self.flush_pv()
        oi = self.oc % 2
        self.oc += 1
        obank = self.psO[:, oi * 512:(oi + 1) * 512]
        okey = "O%d" % oi
        for hh in range(HM):
            for b in range(2):
                S_.add("pe", lambda h, b=b, hh=hh: h.matmul(obank[:, hh * 128:(hh + 1) * 128], lhsT=vm[:, b, hh, :],
                                                             rhs=pt[:, b * 512 + hh * 128: b * 512 + (hh + 1) * 128],
                                                             start=(b == 0), stop=(b == 1)), ["vm", pk], [okey])
        self.evac_o(obank, okey, ostt, ostk, 3)
        den = ostt[64:128, :, :]
        rden = self.rden[0:64, :, :]
        if kind == 0:
            sb = self.sinke[64:128, 0:H].rearrange("p (g h) -> p g h", g=3).unsqueeze(3).to_broadcast([64, 3, 4, 128])
            S_.add("dve", lambda h: h.tensor_tensor(out=den[:, 0:3, :].rearrange("p g (h q) -> p g h q", h=4),
                                                     in0=den[:, 0:3, :].rearrange("p g (h q) -> p g h q", h=4), in1=sb, op=ALU.add),
                   [ostk, "sinke"], [ostk])
        if kind == 2:
            S_.add("dve", lambda h: h.tensor_tensor(out=den[:, 0, :], in0=den[:, 0, :], in1=den[:, 1, :], op=ALU.add), [ostk], [ostk])
            S_.add("dve", lambda h: h.tensor_tensor(out=den[:, 0, :], in0=den[:, 0, :], in1=den[:, 2, :], op=ALU.add), [ostk], [ostk])
            S_.add("dve", lambda h: h.tensor_copy(out=den[:, 1, :], in_=den[:, 0, :]), [ostk], [ostk])
            S_.add("dve", lambda h: h.tensor_copy(out=den[:, 2, :], in_=den[:, 0, :]), [ostk], [ostk])
        S_.add("act", lambda h: h.activation(out=rden, in_=den, func=AF.Ln), [ostk], ["rden"])
        S_.add("act", lambda h: h.activation(out=rden, in_=rden, func=AF.Exp, scale=-1.0), ["rden"], ["rden"])
        OT = self.OT
        for par in range(2):
            num = ostt[0:64, :, :].rearrange("p g (a b q) -> p g a b q", a=2, b=2)[:, :, :, par, :]
            rd = self.rden[0:64, :, :].rearrange("p g (a b q) -> p g a b q", a=2, b=2)[:, :, :, par, :]
            dst = OT[par * 64:(par + 1) * 64, :, t * 128:(t + 1) * 128].rearrange("p (g a) q -> p g a q", g=4)
            S_.add("dve", lambda h, num=num, rd=rd, dst=dst: h.tensor_tensor(out=dst, in0=num, in1=rd, op=ALU.mult),
                   [ostk, "rden"], ["OT"])

    def attn_window(self, g, kind, ktb, vb, QT, PT, ktc, vc, ost, NKC):
        self.PTl = PT
        base = PAD + g * T
        if kind == 0:
            specs = [(kh, 128, 1) for kh in range(KVH)]
        else:
            specs = [(0, 64, 1), (1, 256, 4), (2, 1024, 16)]
        for tp in range(2):
            tiles = [2 * tp, 2 * tp + 1]
            for (kh, halo, dil) in specs:
                k0 = base + tiles[0] * 128 - halo
                n = 256 + 2 * halo
                kt, kk, v, vk = self.load_kv(ktb, vb, kh, k0, n, ktc, vc)
                for ti, t in enumerate(tiles):
                    off = ti * 128
                    if kind == 0:
                        bl = [(off, 0), (off + 128, None), (off + 256, 1)]
                    elif dil == 1:
                        bl = [(off, 0), (off + 128, 1)]
                    elif dil == 4:
                        ms = [3, 2, 2, 2, 4]
                        bl = [(off + d * 128, ms[d]) for d in range(5)]
                    else:
                        ms = [6] + [5] * 15 + [7]
                        bl = [(off + d * 128, ms[d]) for d in range(17)]
                    blocks = [(kc, kc // 128, m) for (kc, m) in bl]
                    oi = self.oc % 2
                    self.oc += 1
                    obank = self.psO[:, oi * 512:(oi + 1) * 512]
                    okey = "O%d" % oi
                    qrhs = QT[:, 4 * kh:4 * kh + 4, t * 128:(t + 1) * 128]
                    self.attn_pairs(qrhs, blocks, kt, kk, v, vk, obank, okey, True, True)
                    self.evac_o(obank, okey, ost[ti], "ost%d" % ti, kh)
            for ti, t in enumerate(tiles):
                self.mem_and_finish(kind, t, QT, ost[ti], "ost%d" % ti)

    def attn_global(self, g, ktb, vb, QT, PT, ktc, vc, ost, NKC):
        self.PTl = PT
        S = self.S
        nchunk = S // NKC
        for tp in range(2):
            tiles = [2 * tp, 2 * tp + 1]
            for kh in range(KVH):
                for ci in range(nchunk):
                    kt, kk, v, vk = self.load_kv(ktb, vb, kh, ci * NKC, NKC, ktc, vc)
                    blocks = [(b * 128, b, None) for b in range(NKC // 128)]
                    for ti, t in enumerate(tiles):
                        obank = self.psO[:, ti * 512:(ti + 1) * 512]
                        okey = "O%d" % ti
                        qrhs = QT[:, 4 * kh:4 * kh + 4, t * 128:(t + 1) * 128]
                        self.attn_pairs(qrhs, blocks, kt, kk, v, vk, obank, okey, ci == 0, ci == nchunk - 1)
                for ti, t in enumerate(tiles):
                    self.evac_o(self.psO[:, ti * 512:(ti + 1) * 512], "O%d" % ti, ost[ti], "ost%d" % ti, kh)
            self.oc = 0
            for ti, t in enumerate(tiles):
                self.mem_and_finish(1, t, QT, ost[ti], "ost%d" % ti)

    def step_ffn(self, l, wi):
        S_ = self.sched
        dr = self.dr
        wgu = self.alloc([NCH, 2 * DFF], BF16)
        wdn = self.alloc([NF, D], BF16)
        m0 = self.aoff
        self.load_weight(wgu, dr["w_gu"][wi], 2 * DFF, NCH, "wgu", colchunk=1408)
        self.load_weight(wdn, dr["w_dn"][wi], D, NF, "wdn", colchunk=1024)
        S_.barrier()
        self.aoff = m0
        gcol = self.gv[:, wi, 2 * NCH:3 * NCH]
        gpost = self.gv[:, wi, 3 * NCH:4 * NCH]
        xg = self.alloc([NCH, T], F32)
        hT = self.alloc([NCH, T], BF16)
        act = self.alloc([NF, T], BF16)
        fst = self.alloc([NCH, T], F32)
        sil = [self.alloc([T], BF16) for _ in range(2)]
        for g in range(self.NG):
            xk = "fx"
            self.load_x(g, xg, xk)
            self.prenorm(xg, xk, gcol, hT, act[:, 0:NCH, :], "act")
            for j in range(NF):
                gi = j % 2
                gps = self.psS[:, gi * 1024:gi * 1024 + 512]
                ups = self.psS[:, gi * 1024 + 512:(gi + 1) * 1024]
                gk = "S%d" % gi
                for c in range(NCH):
                    S_.add("pe", lambda h, c=c, j=j, gps=gps: h.matmul(gps, lhsT=wgu[:, c, j * 128:(j + 1) * 128], rhs=hT[:, c, :],
                                                                        start=(c == 0), stop=(c == NCH - 1)), ["wgu", "hT"], [gk + "g"])
                for c in range(NCH):
                    S_.add("pe", lambda h, c=c, j=j, ups=ups: h.matmul(ups, lhsT=wgu[:, c, DFF + j * 128:DFF + (j + 1) * 128],
                                                                        rhs=hT[:, c, :], start=(c == 0), stop=(c == NCH - 1)),
                           ["wgu", "hT"], [gk + "u"])
                sl = sil[gi]
                S_.add("act", lambda h, sl=sl, gps=gps: h.activation(out=sl, in_=gps, func=AF.Silu), [gk + "g"], ["sil%d" % gi])
                S_.add("dve", lambda h, sl=sl, ups=ups, j=j: h.tensor_tensor(out=act[:, j, :], in0=sl, in1=ups, op=ALU.mult),
                       ["sil%d" % gi, gk + "u"], ["act"])
            for c in range(NCH):
                ps = self.psA[:, (c % 2) * 512:(c % 2 + 1) * 512]
                pk = "A%d" % (c % 2)
                for j in range(NF):
                    S_.add("pe", lambda h, c=c, j=j, ps=ps: h.matmul(ps, lhsT=wdn[:, j, c * 128:(c + 1) * 128], rhs=act[:, j, :],
                                                                      start=(j == 0), stop=(j == NF - 1)), ["wdn", "act"], [pk])
                self.postnorm_chunk_in(c, ps, pk, fst, hT, "hT")
            self.postnorm_finish(xg, xk, gpost, fst, hT, "hT")
            self.store_x(g, xg, xk, q="sp")


def _rope_tables(S):
    pos = np.arange(S, dtype=np.float32)
    inv = (500000.0 ** (-(np.arange(0, 16, 2, dtype=np.float32) / 16))).astype(np.float32)
    ang = pos[:, None] * inv[None, :]
    ropep = np.concatenate([np.cos(ang), np.sin(ang)], axis=1).astype(np.float32)
    rows = (np.arange(S) // GRID_W).astype(np.float32)
    cols = (np.arange(S) % GRID_W).astype(np.float32)
    inv2 = (10000.0 ** (-(np.arange(0, 32, 2, dtype=np.float32) / 32))).astype(np.float32)
    ar = rows[:, None] * inv2[None, :]
    ac = cols[:, None] * inv2[None, :]
    ropea = np.concatenate([np.cos(ar), np.cos(ac), np.sin(ar), np.sin(ac)], axis=1).astype(np.float32)
    return ropep, ropea


def _masks():
    j = np.arange(128)[:, None]
    i = np.arange(128)[None, :]
    ge = (j >= i)
    le = (j <= i)
    m4 = ((j - i) % 4 == 0)
    m16 = ((j - i) % 16 == 0)
    ms = [ge, le, m4, m4 & ge, m4 & le, m16, m16 & ge, m16 & le]
    return np.stack(ms).astype(np.float32)


_PROG_CACHE = {}


def _get_prog(S, steps, nlw):
    k = (S, tuple(steps), nlw)
    if k not in _PROG_CACHE:
        b = Builder(S, list(steps), nlw)
        _PROG_CACHE[k] = (b.build(), b.in_names)
    return _PROG_CACHE[k]


def run_model(inputs, S, depth, fused=False):
    x = np.asarray(inputs["x"], dtype=np.float32)
    B = x.shape[0]
    TPC = S // 4
    ropep, ropea = _rope_tables(S)
    masks = _masks()
    ident = np.eye(128, dtype=np.float32)

    def gv_of(l):
        parts = [inputs["g_mix_pre"][l], inputs["g_mix_post"][l], inputs["g_ffn_pre"][l], inputs["g_ffn_post"][l]]
        return np.concatenate([np.asarray(p, np.float32).reshape(NCH, 128).T for p in parts], axis=1)

    def wmaps(layers):
        d = {}
        d["w_in"] = np.ascontiguousarray(np.asarray(inputs["w_in"])[layers])
        d["w_mem_kv"] = np.ascontiguousarray(np.asarray(inputs["w_mem_kv"])[layers])
        d["w_o"] = np.ascontiguousarray(np.asarray(inputs["w_o"])[layers])
        d["w_gate_up"] = np.ascontiguousarray(np.asarray(inputs["w_gate_up"])[layers])
        d["w_down"] = np.ascontiguousarray(np.asarray(inputs["w_down"])[layers])
        d["gv"] = np.stack([gv_of(l) for l in layers]).astype(np.float32)
        sk = np.zeros((len(layers), 16), np.float32)
        qk = np.ones((len(layers), 2 * HD), np.float32)
        for i, l in enumerate(layers):
            if l % 3 == 0:
                sk[i, :H] = np.asarray(inputs["attn_sink"])[l // 3]
            if l % 3 == 1:
                qk[i] = np.asarray(inputs["qk_norm_g"])[l // 3].reshape(-1)
        d["sink"] = sk
        d["qkg"] = qk
        return d

    def core_common(c):
        b, r = c // 4, c % 4
        return {
            "mem": np.ascontiguousarray(np.asarray(inputs["mem"], np.float32)[b]),
            "mem_g": np.asarray(inputs["mem_norm_g"], np.float32),
            "ropep": np.ascontiguousarray(ropep[r * TPC:(r + 1) * TPC]),
            "ropea": np.ascontiguousarray(ropea[r * TPC:(r + 1) * TPC]),
            "cident": ident, "cmask": masks,
        }

    plans = []
    for l in range(depth + 1):
        steps = []
        layers = []
        if l == 0:
            steps.append(("init",))
        else:
            layers.append(l - 1)
            steps.append(("p2", l - 1, 0))
            steps.append(("ffn", l - 1, 0))
        if l < depth:
            if l not in layers:
                layers.append(l)
            steps.append(("p1", l, layers.index(l)))
        else:
            steps.append(("fin",))
        plans.append((steps, layers))
    if depth == 0:
        plans = [([("init",), ("fin",)], [0])]
    if fused and depth > 0:
        steps = [("init",)]
        for l in range(depth):
            steps += [("p1", l, l), ("exch", l), ("p2", l, l), ("ffn", l, l)]
        steps.append(("fin",))
        plans = [(steps, list(range(depth)))]

    xt = None
    kv = None
    y = None
    for steps, layers in plans:
        nc, in_names = _get_prog(S, steps, len(layers))
        wm = wmaps(layers)
        in_maps = []
        for c in range(NCORES):
            b, r = c // 4, c % 4
            m = dict(wm)
            m.update(core_common(c))
            if steps[0][0] == "init":
                m["x_in"] = np.ascontiguousarray(x[b, r * TPC:(r + 1) * TPC])
            else:
                m["xt_in"] = xt[c]
            for st in steps:
                if st[0] == "p2" and not fused:
                    l = st[1]
                    ktf, vf = kv[b]
                    if l % 3 == 1:
                        m["kt_buf%d" % l] = ktf
                        m["v_buf%d" % l] = vf
                    else:
                        kp = np.zeros((KVH, HD, TPC + 2 * PAD), ml_dtypes.bfloat16)
                        vp = np.zeros((TPC + 2 * PAD, KVH, 128), ml_dtypes.bfloat16)
                        lo = r * TPC - PAD
                        hi = (r + 1) * TPC + PAD
                        slo, shi = max(lo, 0), min(hi, S)
                        kp[:, :, slo - lo:shi - lo] = ktf[:, :, slo:shi]
                        vp[slo - lo:shi - lo] = vf[slo:shi]
                        m["kt_buf%d" % l] = kp
                        m["v_buf%d" % l] = vp
            in_maps.append({k_: v_ for k_, v_ in m.items() if k_ in in_names})
        res = run_bass_kernel_spmd(nc, in_maps, core_ids=list(range(NCORES)))
        outs = res.results
        if "xt_out" in outs[0]:
            xt = [np.asarray(outs[c]["xt_out"]) for c in range(NCORES)]
        if "kt_out" in outs[0]:
            kv = []
            for b in range(B):
                ktf = np.concatenate([np.asarray(outs[b * 4 + r]["kt_out"]) for r in range(4)], axis=2)
                vf = np.concatenate([np.asarray(outs[b * 4 + r]["v_out"]) for r in range(4)], axis=0)
                kv.append((np.ascontiguousarray(ktf), np.ascontiguousarray(vf)))
        if "y_out" in outs[0]:
            y = np.stack([np.concatenate([np.asarray(outs[b * 4 + r]["y_out"]) for r in range(4)], axis=0) for b in range(B)])
    return y.astype(np.float32)


def kernel(**inputs):
    return run_model(inputs, 16384, DEPTH, fused=FUSED)
```
